# Optimizing a Trainium2 kernel written in Bass

```python
import math
import jax, jax.numpy as jnp
from jax import lax
import numpy as np

D_MODEL = 1024
BATCH = 4
SEQ = 4096
DEPTH = 1

CHUNK = 128
SGU_WIDTH = D_MODEL
SGU_GROUPS = 4
SGU_GROUP_DIM = SGU_WIDTH // SGU_GROUPS
RET_HEADS = 4
RET_QK_DIM = 256
RET_V_DIM = 256
RET_WIDTH = RET_HEADS * RET_V_DIM
D_FF = 2816
ROPE_BASE = 10000.0
NORM_EPS = 1e-6
IN_WIDTHS = (SGU_WIDTH, SGU_WIDTH, RET_HEADS * RET_QK_DIM, RET_HEADS * RET_QK_DIM,
             RET_WIDTH, RET_WIDTH, D_MODEL, D_MODEL)
IN_WIDTH = sum(IN_WIDTHS)

kernel_name = "hybrid_sgu_retention_macaron_block"


def rms_norm(x, g):
    xf = x.astype(jnp.float32)
    y = xf * lax.rsqrt(jnp.mean(xf * xf, axis=-1, keepdims=True) + NORM_EPS)
    return (y * g.astype(jnp.float32)).astype(x.dtype)


def swiglu_ffn(h, w_gate, w_up, w_down):
    return (jax.nn.silu(h @ w_gate) * (h @ w_up)) @ w_down


def rotary(t):
    S, D = t.shape[1], t.shape[3]
    theta = ROPE_BASE ** (-jnp.arange(0, D, 2, dtype=jnp.float32) / D)
    ang = jnp.arange(S, dtype=jnp.float32)[:, None] * theta[None, :]
    cos = jnp.cos(ang)[None, :, None, :]
    sin = jnp.sin(ang)[None, :, None, :]
    t1, t2 = jnp.split(t.astype(jnp.float32), 2, axis=-1)
    return jnp.concatenate([t1 * cos - t2 * sin, t2 * cos + t1 * sin], axis=-1)


def spatial_gating(u, v, norm_g, norm_b, w_s, b_s):
    B, S, _ = v.shape
    vf = v.astype(jnp.float32)
    mu = jnp.mean(vf, axis=-1, keepdims=True)
    var = jnp.mean(jnp.square(vf - mu), axis=-1, keepdims=True)
    vn = (vf - mu) * lax.rsqrt(var + NORM_EPS) * norm_g + norm_b
    vc = vn.reshape(B, S // CHUNK, CHUNK, SGU_GROUPS, SGU_GROUP_DIM)
    s = jnp.einsum('gcm,bnmgd->bncgd', w_s.astype(jnp.float32), vc)
    s = s + b_s.astype(jnp.float32).T[None, None, :, :, None]
    return u * s.reshape(B, S, SGU_WIDTH).astype(u.dtype)


def retention_direction(q, k, v, log_gamma, include_diag):
    C = q.shape[3]
    idx = jnp.arange(C, dtype=jnp.float32)
    diff = idx[:, None] - idx[None, :]
    keep = (diff >= 0) if include_diag else (diff > 0)
    lg = log_gamma[:, None, None]
    decay = jnp.where(keep[None], jnp.exp(jnp.maximum(diff, 0.0)[None] * lg), 0.0)
    scores = jnp.einsum('bhncd,bhnmd->bhncm', q, k) * decay[None, :, None]
    intra = jnp.einsum('bhncm,bhnme->bhnce', scores, v)
    q_dec = q * jnp.exp((idx + 1.0)[None, :] * log_gamma[:, None])[None, :, None, :, None]
    k_dec = k * jnp.exp((C - 1.0 - idx)[None, :] * log_gamma[:, None])[None, :, None, :, None]
    chunk_decay = jnp.exp(C * log_gamma)[None, :, None, None]

    def step(state, xs):
        qn, kn, vn = xs
        out = jnp.einsum('bhcd,bhde->bhce', qn, state)
        state = state * chunk_decay + jnp.einsum('bhcd,bhce->bhde', kn, vn)
        return state, out

    B, H = q.shape[0], q.shape[1]
    init = jnp.zeros((B, H, q.shape[-1], v.shape[-1]), jnp.float32)
    xs = (jnp.moveaxis(q_dec, 2, 0), jnp.moveaxis(k_dec, 2, 0), jnp.moveaxis(v, 2, 0))
    _, cross = lax.scan(step, init, xs)
    return intra + jnp.moveaxis(cross, 0, 2)


def bidirectional_retention(q, k, v, decay_logit):
    B, S, H, dv = v.shape
    N = S // CHUNK
    log_gamma = jax.nn.log_sigmoid(decay_logit.astype(jnp.float32))

    def chunk(t):
        return jnp.transpose(t.astype(jnp.float32), (0, 2, 1, 3)).reshape(B, H, N, CHUNK, t.shape[-1])

    def rev(t):
        return jnp.flip(t, axis=1)

    fwd = retention_direction(chunk(q), chunk(k), chunk(v), log_gamma[0], True)
    bwd = retention_direction(chunk(rev(q)), chunk(rev(k)), chunk(rev(v)), log_gamma[1], False)
    out = fwd.reshape(B, H, S, dv) + jnp.flip(bwd.reshape(B, H, S, dv), axis=2)
    return jnp.transpose(out, (0, 2, 1, 3))


def setup_inputs(seed: int = 0) -> dict:
    key = jax.random.key(seed)
    ks = jax.random.split(key, 24)
    L, D = DEPTH, D_MODEL

    def nrm(k, shape, scale):
        return jax.random.normal(k, shape, jnp.float32) * scale

    base_logit = jnp.log(2.0 ** (5.0 + jnp.arange(RET_HEADS, dtype=jnp.float32)) - 1.0)
    return {
        "x": nrm(ks[0], (BATCH, SEQ, D), 1.0),
        "ffn1_norm": 1.0 + nrm(ks[1], (L, D), 0.02),
        "ffn1_w_gate": nrm(ks[2], (L, D, D_FF), D ** -0.5),
        "ffn1_w_up": nrm(ks[3], (L, D, D_FF), D ** -0.5),
        "ffn1_w_down": nrm(ks[4], (L, D_FF, D), D_FF ** -0.5),
        "mix_norm": 1.0 + nrm(ks[5], (L, D), 0.02),
        "w_in": nrm(ks[6], (L, D, IN_WIDTH), D ** -0.5),
        "b_in": nrm(ks[7], (L, IN_WIDTH), 0.02),
        "sgu_norm_g": 1.0 + nrm(ks[8], (L, SGU_WIDTH), 0.02),
        "sgu_norm_b": nrm(ks[9], (L, SGU_WIDTH), 0.02),
        "sgu_w_s": nrm(ks[10], (L, SGU_GROUPS, CHUNK, CHUNK), CHUNK ** -0.5),
        "sgu_b_s": 1.0 + nrm(ks[11], (L, SGU_GROUPS, CHUNK), 0.1),
        "ret_decay_logit": jnp.broadcast_to(base_logit, (L, 2, RET_HEADS)) + nrm(ks[12], (L, 2, RET_HEADS), 0.05),
        "w_branch_a": nrm(ks[13], (L, SGU_WIDTH, D), SGU_WIDTH ** -0.5),
        "w_branch_b": nrm(ks[14], (L, RET_WIDTH, D), RET_WIDTH ** -0.5),
        "w_out": nrm(ks[15], (L, D, D), D ** -0.5),
        "ffn2_norm": 1.0 + nrm(ks[16], (L, D), 0.02),
        "ffn2_w_gate": nrm(ks[17], (L, D, D_FF), D ** -0.5),
        "ffn2_w_up": nrm(ks[18], (L, D, D_FF), D ** -0.5),
        "ffn2_w_down": nrm(ks[19], (L, D_FF, D), D_FF ** -0.5),
        "final_norm": 1.0 + nrm(ks[20], (D,), 0.02),
    }


def reference(x, ffn1_norm, ffn1_w_gate, ffn1_w_up, ffn1_w_down, mix_norm, w_in, b_in,
              sgu_norm_g, sgu_norm_b, sgu_w_s, sgu_b_s, ret_decay_logit,
              w_branch_a, w_branch_b, w_out, ffn2_norm, ffn2_w_gate, ffn2_w_up, ffn2_w_down,
              final_norm):
    B, S, _ = x.shape
    split_at = list(np.cumsum(IN_WIDTHS)[:-1])
    for l in range(DEPTH):
        x = x + 0.5 * swiglu_ffn(rms_norm(x, ffn1_norm[l]), ffn1_w_gate[l], ffn1_w_up[l], ffn1_w_down[l])

        h = rms_norm(x, mix_norm[l])
        proj = h @ w_in[l] + b_in[l]
        u_a, v_a, q_r, k_r, v_r, g_r, gate_a, gate_b = jnp.split(proj, split_at, axis=-1)

        a = spatial_gating(jax.nn.gelu(u_a, approximate=False), jax.nn.gelu(v_a, approximate=False),
                           sgu_norm_g[l], sgu_norm_b[l], sgu_w_s[l], sgu_b_s[l])

        q = rotary(q_r.reshape(B, S, RET_HEADS, RET_QK_DIM))
        k = rotary(k_r.reshape(B, S, RET_HEADS, RET_QK_DIM)) * (RET_QK_DIM ** -0.5)
        v = v_r.reshape(B, S, RET_HEADS, RET_V_DIM)
        r = bidirectional_retention(q, k, v, ret_decay_logit[l])
        r = r * lax.rsqrt(jnp.mean(r * r, axis=-1, keepdims=True) + NORM_EPS)
        r = r.reshape(B, S, RET_WIDTH).astype(x.dtype) * jax.nn.silu(g_r)

        mix = jax.nn.sigmoid(gate_a) * (a @ w_branch_a[l]) + jax.nn.sigmoid(gate_b) * (r @ w_branch_b[l])
        x = x + mix @ w_out[l]

        x = x + 0.5 * swiglu_ffn(rms_norm(x, ffn2_norm[l]), ffn2_w_gate[l], ffn2_w_up[l], ffn2_w_down[l])
    return rms_norm(x, final_norm)
```

```python
import os
import numpy as np
import ml_dtypes
from contextlib import ExitStack
import concourse.bass as bass
import concourse.mybir as mybir
from concourse.bass_utils import run_bass_kernel_spmd

F32 = mybir.dt.float32
BF16 = mybir.dt.bfloat16
AF = mybir.ActivationFunctionType
ALU = mybir.AluOpType

D = 1024
DFF = 2816
NFT = DFF // 128
SEQ = 4096
HALF = 2048
G = 512
NG_ALL = SEQ // G
NG_OWN = HALF // G
EPS = 1e-6
OFF_U, OFF_VA, OFF_Q, OFF_K, OFF_V, OFF_GR, OFF_GA, OFF_GB = [i * 1024 for i in range(8)]
RT_FFN1, RT_MIX, RT_FFN2, RT_FINAL, RT_SGUG, RT_SGUB, RT_BK, RT_BV, RT_BGR, RT_BVA = range(10)
NRT = 10
SL_FFN1, SL_MIX, SL_BK, SL_BV, SL_BGR, SL_BVA, SL_SGUG, SL_SGUB, SL_FFN2, SL_FINAL = 0, 1, 2, 3, 0, 1, 2, 3, 0, 1
CT_BQ, CT_BK, CT_BU, CT_BGA, CT_BGB, CT_LOGIT = 0, 8, 16, 24, 32, 40
NCT = 48


class Sched:
    def __init__(self, nc):
        self.nc = nc
        self.ops = []

    def op(self, eng, fn, r=(), w=()):
        self.ops.append(dict(eng=eng, fn=fn, r=tuple(r), w=tuple(w), dma=False, chan=None, signal=False))

    def dma(self, eng, fn, chan, r=(), w=()):
        self.ops.append(dict(eng=eng, fn=fn, r=tuple(r), w=tuple(w), dma=True, chan=("ch", chan), signal=True))

    def emit(self):
        nc = self.nc
        ops = self.ops
        last_w, readers = {}, {}
        for i, o in enumerate(ops):
            deps = set()
            for k in o["r"]:
                if k in last_w:
                    deps.add(last_w[k])
            for k in o["w"]:
                if k in last_w:
                    deps.add(last_w[k])
                deps.update(readers.get(k, ()))
            deps.discard(i)
            o["deps"] = deps
            for k in o["r"]:
                readers.setdefault(k, []).append(i)
            for k in o["w"]:
                last_w[k] = i
                readers[k] = []
        for i, o in enumerate(ops):
            keep = set()
            for j in o["deps"]:
                p = ops[j]
                if (not p["dma"]) and (not o["dma"]) and p["eng"] == o["eng"] == "pe":
                    continue
                if not p["dma"]:
                    p["signal"] = True
                keep.add(j)
            o["deps"] = keep
        counts = {}
        for o in ops:
            key = o["chan"] if o["dma"] else ("eng", o["eng"])
            o["key"] = key
            if o["signal"] and o["fn"] is not None:
                counts[key] = counts.get(key, 0) + (16 if o["dma"] else 1)
                o["sigval"] = counts[key]
        keys = sorted(counts.keys(), key=str)
        with ExitStack() as es:
            sems = {k: es.enter_context(nc.semaphore("s%d" % n)) for n, k in enumerate(keys)}
            block = es.enter_context(nc.Block())
            engmap = dict(pe=block.tensor, act=block.scalar, dve=block.vector, pool=block.gpsimd, sp=block.sync)

            def make(engname):
                def body(eng):
                    waited = {}
                    for o in ops:
                        if o["eng"] != engname:
                            continue
                        need = {}
                        for j in o["deps"]:
                            p = ops[j]
                            need[p["key"]] = max(need.get(p["key"], 0), p["sigval"])
                        for k, v in sorted(need.items(), key=str):
                            if waited.get(k, 0) < v:
                                eng.wait_ge(sems[k], v)
                                waited[k] = v
                        if o["fn"] is None:
                            continue
                        ins = o["fn"](eng)
                        if o["signal"]:
                            ins.then_inc(sems[o["key"]], 16 if o["dma"] else 1)
                return body

            for name, deco in engmap.items():
                deco(make(name))


def build_program(debug=False, stop_after=None):
    nc = bass.Bass("TRN2", target_bir_lowering=False)
    ein = lambda name, shape, dt=F32: nc.dram_tensor(name, list(shape), dt, kind="ExternalInput").ap()
    xall = ein("xall", [SEQ, D])
    w1g, w1u, w1d = ein("w1g", [D, DFF]), ein("w1u", [D, DFF]), ein("w1d", [DFF, D])
    w2g, w2u, w2d = ein("w2g", [D, DFF]), ein("w2u", [D, DFF]), ein("w2d", [DFF, D])
    win = ein("win", [D, 8 * D])
    wa, wb, wo = ein("wa", [D, D]), ein("wb", [D, D]), ein("wo", [D, D])
    rowtab = ein("rowtab", [NRT, 128, D])
    coltab = ein("coltab", [128, NCT])
    wst = ein("wst", [128, 4 * 128])
    bsrow = ein("bsrow", [1, 4 * 128])
    ident_in = ein("ident", [128, 128])
    consts = ein("consts", [128, 2 + 128])
    cs_fm = ein("cs_fm", [2, 128, HALF])
    cs_tm = ein("cs_tm", [2, SEQ, 128])
    y = nc.dram_tensor("y", [HALF, D], F32, kind="ExternalOutput").ap()

    skind = "ExternalOutput" if debug else "Internal"
    scr = lambda name, shape, dt: nc.dram_tensor(name, list(shape), dt, kind=skind).ap()
    x1s = scr("x1s", [HALF, D], F32)
    h2Ts = scr("h2Ts", [NG_ALL, 128, 8 * G], BF16)
    rps = scr("rps", [HALF, D], F32)
    qTs = scr("qTs", [NG_OWN, 128, 8 * G], BF16)
    ktms = scr("ktms", [HALF, D], BF16)
    vtms = scr("vtms", [HALF, D], BF16)
    rgTs = scr("rgTs", [NG_OWN, 128, 8 * G], BF16)
    x2s = scr("x2s", [HALF, D], F32)
    dbg_state = scr("dbg_state", [128, 8 * 256], F32) if debug else None
    dbg_ct = scr("dbg_ct", [128, NCT], F32) if debug else None
    dbg_lg = scr("dbg_lg", [128, 8], F32) if debug else None
    dbg_dec = scr("dbg_dec", [128, 24], F32) if debug else None
    dbg_MT = scr("dbg_MT", [128, 512], F32) if debug else None
    dbg_kT = scr("dbg_kT", [128, 8 * G], BF16) if debug else None

    S = Sched(nc)
    es = ExitStack()
    sb = lambda name, shape, dt=F32: es.enter_context(nc.sbuf_tensor(name, list(shape), dt))
    rt = sb("rt", [128, 4, D])
    ct = sb("ct", [128, NCT])
    cst = sb("cst", [128, 130])
    ident = sb("identb", [128, 128], BF16)
    wsT = sb("wsT", [128, 512], BF16)
    bsr = sb("bsr", [1, 512], BF16)
    ones = sb("ones", [1, 128], BF16)
    ones33 = sb("ones33", [33, 128], BF16)
    brow = sb("brow", [33, 2, D], BF16)
    lg = sb("lg", [128, 8])
    dec = sb("dec", [128, 24])
    MT = sb("MT", [128, 4, 128])
    mtmp = sb("mtmp", [128, 2, 128])
    Sst = sb("Sst", [128, 8, 256])
    Tst = sb("Tst", [128, 8, 256])
    Stb = sb("Stb", [128, 8, 256], BF16)
    ss = sb("ss", [128, 32])
    rstd = sb("rstd", [128, 32])
    xt = sb("xt", [128, 4, D])
    hb = sb("hb", [128, D], BF16)
    hb2 = sb("hb2", [128, D], BF16)
    junk = sb("junk", [128, D], BF16)
    h2T = sb("h2T", [128, 8, G], BF16)
    wp = [sb("wp%d" % i, [128, 8, 256], BF16) for i in range(4)]
    ps = [es.enter_context(nc.psum_tensor("ps%d" % i, [128, 512], F32)) for i in range(6)]
    pst = [es.enter_context(nc.psum_tensor("pst%d" % i, [128, 1024], BF16)) for i in range(2)]
    cnt = dict(ps=0, pst=0, wp=0, nwp=4)

    def nps():
        i = cnt["ps"] % 6
        cnt["ps"] += 1
        return i

    def npst():
        i = cnt["pst"] % 2
        cnt["pst"] += 1
        return i

    def nwp():
        i = cnt["wp"] % cnt["nwp"]
        cnt["wp"] += 1
        return i

    def ts_ap(e, out, in0, sc, op0):
        return e.tensor_scalar(out=out, in0=in0, scalar1=sc, scalar2=None, op0=op0)

    S.dma("sp", lambda e: e.dma_start(out=ct[:], in_=coltab[:, :]), "ct", w=["ct"])
    S.dma("sp", lambda e: e.dma_start(out=cst[:], in_=consts[:, :]), "cst", w=["cst"])
    S.dma("pool", lambda e: e.dma_start(out=ident[:], in_=ident_in[:, :]), "ident", w=["ident"])
    S.dma("pool", lambda e: e.dma_start(out=wsT[:], in_=wst[:, :]), "wsT", w=["wsT"])
    S.dma("pool", lambda e: e.dma_start(out=bsr[:], in_=bsrow[:, :]), "bsr", w=["bsr"])
    S.op("dve", lambda e: e.memset(ones[:], 1.0), w=["ones"])
    S.op("dve", lambda e: e.memset(ones33[:], 1.0), w=["ones33"])
    for (bp, bi, ridx) in ((0, 0, RT_BK), (0, 1, RT_BV), (32, 0, RT_BGR), (32, 1, RT_BVA)):
        S.dma("pool", lambda e, bp=bp, bi=bi, ridx=ridx: e.dma_start(out=brow[bp:bp + 1, bi, :],
                                                                      in_=rowtab[ridx][0:1, :]),
              "brow%d_%d" % (bp, bi), w=[("brow", bp, bi)])
    S.op("dve", lambda e: e.memset(Sst[:], 0.0), w=["Sst"])
    S.op("dve", lambda e: e.memset(Tst[:], 0.0), w=["Tst"])
    S.op("dve", lambda e: e.memset(Stb[:], 0.0), w=["Stb"])
    S.op("dve", lambda e: e.tensor_scalar(out=lg[:], in0=ct[:, CT_LOGIT:CT_LOGIT + 8], scalar1=-1.0, scalar2=0.0,
                                          op0=ALU.mult, op1=ALU.add), r=["ct"], w=["lg"])
    S.op("act", lambda e: e.activation(out=lg[:], in_=lg[:], func=AF.Exp), r=["lg"], w=["lg"])
    S.op("dve", lambda e: e.tensor_scalar(out=lg[:], in0=lg[:], scalar1=1.0, scalar2=0.0, op0=ALU.add, op1=ALU.add),
         r=["lg"], w=["lg"])
    S.op("act", lambda e: e.activation(out=lg[:], in_=lg[:], func=AF.Ln), r=["lg"], w=["lg"])
    S.op("dve", lambda e: e.tensor_scalar(out=lg[:], in0=lg[:], scalar1=-1.0, scalar2=0.0, op0=ALU.mult, op1=ALU.add),
         r=["lg"], w=["lg"])
    S.op("dve", lambda e: ts_ap(e, dec[:, 0:4], lg[:, 0:4], cst[:, 1:2], ALU.mult), r=["lg", "cst"], w=["dec"])
    S.op("dve", lambda e: e.tensor_scalar(out=mtmp[:, 0, 0:1], in0=cst[:, 0:1], scalar1=-1.0, scalar2=127.0,
                                          op0=ALU.mult, op1=ALU.add), r=["cst"], w=["mtmp"])
    S.op("dve", lambda e: ts_ap(e, dec[:, 4:8], lg[:, 0:4], mtmp[:, 0, 0:1], ALU.mult), r=["lg", "mtmp"], w=["dec"])
    S.op("dve", lambda e: e.tensor_scalar(out=dec[:, 8:12], in0=lg[:, 0:4], scalar1=128.0, scalar2=0.0,
                                          op0=ALU.mult, op1=ALU.add), r=["lg"], w=["dec"])
    S.op("dve", lambda e: e.tensor_scalar(out=mtmp[:, 0, 1:2], in0=cst[:, 0:1], scalar1=-1.0, scalar2=128.0,
                                          op0=ALU.mult, op1=ALU.add), r=["cst", "mtmp"], w=["mtmp"])
    S.op("dve", lambda e: ts_ap(e, dec[:, 12:16], lg[:, 4:8], mtmp[:, 0, 1:2], ALU.mult), r=["lg", "mtmp"], w=["dec"])
    S.op("dve", lambda e: ts_ap(e, dec[:, 16:20], lg[:, 4:8], cst[:, 0:1], ALU.mult), r=["lg", "cst"], w=["dec"])
    S.op("dve", lambda e: e.tensor_scalar(out=dec[:, 20:24], in0=lg[:, 4:8], scalar1=128.0, scalar2=0.0,
                                          op0=ALU.mult, op1=ALU.add), r=["lg"], w=["dec"])
    S.op("act", lambda e: e.activation(out=dec[:], in_=dec[:], func=AF.Exp), r=["dec"], w=["dec"])
    S.op("dve", lambda e: e.tensor_scalar(out=dec[:, 4:8], in0=dec[:, 4:8], scalar1=0.0625, scalar2=0.0,
                                          op0=ALU.mult, op1=ALU.add), r=["dec"], w=["dec"])
    S.op("dve", lambda e: e.tensor_scalar(out=dec[:, 16:20], in0=dec[:, 16:20], scalar1=0.0625, scalar2=0.0,
                                          op0=ALU.mult, op1=ALU.add), r=["dec"], w=["dec"])
    QD1, KD1, G1, QD2, KD2, G2 = 0, 4, 8, 12, 16, 20
    S.op("dve", lambda e: e.tensor_scalar(out=mtmp[:, 0, :], in0=cst[:, 2:130], scalar1=0.0, scalar2=0.0,
                                          op0=ALU.max, op1=ALU.add), r=["cst", "mtmp"], w=["mtmp"])
    S.op("dve", lambda e: e.tensor_scalar(out=mtmp[:, 1, :], in0=cst[:, 2:130], scalar1=-1.0, scalar2=0.0,
                                          op0=ALU.mult, op1=ALU.max), r=["cst", "mtmp"], w=["mtmp"])
    for h in range(4):
        S.op("dve", lambda e, h=h: ts_ap(e, MT[:, h, :], mtmp[:, 0, :], lg[:, h:h + 1], ALU.mult), r=["mtmp", "lg"], w=["MT"])
        S.op("dve", lambda e, h=h: e.scalar_tensor_tensor(out=MT[:, h, :], in0=mtmp[:, 1, :],
                                                          scalar=lg[:, 4 + h:5 + h], in1=MT[:, h, :],
                                                          op0=ALU.mult, op1=ALU.add), r=["mtmp", "lg", "MT"], w=["MT"])
    S.op("act", lambda e: e.activation(out=MT[:], in_=MT[:], func=AF.Exp), r=["MT"], w=["MT"])
    S.op("dve", lambda e: e.tensor_scalar(out=MT[:], in0=MT[:], scalar1=0.0625, scalar2=0.0, op0=ALU.mult, op1=ALU.add),
         r=["MT"], w=["MT"])

    if debug:
        S.dma("sp", lambda e: e.dma_start(out=dbg_ct[:, :], in_=ct[:]), "dbg1", r=["ct"], w=["dbg1"])
        S.dma("sp", lambda e: e.dma_start(out=dbg_lg[:, :], in_=lg[:]), "dbg2", r=["lg"], w=["dbg2"])
        S.dma("sp", lambda e: e.dma_start(out=dbg_dec[:, :], in_=dec[:]), "dbg3", r=["dec"], w=["dbg3"])
        S.dma("sp", lambda e: e.dma_start(out=dbg_MT[:, :], in_=MT[:].rearrange("p a b -> p (a b)")), "dbg4",
              r=["MT"], w=["dbg4"])
    def rt_load(slot, idx):
        S.dma("sp", lambda e: e.dma_start(out=rt[:, slot, :], in_=rowtab[idx]), "rt%d" % slot, w=[("rt", slot)])

    arena = sb("arena", [128, 21504])
    dummy = sb("bdummy", [128, 8])

    def carve(off, shape, dt=F32):
        nb = int(np.prod(shape)) * (4 if dt == F32 else 2)
        ap = arena[:, off // 4:(off + nb) // 4]
        if dt == BF16:
            ap = ap.bitcast(BF16)
        if len(shape) == 2:
            ap = ap.rearrange("p (a b) -> p a b", a=shape[0])
        return ap

    FFN_KEYS = ["hT", ("wp", 4), ("wp", 5)] + [("tT", i) for i in range(NFT)] + \
        [("wd", h, b) for h in range(2) for b in range(NFT // 2)]
    CTMP_KEYS = ["qtmp", "fA", "fB", "csfm", "rp", "ri", ("PT", 0), ("PT", 1)]
    E1_KEYS = ["sga", "sgb", "mixT"]
    MIX_KEYS = ["qT", "kT", "rpg", "rgT", "rr"] + [("sgr", i) for i in range(4)] + CTMP_KEYS

    def barrier(rk, wk):
        S.op("dve", lambda e: e.memset(dummy[:], 0.0), w=list(rk) + list(wk))

    def load_w(W, col0, ncols=256, row_tiles=8):
        i = nwp()
        S.dma("pool", lambda e: e.dma_start(
            out=wp[i][:, 0:row_tiles, 0:ncols],
            in_=W[0:row_tiles * 128, col0:col0 + ncols].rearrange("(kt p) c -> p kt c", p=128)),
            "wp%d" % i, w=[("wp", i)])
        return i

    def norm_group(rtidx, dstT, dstkey):
        for c in range(4):
            S.op("act", lambda e, c=c: e.activation(out=junk[:], in_=xt[:, c, :], func=AF.Square,
                                                    accum_out=ss[:, c:c + 1]), r=[("xt", c)], w=[("ss", c)])
        allss = [("ss", c) for c in range(4)]
        allr = [("rstd", c) for c in range(4)]
        S.op("dve", lambda e: e.tensor_scalar(out=rstd[:, 0:4], in0=ss[:, 0:4], scalar1=1.0 / D, scalar2=EPS,
                                              op0=ALU.mult, op1=ALU.add), r=allss, w=allr)
        S.op("act", lambda e: e.activation(out=rstd[:, 0:4], in_=rstd[:, 0:4], func=AF.Sqrt), r=allr, w=allr)
        S.op("dve", lambda e: e.reciprocal(out=rstd[:, 0:4], in_=rstd[:, 0:4]), r=allr, w=allr)
        for c in range(4):
            hbuf, hkey = (hb, "hb") if c % 2 == 0 else (hb2, "hb2")
            S.op("dve", lambda e, c=c, hbuf=hbuf: e.scalar_tensor_tensor(
                out=hbuf[:], in0=xt[:, c, :], scalar=rstd[:, c:c + 1], in1=rt[:, rtidx, :], op0=ALU.mult,
                op1=ALU.mult), r=[("xt", c), ("rstd", c), ("rt", rtidx)], w=[hkey])
            transpose_into(hbuf, dstT, c, hkey, dstkey)

    def transpose_into(srcb, dstT, c, srckey, dstkey):
        p = npst()
        for kt in range(8):
            S.op("pe", lambda e, kt=kt: e.transpose(out=pst[p][:, kt * 128:(kt + 1) * 128],
                                                    in_=srcb[:, kt * 128:(kt + 1) * 128], identity=ident[:]),
                 r=[srckey, "ident"], w=[("pst", p)])
        S.op("act", lambda e: e.copy(out=dstT[:, :, c * 128:(c + 1) * 128],
                                     in_=pst[p][:].rearrange("p (k t) -> p k t", k=8)),
             r=[("pst", p)], w=[dstkey])

    hT = carve(0, [8, G], BF16)
    wp.append(carve(75776, [8, 256], BF16))
    wp.append(carve(79872, [8, 256], BF16))
    tT = carve(8192, [NFT, G], BF16)
    wd = [carve(30720 + i * 22528, [NFT, 512], BF16) for i in range(2)]
    sg = [sb("sg%d" % i, [128, 512]) for i in range(2)]
    gtmp = sb("gtmp", [128, 256])

    def ffn(rtidx, Wg, Wu, Wd, load_wd=True):
        cnt["nwp"] = 6
        norm_group(rtidx, hT, "hT")
        for blk in range(NFT // 2):
            ig = load_w(Wg, blk * 256)
            iu = load_w(Wu, blk * 256)
            for j in range(2):
                ft = blk * 2 + j
                pg, pu = nps(), nps()
                for kt in range(8):
                    S.op("pe", lambda e, kt=kt, pg=pg, ig=ig, j=j: e.matmul(
                        ps[pg][:], lhsT=wp[ig][:, kt, j * 128:(j + 1) * 128], rhs=hT[:, kt, :],
                        start=(kt == 0), stop=(kt == 7)), r=[("wp", ig), "hT"], w=[("ps", pg)])
                for kt in range(8):
                    S.op("pe", lambda e, kt=kt, pu=pu, iu=iu, j=j: e.matmul(
                        ps[pu][:], lhsT=wp[iu][:, kt, j * 128:(j + 1) * 128], rhs=hT[:, kt, :],
                        start=(kt == 0), stop=(kt == 7)), r=[("wp", iu), "hT"], w=[("ps", pu)])
                si = ft % 2
                if j == 0 and load_wd:
                    for half in range(2):
                        S.dma("pool", lambda e, half=half, blk=blk: e.dma_start(
                            out=wd[half][:, 2 * blk:2 * blk + 2, :],
                            in_=Wd[blk * 256:(blk + 1) * 256, half * 512:(half + 1) * 512].rearrange(
                                "(ft p) c -> p ft c", p=128)),
                            "wd%d_%d" % (half, blk), w=[("wd", half, blk)])
                S.op("act", lambda e, pg=pg, si=si: e.activation(out=sg[si][:], in_=ps[pg][:], func=AF.Silu),
                     r=[("ps", pg)], w=[("sg", si)])
                S.op("dve", lambda e, pu=pu, si=si, ft=ft: e.tensor_tensor(out=tT[:, ft, :], in0=sg[si][:],
                                                                           in1=ps[pu][:], op=ALU.mult),
                     r=[("ps", pu), ("sg", si)], w=[("tT", ft)])
        for half in range(2):
            for tt in range(4):
                p = nps()
                for ft in range(NFT):
                    S.op("pe", lambda e, ft=ft, p=p, tt=tt, half=half: e.matmul(
                        ps[p][:], lhsT=tT[:, ft, tt * 128:(tt + 1) * 128], rhs=wd[half][:, ft, :],
                        start=(ft == 0), stop=(ft == NFT - 1)), r=[("tT", ft), ("wd", half, ft // 2)], w=[("ps", p)])
                S.op("dve", lambda e, p=p, tt=tt, half=half: e.scalar_tensor_tensor(
                    out=xt[:, tt, half * 512:(half + 1) * 512], in0=ps[p][:], scalar=0.5,
                    in1=xt[:, tt, half * 512:(half + 1) * 512], op0=ALU.mult, op1=ALU.add),
                    r=[("ps", p), ("xt", tt)], w=[("xt", tt)])
        cnt["nwp"] = 4

    def load_xt(src, g, srckey=None):
        for c in range(4):
            S.dma("sp", lambda e, c=c: e.dma_start(out=xt[:, c, :], in_=src[g * G + c * 128:g * G + (c + 1) * 128, :]),
                  "xt%d" % c, r=([(srckey, g, c)] if srckey else []), w=[("xt", c)])

    def store_xt(dst, g, dstkey):
        for c in range(4):
            S.dma("sp", lambda e, c=c: e.dma_start(out=dst[g * G + c * 128:g * G + (c + 1) * 128, :], in_=xt[:, c, :]),
                  "xt_st%d" % c, r=[("xt", c)], w=[(dstkey, g, c)])

    def rows(ap, g):
        return ap[g * G:(g + 1) * G, :].rearrange("(c p) d -> p c d", p=128)

    def own_rows(ap, g):
        return rows(ap, g)

    rt_load(SL_FFN1, RT_FFN1)
    rt_load(SL_MIX, RT_MIX)
    for g in [7, 6, 5, 4, 0, 1, 2, 3]:
        load_xt(xall, g)
        ffn(SL_FFN1, w1g, w1u, w1d, load_wd=(g == 7))
        if g < NG_OWN:
            store_xt(x1s, g, "x1s")
        norm_group(SL_MIX, h2T, "h2T")
        S.dma("sp", lambda e, g=g: e.dma_start(out=h2Ts[g], in_=h2T[:].rearrange("p k t -> p (k t)")), "h2T_st",
              r=["h2T"], w=[("h2Ts", g)])

    if stop_after == "A":
        return finish(nc, S, es, y, None)

    ktm = sb("ktm", [128, 4, D], BF16)
    vtm = sb("vtm", [128, 4, D], BF16)
    Vd = sb("Vd", [128, D], BF16)
    rA = sb("rA", [128, 2, 128])
    rB = sb("rB", [128, 2, 128])
    cstm = sb("cstm", [128, 2, 4, 128])

    def load_h2T(g):
        S.dma("sp", lambda e: e.dma_start(out=h2T[:].rearrange("p k t -> p (k t)"), in_=h2Ts[g]), "h2T_ld",
              r=[("h2Ts", g)], w=["h2T"])

    def proj_tm(off, bp, bi, consume):
        for cb in range(4):
            iw = load_w(win, off + cb * 256)
            for tt in range(4):
                p = nps()
                for kt in range(8):
                    S.op("pe", lambda e, kt=kt, p=p, tt=tt, iw=iw: e.matmul(
                        ps[p][:, 0:256], lhsT=h2T[:, kt, tt * 128:(tt + 1) * 128], rhs=wp[iw][:, kt, :],
                        start=(kt == 0), stop=False), r=["h2T", ("wp", iw)], w=[("ps", p)])
                S.op("pe", lambda e, p=p, cb=cb: e.matmul(
                    ps[p][:, 0:256], lhsT=ones33[bp:bp + 1, :], rhs=brow[bp:bp + 1, bi, cb * 256:(cb + 1) * 256],
                    start=False, stop=True), r=["ones33", ("brow", bp, bi)], w=[("ps", p)])
                consume(p, cb, tt)

    def kv_tm(g):
        for s2 in range(2):
            S.dma("sp", lambda e, s2=s2: e.dma_start(
                out=cstm[:, s2, :, :], in_=cs_tm[s2, g * G:(g + 1) * G, :].rearrange("(c p) f -> p c f", p=128)),
                "cstm%d" % s2, w=[("cstm", s2)])

        def k_consume(p, cb, tt):
            pv = ps[p][:, 0:256].rearrange("p (a b) -> p a b", a=2)
            S.op("dve", lambda e: e.tensor_tensor(
                out=rA[:], in0=pv, in1=cstm[:, 0, tt, :].unsqueeze(1).to_broadcast([128, 2, 128]),
                op=ALU.mult), r=[("ps", p), ("cstm", 0)], w=["rA"])
            S.op("dve", lambda e: e.tensor_tensor(
                out=rB[:], in0=pv, in1=cstm[:, 1, tt, :].unsqueeze(1).to_broadcast([128, 2, 128]),
                op=ALU.mult), r=[("ps", p), ("cstm", 1)], w=["rB"])
            S.op("dve", lambda e: e.tensor_tensor(
                out=ktm[:, tt, cb * 256:cb * 256 + 128], in0=rA[:, 0, :], in1=rB[:, 1, :], op=ALU.subtract),
                r=["rA", "rB"], w=[("ktm", tt)])
            S.op("dve", lambda e: e.tensor_tensor(
                out=ktm[:, tt, cb * 256 + 128:cb * 256 + 256], in0=rA[:, 1, :], in1=rB[:, 0, :], op=ALU.add),
                r=["rA", "rB"], w=[("ktm", tt)])

        def v_consume(p, cb, tt):
            S.op("act", lambda e: e.copy(out=vtm[:, tt, cb * 256:(cb + 1) * 256], in_=ps[p][:, 0:256]),
                 r=[("ps", p)], w=[("vtm", tt)])

        proj_tm(OFF_K, 0, 0, k_consume)
        proj_tm(OFF_V, 0, 1, v_consume)

    def state_mm(c, kd):
        for h in range(4):
            S.op("act", lambda e, h=h: e.activation(out=Vd[:, h * 256:(h + 1) * 256], in_=vtm[:, c, h * 256:(h + 1) * 256],
                                                    func=AF.Copy, scale=dec[:, kd + h:kd + h + 1]),
                 r=[("vtm", c), "dec"], w=["Vd"])
        pids = []
        for h in range(4):
            p = nps()
            pids.append(p)
            for dt in range(2):
                S.op("pe", lambda e, h=h, dt=dt, p=p: e.matmul(
                    ps[p][:, dt * 256:(dt + 1) * 256], lhsT=ktm[:, c, h * 256 + dt * 128:h * 256 + (dt + 1) * 128],
                    rhs=Vd[:, h * 256:(h + 1) * 256], start=True, stop=True),
                    r=[("ktm", c), "Vd"], w=[("ps", p)])
        return pids

    def state_acc(Sin, inkey, Sout, outkey, gd, pids):
        for h in range(4):
            p = pids[h]
            S.op("dve", lambda e, h=h, p=p: e.scalar_tensor_tensor(
                out=Sout[:, 2 * h:2 * h + 2, :], in0=Sin[:, 2 * h:2 * h + 2, :], scalar=dec[:, gd + h:gd + h + 1],
                in1=ps[p][:].rearrange("p (a b) -> p a b", a=2), op0=ALU.mult, op1=ALU.add),
                r=[inkey, ("ps", p), "dec"], w=[outkey])

    def state_update(St, stkey, c, kd, gd):
        state_acc(St, stkey, St, stkey, gd, state_mm(c, kd))

    for g in [7, 6, 5, 4]:
        load_h2T(g)
        kv_tm(g)
        for c in [3, 2, 1, 0]:
            state_update(Tst, "Tst", c, KD2, G2)

    if debug:
        S.dma("sp", lambda e: e.dma_start(out=dbg_state[:, :], in_=Tst[:].rearrange("p a b -> p (a b)")), "dbg",
              r=["Tst"], w=["dbgs"])
    if stop_after == "B":
        return finish(nc, S, es, y, None)

    barrier(FFN_KEYS, MIX_KEYS)
    qT = carve(0, [8, G], BF16)
    kT = carve(8192, [8, G], BF16)
    rpg = carve(16384, [4, D])
    sgr = carve(32768, [4, D])
    rgT = carve(49152, [8, G], BF16)
    rr = carve(57344, [1, D])[:, 0, :]
    qtmp = carve(61440, [2, G])
    fA = carve(65536, [2, G])
    fB = carve(69632, [2, G])
    csfm = carve(73728, [2, G])
    rp = carve(77824, [1, D])[:, 0, :]
    ri = carve(81920, [1, 256])[:, 0, :]
    PT = [carve(82944 + 256 * i, [1, 128], BF16)[:, 0, :] for i in range(2)]
    sga = carve(61440, [8, G], BF16)
    sgb = carve(69632, [8, G], BF16)
    mixT = carve(77824, [8, G], BF16)

    def proj_fm_rot(off, ctoff, dst, dstkey):
        for blk in range(4):
            iw = load_w(win, off + blk * 256)
            for j in range(2):
                jt = 2 * blk + j
                p = nps()
                for kt in range(8):
                    S.op("pe", lambda e, kt=kt, p=p, j=j, iw=iw: e.matmul(
                        ps[p][:], lhsT=wp[iw][:, kt, j * 128:(j + 1) * 128], rhs=h2T[:, kt, :],
                        start=(kt == 0), stop=(kt == 7)), r=["h2T", ("wp", iw)], w=[("ps", p)])
                S.op("dve", lambda e, p=p, j=j, jt=jt: ts_ap(e, qtmp[:, j, :], ps[p][:], ct[:, ctoff + jt:ctoff + jt + 1], ALU.add), r=[("ps", p), "ct"], w=["qtmp"])
            S.op("dve", lambda e: e.tensor_tensor(
                out=fA[:], in0=qtmp[:], in1=csfm[:, 0, :].unsqueeze(1).to_broadcast([128, 2, G]), op=ALU.mult),
                r=["qtmp", "csfm"], w=["fA"])
            S.op("dve", lambda e: e.tensor_tensor(
                out=fB[:], in0=qtmp[:], in1=csfm[:, 1, :].unsqueeze(1).to_broadcast([128, 2, G]), op=ALU.mult),
                r=["qtmp", "csfm"], w=["fB"])
            S.op("dve", lambda e, blk=blk: e.tensor_tensor(out=dst[:, 2 * blk, :], in0=fA[:, 0, :], in1=fB[:, 1, :],
                                                           op=ALU.subtract), r=["fA", "fB"], w=[dstkey])
            S.op("dve", lambda e, blk=blk: e.tensor_tensor(out=dst[:, 2 * blk + 1, :], in0=fA[:, 1, :],
                                                           in1=fB[:, 0, :], op=ALU.add), r=["fA", "fB"], w=[dstkey])

    for g in range(NG_OWN):
        load_h2T(g)
        kv_tm(g)
        S.dma("sp", lambda e, g=g: e.dma_start(out=rows(ktms, g), in_=ktm[:]), "ktm_st",
              r=[("ktm", i) for i in range(4)], w=[("ktms", g)])
        S.dma("sp", lambda e, g=g: e.dma_start(out=rows(vtms, g), in_=vtm[:]), "vtm_st",
              r=[("vtm", i) for i in range(4)], w=[("vtms", g)])
        S.dma("sp", lambda e, g=g: e.dma_start(
            out=csfm[:], in_=cs_fm[:, :, g * G:(g + 1) * G].rearrange("s p t -> p s t")), "csfm", w=["csfm"])
        proj_fm_rot(OFF_Q, CT_BQ, qT, "qT")
        proj_fm_rot(OFF_K, CT_BK, kT, "kT")
        if debug and g == 0:
            S.dma("sp", lambda e: e.dma_start(out=dbg_kT[:, :], in_=kT[:].rearrange("p k t -> p (k t)")), "dbg5",
                  r=["kT"], w=["dbg5"])
        S.dma("sp", lambda e, g=g: e.dma_start(out=qTs[g], in_=qT[:].rearrange("p k t -> p (k t)")), "qT_st",
              r=["qT"], w=[("qTs", g)])
        for c in range(4):
            cs = slice(c * 128, (c + 1) * 128)
            state_update(Sst, "Sst", c, KD1, G1)
            for h in range(4):
                p1 = nps()
                for dt in range(2):
                    S.op("pe", lambda e, h=h, dt=dt, p1=p1, cs=cs: e.matmul(
                        ps[p1][:, 0:128], lhsT=kT[:, 2 * h + dt, cs], rhs=qT[:, 2 * h + dt, cs],
                        start=(dt == 0), stop=(dt == 1)), r=["kT", "qT"], w=[("ps", p1)])
                pi = h % 2
                S.op("dve", lambda e, h=h, p1=p1, pi=pi: e.tensor_tensor(out=PT[pi][:], in0=ps[p1][:, 0:128],
                                                                         in1=MT[:, h, :], op=ALU.mult),
                     r=[("ps", p1), "MT"], w=[("PT", pi)])
                p2 = nps()
                S.op("pe", lambda e, h=h, p2=p2, pi=pi, c=c: e.matmul(
                    ps[p2][:, 0:256], lhsT=PT[pi][:], rhs=vtm[:, c, h * 256:(h + 1) * 256], start=True, stop=True),
                    r=[("PT", pi), ("vtm", c)], w=[("ps", p2)])
                for dt in range(2):
                    S.op("pe", lambda e, h=h, dt=dt, p2=p2, cs=cs: e.matmul(
                        ps[p2][:, 256:512], lhsT=qT[:, 2 * h + dt, cs], rhs=Stb[:, 2 * h + dt, :],
                        start=(dt == 0), stop=(dt == 1)), r=["qT", "Stb"], w=[("ps", p2)])
                S.op("act", lambda e, p2=p2: e.copy(out=ri[:], in_=ps[p2][:, 0:256]), r=[("ps", p2)], w=["ri"])
                S.op("dve", lambda e, h=h, p2=p2: e.scalar_tensor_tensor(
                    out=rp[:, h * 256:(h + 1) * 256], in0=ps[p2][:, 256:512], scalar=dec[:, QD1 + h:QD1 + h + 1],
                    in1=ri[:], op0=ALU.mult, op1=ALU.add), r=[("ps", p2), "ri", "dec"], w=["rp"])
            S.dma("sp", lambda e, g=g, c=c: e.dma_start(out=rps[g * G + c * 128:g * G + (c + 1) * 128, :], in_=rp[:]),
                  "rp_st", r=["rp"], w=[("rps", g, c)])
            S.op("act", lambda e: e.copy(out=Stb[:], in_=Sst[:]), r=["Sst"], w=["Stb"])

    if stop_after == "C":
        return finish(nc, S, es, y, None)

    Tbufs = [(Tst, "Tst"), (Sst, "Sst")]
    for g in [3, 2, 1, 0]:
        load_h2T(g)
        S.dma("sp", lambda e, g=g: e.dma_start(out=qT[:].rearrange("p k t -> p (k t)"), in_=qTs[g]), "qT_ld",
              r=[("qTs", g)], w=["qT"])
        S.dma("sp", lambda e, g=g: e.dma_start(out=ktm[:], in_=rows(ktms, g)), "ktm_ld", r=[("ktms", g)],
              w=[("ktm", i) for i in range(4)])
        S.dma("sp", lambda e, g=g: e.dma_start(out=vtm[:], in_=rows(vtms, g)), "vtm_ld", r=[("vtms", g)],
              w=[("vtm", i) for i in range(4)])
        S.dma("sp", lambda e, g=g: e.dma_start(out=rpg[:], in_=rows(rps, g)), "rpg_ld",
              r=[("rps", g, c) for c in range(4)], w=["rpg"] + [("rpgc", c) for c in range(4)])
        def gr_piece(cb):
            iw = load_w(win, OFF_GR + cb * 256)
            for tt in range(4):
                p = nps()
                for kt in range(8):
                    S.op("pe", lambda e, kt=kt, p=p, tt=tt, iw=iw: e.matmul(
                        ps[p][:, 0:256], lhsT=h2T[:, kt, tt * 128:(tt + 1) * 128], rhs=wp[iw][:, kt, :],
                        start=(kt == 0), stop=False), r=["h2T", ("wp", iw)], w=[("ps", p)])
                S.op("pe", lambda e, p=p, cb=cb: e.matmul(
                    ps[p][:, 0:256], lhsT=ones33[32:33, :], rhs=brow[32:33, 0, cb * 256:(cb + 1) * 256],
                    start=False, stop=True), r=["ones33", ("brow", 32, 0)], w=[("ps", p)])
                S.op("act", lambda e, cb=cb, tt=tt, p=p: e.activation(
                    out=sgr[:, tt, cb * 256:(cb + 1) * 256], in_=ps[p][:, 0:256], func=AF.Silu),
                    r=[("ps", p)], w=[("sgr", tt)])

        for c in [3, 2, 1, 0]:
            cs = slice(c * 128, (c + 1) * 128)
            cur, curk = Tbufs[0]
            nxt, nxtk = Tbufs[1]
            pids = state_mm(c, KD2)
            S.op("act", lambda e, cur=cur: e.copy(out=Stb[:], in_=cur[:]), r=[curk], w=["Stb"])
            state_acc(cur, curk, nxt, nxtk, G2, pids)
            Tbufs.reverse()
            gr_piece(3 - c)
            for h in range(4):
                p = nps()
                for dt in range(2):
                    S.op("pe", lambda e, h=h, dt=dt, p=p, cs=cs: e.matmul(
                        ps[p][:, 0:256], lhsT=qT[:, 2 * h + dt, cs], rhs=Stb[:, 2 * h + dt, :],
                        start=(dt == 0), stop=(dt == 1)), r=["qT", "Stb"], w=[("ps", p)])
                S.op("dve", lambda e, h=h, p=p, c=c: e.scalar_tensor_tensor(
                    out=rpg[:, c, h * 256:(h + 1) * 256], in0=ps[p][:, 0:256], scalar=dec[:, QD2 + h:QD2 + h + 1],
                    in1=rpg[:, c, h * 256:(h + 1) * 256], op0=ALU.mult, op1=ALU.add),
                    r=[("ps", p), ("rpgc", c), "rpg", "dec"], w=[("rpgc", c)])
        for c in range(4):
            for h in range(4):
                S.op("act", lambda e, h=h, c=c: e.activation(
                    out=junk[:, 0:256], in_=rpg[:, c, h * 256:(h + 1) * 256], func=AF.Square,
                    accum_out=ss[:, 16 + 4 * c + h:17 + 4 * c + h]), r=[("rpgc", c), "rpg"], w=[("ssd", c)])
        allss = [("ssd", c) for c in range(4)]
        S.op("dve", lambda e: e.tensor_scalar(out=rstd[:, 16:32], in0=ss[:, 16:32], scalar1=1.0 / 256, scalar2=EPS,
                                              op0=ALU.mult, op1=ALU.add), r=allss, w=["rstdd"])
        S.op("act", lambda e: e.activation(out=rstd[:, 16:32], in_=rstd[:, 16:32], func=AF.Sqrt), r=["rstdd"], w=["rstdd"])
        S.op("dve", lambda e: e.reciprocal(out=rstd[:, 16:32], in_=rstd[:, 16:32]), r=["rstdd"], w=["rstdd"])
        for c in range(4):
            hbuf, hkey = (hb, "hb") if c % 2 == 0 else (hb2, "hb2")
            for h in range(4):
                S.op("dve", lambda e, h=h, c=c, hbuf=hbuf: e.scalar_tensor_tensor(
                    out=hbuf[:, h * 256:(h + 1) * 256], in0=rpg[:, c, h * 256:(h + 1) * 256],
                    scalar=rstd[:, 16 + 4 * c + h:17 + 4 * c + h], in1=sgr[:, c, h * 256:(h + 1) * 256],
                    op0=ALU.mult, op1=ALU.mult), r=[("rpgc", c), "rpg", "rstdd", ("sgr", c)], w=[hkey])
            transpose_into(hbuf, rgT, c, hkey, "rgT")
        S.dma("sp", lambda e, g=g: e.dma_start(out=rgTs[g], in_=rgT[:].rearrange("p k t -> p (k t)")), "rgT_st",
              r=["rgT"], w=[("rgTs", g)])

    if stop_after == "D":
        return finish(nc, S, es, y, None)

    vaf = rpg
    vn = ktm
    uT = kT
    aT = qT
    barrier(CTMP_KEYS, E1_KEYS)
    rt_load(SL_BVA, RT_BVA)
    rt_load(SL_SGUG, RT_SGUG)
    rt_load(SL_SGUB, RT_SGUB)
    maT = sgr
    maTv = maT[:].rearrange("p a b -> p (a b)").rearrange("p (k t) -> p k t", k=8)
    for g in range(NG_OWN):
        load_h2T(g)
        S.dma("sp", lambda e, g=g: e.dma_start(out=rgT[:].rearrange("p k t -> p (k t)"), in_=rgTs[g]), "rgT_ld",
              r=[("rgTs", g)], w=["rgT"])
        load_xt(x1s, g, "x1s")
        for cb in range(4):
            iw = load_w(win, OFF_VA + cb * 256)
            for tt in range(4):
                p = nps()
                for kt in range(8):
                    S.op("pe", lambda e, kt=kt, p=p, tt=tt, iw=iw: e.matmul(
                        ps[p][:, 0:256], lhsT=h2T[:, kt, tt * 128:(tt + 1) * 128], rhs=wp[iw][:, kt, :],
                        start=(kt == 0), stop=(kt == 7)), r=["h2T", ("wp", iw)], w=[("ps", p)])
                S.op("dve", lambda e, p=p, cb=cb, tt=tt: e.tensor_tensor(
                    out=vaf[:, tt, cb * 256:(cb + 1) * 256], in0=ps[p][:, 0:256],
                    in1=rt[:, SL_BVA, cb * 256:(cb + 1) * 256], op=ALU.add), r=[("ps", p), ("rt", SL_BVA)], w=["rpg"])
        for tt in range(4):
            S.op("dve", lambda e: e.memset(ss[:, 8:10], 0.0), w=["ss01"])
            S.op("act", lambda e, tt=tt: e.activation(out=vaf[:, tt, :], in_=vaf[:, tt, :], func=AF.Gelu,
                                                      accum_out=ss[:, 8:9]), r=["rpg", "ss01"], w=["rpg", "ss01"])
            S.op("dve", lambda e: e.tensor_scalar(out=ss[:, 10:11], in0=ss[:, 8:9], scalar1=1.0 / D, scalar2=0.0,
                                                  op0=ALU.mult, op1=ALU.add), r=["ss01"], w=["ssm"])
            S.op("dve", lambda e, tt=tt: ts_ap(e, vaf[:, tt, :], vaf[:, tt, :], ss[:, 10:11], ALU.subtract), r=["rpg", "ssm"], w=["rpg"])
            S.op("act", lambda e, tt=tt: e.activation(out=rr[:], in_=vaf[:, tt, :], func=AF.Square,
                                                      accum_out=ss[:, 9:10]), r=["rpg", "ss01"], w=["rr", "ss01"])
            S.op("dve", lambda e: e.tensor_scalar(out=ss[:, 11:12], in0=ss[:, 9:10], scalar1=1.0 / D, scalar2=EPS,
                                                  op0=ALU.mult, op1=ALU.add), r=["ss01"], w=["ssr"])
            S.op("act", lambda e: e.activation(out=ss[:, 11:12], in_=ss[:, 11:12], func=AF.Sqrt), r=["ssr"], w=["ssr"])
            S.op("dve", lambda e: e.reciprocal(out=ss[:, 11:12], in_=ss[:, 11:12]), r=["ssr"], w=["ssr"])
            S.op("dve", lambda e, tt=tt: e.scalar_tensor_tensor(
                out=vaf[:, tt, :], in0=vaf[:, tt, :], scalar=ss[:, 11:12], in1=rt[:, SL_SGUG, :], op0=ALU.mult,
                op1=ALU.mult), r=["rpg", "ssr", ("rt", SL_SGUG)], w=["rpg"])
            S.op("dve", lambda e, tt=tt: e.tensor_tensor(out=vn[:, tt, :], in0=vaf[:, tt, :], in1=rt[:, SL_SGUB, :],
                                                         op=ALU.add), r=["rpg", ("rt", SL_SGUB)], w=[("ktm", tt)])
        for blk in range(4):
            iw = load_w(win, OFF_U + blk * 256)
            for j in range(2):
                jt = 2 * blk + j
                p = nps()
                for kt in range(8):
                    S.op("pe", lambda e, kt=kt, p=p, j=j, iw=iw: e.matmul(
                        ps[p][:], lhsT=wp[iw][:, kt, j * 128:(j + 1) * 128], rhs=h2T[:, kt, :],
                        start=(kt == 0), stop=(kt == 7)), r=["h2T", ("wp", iw)], w=[("ps", p)])
                si = jt % 2
                S.op("dve", lambda e, p=p, jt=jt, si=si: ts_ap(e, sg[si][:], ps[p][:], ct[:, CT_BU + jt:CT_BU + jt + 1], ALU.add), r=[("ps", p), "ct"], w=[("sg", si)])
                S.op("act", lambda e, jt=jt, si=si: e.activation(out=uT[:, jt, :], in_=sg[si][:], func=AF.Gelu),
                     r=[("sg", si)], w=["kT"])
        for c in range(4):
            cs = slice(c * 128, (c + 1) * 128)
            for fh in range(2):
                p = nps()
                for f4 in range(4):
                    ft = fh * 4 + f4
                    gg = ft // 2
                    S.op("pe", lambda e, ft=ft, f4=f4, gg=gg, p=p, c=c: e.matmul(
                        ps[p][:, f4 * 128:(f4 + 1) * 128], lhsT=vn[:, c, ft * 128:(ft + 1) * 128],
                        rhs=wsT[:, gg * 128:(gg + 1) * 128], start=True, stop=False),
                        r=[("ktm", c), "wsT"], w=[("ps", p)])
                    S.op("pe", lambda e, f4=f4, gg=gg, p=p: e.matmul(
                        ps[p][:, f4 * 128:(f4 + 1) * 128], lhsT=ones[0:1, :], rhs=bsr[0:1, gg * 128:(gg + 1) * 128],
                        start=False, stop=True), r=["ones", "bsr"], w=[("ps", p)])
                S.op("dve", lambda e, fh=fh, p=p, cs=cs: e.tensor_tensor(
                    out=aT[:, fh * 4:(fh + 1) * 4, cs], in0=uT[:, fh * 4:(fh + 1) * 4, cs],
                    in1=ps[p][:].rearrange("p (a b) -> p a b", a=4), op=ALU.mult), r=["kT", ("ps", p)], w=["qT"])
        for (off, cto, dst, dk) in ((OFF_GA, CT_BGA, sga, "sga"), (OFF_GB, CT_BGB, sgb, "sgb")):
            for blk in range(4):
                iw = load_w(win, off + blk * 256)
                for j in range(2):
                    jt = 2 * blk + j
                    p = nps()
                    for kt in range(8):
                        S.op("pe", lambda e, kt=kt, p=p, j=j, iw=iw: e.matmul(
                            ps[p][:], lhsT=wp[iw][:, kt, j * 128:(j + 1) * 128], rhs=h2T[:, kt, :],
                            start=(kt == 0), stop=(kt == 7)), r=["h2T", ("wp", iw)], w=[("ps", p)])
                    si = jt % 2
                    S.op("dve", lambda e, p=p, jt=jt, si=si, cto=cto: ts_ap(e, sg[si][:], ps[p][:], ct[:, cto + jt:cto + jt + 1], ALU.add), r=[("ps", p), "ct"], w=[("sg", si)])
                    S.op("act", lambda e, jt=jt, dst=dst, si=si: e.activation(
                        out=dst[:, jt, :], in_=sg[si][:], func=AF.Sigmoid), r=[("sg", si)], w=[dk])
        for blk in range(4):
            iw = load_w(wa, blk * 256)
            for j in range(2):
                jt = 2 * blk + j
                p = nps()
                for kt in range(8):
                    S.op("pe", lambda e, kt=kt, p=p, j=j, iw=iw: e.matmul(
                        ps[p][:], lhsT=wp[iw][:, kt, j * 128:(j + 1) * 128], rhs=aT[:, kt, :],
                        start=(kt == 0), stop=(kt == 7)), r=["qT", ("wp", iw)], w=[("ps", p)])
                S.op("dve", lambda e, p=p, jt=jt: e.tensor_tensor(out=maTv[:, jt, :], in0=ps[p][:], in1=sga[:, jt, :],
                                                                  op=ALU.mult), r=[("ps", p), "sga"],
                     w=[("sgr", i) for i in range(4)])
        for blk in range(4):
            iw = load_w(wb, blk * 256)
            for j in range(2):
                jt = 2 * blk + j
                p = nps()
                for kt in range(8):
                    S.op("pe", lambda e, kt=kt, p=p, j=j, iw=iw: e.matmul(
                        ps[p][:], lhsT=wp[iw][:, kt, j * 128:(j + 1) * 128], rhs=rgT[:, kt, :],
                        start=(kt == 0), stop=(kt == 7)), r=["rgT", ("wp", iw)], w=[("ps", p)])
                si = jt % 2
                S.op("dve", lambda e, p=p, jt=jt, si=si: e.tensor_tensor(out=sg[si][:], in0=ps[p][:],
                                                                         in1=sgb[:, jt, :], op=ALU.mult),
                     r=[("ps", p), "sgb"], w=[("sg", si)])
                S.op("dve", lambda e, jt=jt, si=si: e.tensor_tensor(out=mixT[:, jt, :], in0=sg[si][:],
                                                                    in1=maTv[:, jt, :], op=ALU.add),
                     r=[("sg", si)] + [("sgr", i) for i in range(4)], w=["mixT"])
        for cb in range(4):
            iw = load_w(wo, cb * 256)
            for tt in range(4):
                p = nps()
                for kt in range(8):
                    S.op("pe", lambda e, kt=kt, p=p, tt=tt, iw=iw: e.matmul(
                        ps[p][:, 0:256], lhsT=mixT[:, kt, tt * 128:(tt + 1) * 128], rhs=wp[iw][:, kt, :],
                        start=(kt == 0), stop=(kt == 7)), r=["mixT", ("wp", iw)], w=[("ps", p)])
                S.op("dve", lambda e, p=p, cb=cb, tt=tt: e.tensor_tensor(
                    out=xt[:, tt, cb * 256:(cb + 1) * 256], in0=ps[p][:, 0:256],
                    in1=xt[:, tt, cb * 256:(cb + 1) * 256], op=ALU.add), r=[("ps", p), ("xt", tt)], w=[("xt", tt)])
        store_xt(x2s, g, "x2s")

    if stop_after == "E1":
        return finish(nc, S, es, y, None)

    barrier(MIX_KEYS + E1_KEYS, FFN_KEYS)
    rt_load(SL_FFN2, RT_FFN2)
    rt_load(SL_FINAL, RT_FINAL)
    ykeys = []
    for g in range(NG_OWN):
        load_xt(x2s, g, "x2s")
        ffn(SL_FFN2, w2g, w2u, w2d, load_wd=(g == 0))
        for c in range(4):
            S.op("act", lambda e, c=c: e.activation(out=junk[:], in_=xt[:, c, :], func=AF.Square,
                                                    accum_out=ss[:, c:c + 1]), r=[("xt", c)], w=[("ss", c)])
        allss = [("ss", c) for c in range(4)]
        allr = [("rstd", c) for c in range(4)]
        S.op("dve", lambda e: e.tensor_scalar(out=rstd[:, 0:4], in0=ss[:, 0:4], scalar1=1.0 / D, scalar2=EPS,
                                              op0=ALU.mult, op1=ALU.add), r=allss, w=allr)
        S.op("act", lambda e: e.activation(out=rstd[:, 0:4], in_=rstd[:, 0:4], func=AF.Sqrt), r=allr, w=allr)
        S.op("dve", lambda e: e.reciprocal(out=rstd[:, 0:4], in_=rstd[:, 0:4]), r=allr, w=allr)
        for c in range(4):
            S.op("dve", lambda e, c=c: e.scalar_tensor_tensor(out=xt[:, c, :], in0=xt[:, c, :],
                                                              scalar=rstd[:, c:c + 1], in1=rt[:, SL_FINAL, :],
                                                              op0=ALU.mult, op1=ALU.mult),
                 r=[("xt", c), ("rstd", c), ("rt", SL_FINAL)], w=[("xt", c)])
        store_xt(y, g, "y")
    return finish(nc, S, es, y, ykeys)


def finish(nc, S, es, y, ykeys):
    allw = set()
    for o in S.ops:
        if o["dma"]:
            allw.update(o["w"])
    S.op("sp", None, r=sorted(allw, key=str))
    S.emit()
    es.close()
    return nc


def _host_inputs(inputs):
    f = lambda a: np.ascontiguousarray(np.asarray(a, dtype=np.float32))
    x = f(inputs["x"])
    L = 0
    rep = lambda v: np.ascontiguousarray(np.broadcast_to(f(v).reshape(1, -1), (128, v.size)))
    b_in = f(inputs["b_in"])[L]
    col = lambda v: np.ascontiguousarray(f(v).reshape(8, 128).T)
    rowtab = np.stack([rep(f(inputs["ffn1_norm"])[L]), rep(f(inputs["mix_norm"])[L]),
                       rep(f(inputs["ffn2_norm"])[L]), rep(f(inputs["final_norm"])),
                       rep(f(inputs["sgu_norm_g"])[L]), rep(f(inputs["sgu_norm_b"])[L]),
                       rep(b_in[OFF_K:OFF_K + D]), rep(b_in[OFF_V:OFF_V + D]),
                       rep(b_in[OFF_GR:OFF_GR + D]), rep(b_in[OFF_VA:OFF_VA + D])], axis=0)
    ws = f(inputs["sgu_w_s"])[L]
    bs = f(inputs["sgu_b_s"])[L]
    logit = f(inputs["ret_decay_logit"])[L]
    p = np.arange(128, dtype=np.float32)
    consts = np.concatenate([p[:, None], p[:, None] + 1.0, (p[None, :] - p[:, None])], axis=1).astype(np.float32)
    ident = np.eye(128, dtype=np.float32)
    theta = (10000.0 ** (-np.arange(0, 256, 2, dtype=np.float32) / np.float32(256))).astype(np.float32)
    shared = dict(
        w1g=f(inputs["ffn1_w_gate"])[L], w1u=f(inputs["ffn1_w_up"])[L], w1d=f(inputs["ffn1_w_down"])[L],
        w2g=f(inputs["ffn2_w_gate"])[L], w2u=f(inputs["ffn2_w_up"])[L], w2d=f(inputs["ffn2_w_down"])[L],
        win=f(inputs["w_in"])[L], wa=f(inputs["w_branch_a"])[L], wb=f(inputs["w_branch_b"])[L],
        wo=f(inputs["w_out"])[L], rowtab=np.ascontiguousarray(rowtab), consts=consts, ident=ident)
    maps = []
    for core in range(8):
        b, half = core // 2, core % 2
        xs = x[b] if half == 0 else x[b, ::-1]
        pos = np.arange(SEQ, dtype=np.float32) if half == 0 else np.arange(SEQ - 1, -1, -1, dtype=np.float32)
        ang = (pos[:, None] * theta[None, :]).astype(np.float32)
        cs_tm = np.stack([np.cos(ang), np.sin(ang)]).astype(np.float32)
        cs_fm = np.ascontiguousarray(cs_tm[:, :HALF, :].transpose(0, 2, 1))
        if half == 0:
            ws_l, bs_l, lg_l = ws, bs, logit
        else:
            ws_l, bs_l, lg_l = ws[:, ::-1, ::-1], bs[:, ::-1], logit[::-1]
        wst = np.ascontiguousarray(ws_l.transpose(2, 0, 1).reshape(128, 512))
        coltab = np.concatenate([col(b_in[OFF_Q:OFF_Q + D]), col(b_in[OFF_K:OFF_K + D]), col(b_in[OFF_U:OFF_U + D]),
                                 col(b_in[OFF_GA:OFF_GA + D]), col(b_in[OFF_GB:OFF_GB + D]),
                                 np.broadcast_to(np.ascontiguousarray(lg_l).reshape(1, 8), (128, 8))], axis=1)
        m = dict(shared)
        m.update(xall=np.ascontiguousarray(xs), coltab=np.ascontiguousarray(coltab.astype(np.float32)), wst=wst,
                 bsrow=np.ascontiguousarray(bs_l.reshape(1, 512)), cs_fm=cs_fm, cs_tm=np.ascontiguousarray(cs_tm))
        maps.append(m)
    return maps


def kernel(**inputs):
    maps = _host_inputs(inputs)
    nc = build_program()
    res = run_bass_kernel_spmd(nc, maps, core_ids=list(range(8)))
    out = np.empty((4, SEQ, D), dtype=np.float32)
    for core in range(8):
        b, half = core // 2, core % 2
        yc = np.asarray(res.results[core]["y"], dtype=np.float32)
        if half == 0:
            out[b, :HALF] = yc
        else:
            out[b, HALF:] = yc[::-1]
    return out
```

```python
import os
import numpy as np
import ml_dtypes
from contextlib import ExitStack
import concourse.bass as bass
import concourse.mybir as mybir
from concourse.bass_utils import run_bass_kernel_spmd

F32 = mybir.dt.float32
BF16 = mybir.dt.bfloat16
AF = mybir.ActivationFunctionType
ALU = mybir.AluOpType

D = 1024
DFF = 2816
NFT = DFF // 128
SEQ = 4096
HALF = 2048
G = 512
NG_ALL = SEQ // G
NG_OWN = HALF // G
EPS = 1e-6
OFF_U, OFF_VA, OFF_Q, OFF_K, OFF_V, OFF_GR, OFF_GA, OFF_GB = [i * 1024 for i in range(8)]
RT_FFN1, RT_MIX, RT_FFN2, RT_FINAL, RT_SGUG, RT_SGUB, RT_BK, RT_BV, RT_BGR, RT_BVA = range(10)
NRT = 10
SL_FFN1, SL_MIX, SL_BK, SL_BV, SL_BGR, SL_BVA, SL_SGUG, SL_SGUB, SL_FFN2, SL_FINAL = 0, 1, 2, 3, 0, 1, 2, 3, 0, 1
CT_BQ, CT_BK, CT_BU, CT_BGA, CT_BGB, CT_LOGIT = 0, 8, 16, 24, 32, 40
NCT = 48


class Sched:
    def __init__(self, nc):
        self.nc = nc
        self.ops = []

    def op(self, eng, fn, r=(), w=()):
        self.ops.append(dict(eng=eng, fn=fn, r=tuple(r), w=tuple(w), dma=False, chan=None, signal=False))

    def dma(self, eng, fn, chan, r=(), w=()):
        self.ops.append(dict(eng=eng, fn=fn, r=tuple(r), w=tuple(w), dma=True, chan=("ch", chan), signal=True))

    def emit(self):
        nc = self.nc
        ops = self.ops
        last_w, readers = {}, {}
        for i, o in enumerate(ops):
            deps = set()
            for k in o["r"]:
                if k in last_w:
                    deps.add(last_w[k])
            for k in o["w"]:
                if k in last_w:
                    deps.add(last_w[k])
                deps.update(readers.get(k, ()))
            deps.discard(i)
            o["deps"] = deps
            for k in o["r"]:
                readers.setdefault(k, []).append(i)
            for k in o["w"]:
                last_w[k] = i
                readers[k] = []
        for i, o in enumerate(ops):
            keep = set()
            for j in o["deps"]:
                p = ops[j]
                if (not p["dma"]) and (not o["dma"]) and p["eng"] == o["eng"] == "pe":
                    continue
                if not p["dma"]:
                    p["signal"] = True
                keep.add(j)
            o["deps"] = keep
        counts = {}
        for o in ops:
            key = o["chan"] if o["dma"] else ("eng", o["eng"])
            o["key"] = key
            if o["signal"] and o["fn"] is not None:
                counts[key] = counts.get(key, 0) + (16 if o["dma"] else 1)
                o["sigval"] = counts[key]
        keys = sorted(counts.keys(), key=str)
        with ExitStack() as es:
            sems = {k: es.enter_context(nc.semaphore("s%d" % n)) for n, k in enumerate(keys)}
            block = es.enter_context(nc.Block())
            engmap = dict(pe=block.tensor, act=block.scalar, dve=block.vector, pool=block.gpsimd, sp=block.sync)

            def make(engname):
                def body(eng):
                    waited = {}
                    for o in ops:
                        if o["eng"] != engname:
                            continue
                        need = {}
                        for j in o["deps"]:
                            p = ops[j]
                            need[p["key"]] = max(need.get(p["key"], 0), p["sigval"])
                        for k, v in sorted(need.items(), key=str):
                            if waited.get(k, 0) < v:
                                eng.wait_ge(sems[k], v)
                                waited[k] = v
                        if o["fn"] is None:
                            continue
                        ins = o["fn"](eng)
                        if o["signal"]:
                            ins.then_inc(sems[o["key"]], 16 if o["dma"] else 1)
                return body

            for name, deco in engmap.items():
                deco(make(name))


def build_program(debug=False, stop_after=None):
    nc = bass.Bass("TRN2", target_bir_lowering=False)
    ein = lambda name, shape, dt=F32: nc.dram_tensor(name, list(shape), dt, kind="ExternalInput").ap()
    xall = ein("xall", [SEQ, D])
    w1g, w1u, w1d = ein("w1g", [D, DFF]), ein("w1u", [D, DFF]), ein("w1d", [DFF, D])
    w2g, w2u, w2d = ein("w2g", [D, DFF]), ein("w2u", [D, DFF]), ein("w2d", [DFF, D])
    win = ein("win", [D, 8 * D])
    wa, wb, wo = ein("wa", [D, D]), ein("wb", [D, D]), ein("wo", [D, D])
    rowtab = ein("rowtab", [NRT, 128, D])
    coltab = ein("coltab", [128, NCT])
    wst = ein("wst", [128, 4 * 128])
    bsrow = ein("bsrow", [1, 4 * 128])
    ident_in = ein("ident", [128, 128])
    consts = ein("consts", [128, 2 + 128])
    cs_fm = ein("cs_fm", [2, 128, HALF])
    cs_tm = ein("cs_tm", [2, SEQ, 128])
    y = nc.dram_tensor("y", [HALF, D], F32, kind="ExternalOutput").ap()

    skind = "ExternalOutput" if debug else "Internal"
    scr = lambda name, shape, dt: nc.dram_tensor(name, list(shape), dt, kind=skind).ap()
    x1s = scr("x1s", [HALF, D], F32)
    h2Ts = scr("h2Ts", [NG_ALL, 128, 8 * G], BF16)
    rps = scr("rps", [HALF, D], F32)
    qTs = scr("qTs", [NG_OWN, 128, 8 * G], BF16)
    ktms = scr("ktms", [HALF, D], BF16)
    vtms = scr("vtms", [HALF, D], BF16)
    rgTs = scr("rgTs", [NG_OWN, 128, 8 * G], BF16)
    x2s = scr("x2s", [HALF, D], F32)
    dbg_state = scr("dbg_state", [128, 8 * 256], F32) if debug else None
    dbg_ct = scr("dbg_ct", [128, NCT], F32) if debug else None
    dbg_lg = scr("dbg_lg", [128, 8], F32) if debug else None
    dbg_dec = scr("dbg_dec", [128, 24], F32) if debug else None
    dbg_MT = scr("dbg_MT", [128, 512], F32) if debug else None
    dbg_kT = scr("dbg_kT", [128, 8 * G], BF16) if debug else None

    S = Sched(nc)
    es = ExitStack()
    sb = lambda name, shape, dt=F32: es.enter_context(nc.sbuf_tensor(name, list(shape), dt))
    rt = sb("rt", [128, 4, D])
    ct = sb("ct", [128, NCT])
    cst = sb("cst", [128, 130])
    ident = sb("identb", [128, 128], BF16)
    wsT = sb("wsT", [128, 512], BF16)
    bsr = sb("bsr", [1, 512], BF16)
    ones = sb("ones", [1, 128], BF16)
    ones33 = sb("ones33", [33, 128], BF16)
    brow = sb("brow", [33, 2, D], BF16)
    lg = sb("lg", [128, 8])
    dec = sb("dec", [128, 24])
    MT = sb("MT", [128, 4, 128])
    mtmp = sb("mtmp", [128, 2, 128])
    Sst = sb("Sst", [128, 8, 256])
    Tst = sb("Tst", [128, 8, 256])
    Stb = sb("Stb", [128, 8, 256], BF16)
    ss = sb("ss", [128, 32])
    rstd = sb("rstd", [128, 32])
    xt = sb("xt", [128, 4, D])
    hb = sb("hb", [128, D], BF16)
    hb2 = sb("hb2", [128, D], BF16)
    junk = sb("junk", [128, D], BF16)
    h2T = sb("h2T", [128, 8, G], BF16)
    wp = [sb("wp%d" % i, [128, 8, 256], BF16) for i in range(4)]
    ps = [es.enter_context(nc.psum_tensor("ps%d" % i, [128, 512], F32)) for i in range(6)]
    pst = [es.enter_context(nc.psum_tensor("pst%d" % i, [128, 1024], BF16)) for i in range(2)]
    cnt = dict(ps=0, pst=0, wp=0, nwp=4)

    def nps():
        i = cnt["ps"] % 6
        cnt["ps"] += 1
        return i

    def npst():
        i = cnt["pst"] % 2
        cnt["pst"] += 1
        return i

    def nwp():
        i = cnt["wp"] % cnt["nwp"]
        cnt["wp"] += 1
        return i

    def ts_ap(e, out, in0, sc, op0):
        return e.tensor_scalar(out=out, in0=in0, scalar1=sc, scalar2=None, op0=op0)

    S.dma("sp", lambda e: e.dma_start(out=ct[:], in_=coltab[:, :]), "ct", w=["ct"])
    S.dma("sp", lambda e: e.dma_start(out=cst[:], in_=consts[:, :]), "cst", w=["cst"])
    S.dma("pool", lambda e: e.dma_start(out=ident[:], in_=ident_in[:, :]), "ident", w=["ident"])
    S.dma("pool", lambda e: e.dma_start(out=wsT[:], in_=wst[:, :]), "wsT", w=["wsT"])
    S.dma("pool", lambda e: e.dma_start(out=bsr[:], in_=bsrow[:, :]), "bsr", w=["bsr"])
    S.op("dve", lambda e: e.memset(ones[:], 1.0), w=["ones"])
    S.op("dve", lambda e: e.memset(ones33[:], 1.0), w=["ones33"])
    for (bp, bi, ridx) in ((0, 0, RT_BK), (0, 1, RT_BV), (32, 0, RT_BGR), (32, 1, RT_BVA)):
        S.dma("pool", lambda e, bp=bp, bi=bi, ridx=ridx: e.dma_start(out=brow[bp:bp + 1, bi, :],
                                                                      in_=rowtab[ridx][0:1, :]),
              "brow%d_%d" % (bp, bi), w=[("brow", bp, bi)])
    S.op("dve", lambda e: e.memset(Sst[:], 0.0), w=[("Sst", h) for h in range(4)])
    S.op("dve", lambda e: e.memset(Tst[:], 0.0), w=[("Tst", h) for h in range(4)])
    S.op("dve", lambda e: e.memset(Stb[:], 0.0), w=["Stb"])
    S.op("dve", lambda e: e.tensor_scalar(out=lg[:], in0=ct[:, CT_LOGIT:CT_LOGIT + 8], scalar1=-1.0, scalar2=0.0,
                                          op0=ALU.mult, op1=ALU.add), r=["ct"], w=["lg"])
    S.op("act", lambda e: e.activation(out=lg[:], in_=lg[:], func=AF.Exp), r=["lg"], w=["lg"])
    S.op("dve", lambda e: e.tensor_scalar(out=lg[:], in0=lg[:], scalar1=1.0, scalar2=0.0, op0=ALU.add, op1=ALU.add),
         r=["lg"], w=["lg"])
    S.op("act", lambda e: e.activation(out=lg[:], in_=lg[:], func=AF.Ln), r=["lg"], w=["lg"])
    S.op("dve", lambda e: e.tensor_scalar(out=lg[:], in0=lg[:], scalar1=-1.0, scalar2=0.0, op0=ALU.mult, op1=ALU.add),
         r=["lg"], w=["lg"])
    S.op("dve", lambda e: ts_ap(e, dec[:, 0:4], lg[:, 0:4], cst[:, 1:2], ALU.mult), r=["lg", "cst"], w=["dec"])
    S.op("dve", lambda e: e.tensor_scalar(out=mtmp[:, 0, 0:1], in0=cst[:, 0:1], scalar1=-1.0, scalar2=127.0,
                                          op0=ALU.mult, op1=ALU.add), r=["cst"], w=["mtmp"])
    S.op("dve", lambda e: ts_ap(e, dec[:, 4:8], lg[:, 0:4], mtmp[:, 0, 0:1], ALU.mult), r=["lg", "mtmp"], w=["dec"])
    S.op("dve", lambda e: e.tensor_scalar(out=dec[:, 8:12], in0=lg[:, 0:4], scalar1=128.0, scalar2=0.0,
                                          op0=ALU.mult, op1=ALU.add), r=["lg"], w=["dec"])
    S.op("dve", lambda e: e.tensor_scalar(out=mtmp[:, 0, 1:2], in0=cst[:, 0:1], scalar1=-1.0, scalar2=128.0,
                                          op0=ALU.mult, op1=ALU.add), r=["cst", "mtmp"], w=["mtmp"])
    S.op("dve", lambda e: ts_ap(e, dec[:, 12:16], lg[:, 4:8], mtmp[:, 0, 1:2], ALU.mult), r=["lg", "mtmp"], w=["dec"])
    S.op("dve", lambda e: ts_ap(e, dec[:, 16:20], lg[:, 4:8], cst[:, 0:1], ALU.mult), r=["lg", "cst"], w=["dec"])
    S.op("dve", lambda e: e.tensor_scalar(out=dec[:, 20:24], in0=lg[:, 4:8], scalar1=128.0, scalar2=0.0,
                                          op0=ALU.mult, op1=ALU.add), r=["lg"], w=["dec"])
    S.op("act", lambda e: e.activation(out=dec[:], in_=dec[:], func=AF.Exp), r=["dec"], w=["dec"])
    S.op("dve", lambda e: e.tensor_scalar(out=dec[:, 4:8], in0=dec[:, 4:8], scalar1=0.0625, scalar2=0.0,
                                          op0=ALU.mult, op1=ALU.add), r=["dec"], w=["dec"])
    S.op("dve", lambda e: e.tensor_scalar(out=dec[:, 16:20], in0=dec[:, 16:20], scalar1=0.0625, scalar2=0.0,
                                          op0=ALU.mult, op1=ALU.add), r=["dec"], w=["dec"])
    QD1, KD1, G1, QD2, KD2, G2 = 0, 4, 8, 12, 16, 20
    S.op("dve", lambda e: e.tensor_scalar(out=mtmp[:, 0, :], in0=cst[:, 2:130], scalar1=0.0, scalar2=0.0,
                                          op0=ALU.max, op1=ALU.add), r=["cst", "mtmp"], w=["mtmp"])
    S.op("dve", lambda e: e.tensor_scalar(out=mtmp[:, 1, :], in0=cst[:, 2:130], scalar1=-1.0, scalar2=0.0,
                                          op0=ALU.mult, op1=ALU.max), r=["cst", "mtmp"], w=["mtmp"])
    for h in range(4):
        S.op("dve", lambda e, h=h: ts_ap(e, MT[:, h, :], mtmp[:, 0, :], lg[:, h:h + 1], ALU.mult), r=["mtmp", "lg"], w=["MT"])
        S.op("dve", lambda e, h=h: e.scalar_tensor_tensor(out=MT[:, h, :], in0=mtmp[:, 1, :],
                                                          scalar=lg[:, 4 + h:5 + h], in1=MT[:, h, :],
                                                          op0=ALU.mult, op1=ALU.add), r=["mtmp", "lg", "MT"], w=["MT"])
    S.op("act", lambda e: e.activation(out=MT[:], in_=MT[:], func=AF.Exp), r=["MT"], w=["MT"])
    S.op("dve", lambda e: e.tensor_scalar(out=MT[:], in0=MT[:], scalar1=0.0625, scalar2=0.0, op0=ALU.mult, op1=ALU.add),
         r=["MT"], w=["MT"])

    if debug:
        S.dma("sp", lambda e: e.dma_start(out=dbg_ct[:, :], in_=ct[:]), "dbg1", r=["ct"], w=["dbg1"])
        S.dma("sp", lambda e: e.dma_start(out=dbg_lg[:, :], in_=lg[:]), "dbg2", r=["lg"], w=["dbg2"])
        S.dma("sp", lambda e: e.dma_start(out=dbg_dec[:, :], in_=dec[:]), "dbg3", r=["dec"], w=["dbg3"])
        S.dma("sp", lambda e: e.dma_start(out=dbg_MT[:, :], in_=MT[:].rearrange("p a b -> p (a b)")), "dbg4",
              r=["MT"], w=["dbg4"])
    def rt_load(slot, idx):
        S.dma("sp", lambda e: e.dma_start(out=rt[:, slot, :], in_=rowtab[idx]), "rt%d" % slot, w=[("rt", slot)])

    arena = sb("arena", [128, 21504])
    dummy = sb("bdummy", [128, 8])

    def carve(off, shape, dt=F32):
        nb = int(np.prod(shape)) * (4 if dt == F32 else 2)
        ap = arena[:, off // 4:(off + nb) // 4]
        if dt == BF16:
            ap = ap.bitcast(BF16)
        if len(shape) == 2:
            ap = ap.rearrange("p (a b) -> p a b", a=shape[0])
        return ap

    FFN_KEYS = ["hT", ("wp", 4), ("wp", 5)] + [("tT", i) for i in range(NFT)] + \
        [("wd", h, b) for h in range(2) for b in range(NFT // 2)]
    CTMP_KEYS = ["qtmp", "fA", "fB", "csfm", "rp", "ri", ("PT", 0), ("PT", 1)]
    E1_KEYS = ["sga", "sgb", "mixT"]
    MIX_KEYS = ["qT", "kT", "rpg", "rgT", "rr"] + [("sgr", i) for i in range(4)] + CTMP_KEYS

    def barrier(rk, wk):
        S.op("dve", lambda e: e.memset(dummy[:], 0.0), w=list(rk) + list(wk))

    def load_w(W, col0, ncols=256, row_tiles=8):
        i = nwp()
        S.dma("pool", lambda e: e.dma_start(
            out=wp[i][:, 0:row_tiles, 0:ncols],
            in_=W[0:row_tiles * 128, col0:col0 + ncols].rearrange("(kt p) c -> p kt c", p=128)),
            "wp%d" % i, w=[("wp", i)])
        return i

    def norm_group(rtidx, dstT, dstkey):
        for c in range(4):
            S.op("act", lambda e, c=c: e.activation(out=junk[:], in_=xt[:, c, :], func=AF.Square,
                                                    accum_out=ss[:, c:c + 1]), r=[("xt", c)], w=[("ss", c)])
        allss = [("ss", c) for c in range(4)]
        allr = [("rstd", c) for c in range(4)]
        S.op("dve", lambda e: e.tensor_scalar(out=rstd[:, 0:4], in0=ss[:, 0:4], scalar1=1.0 / D, scalar2=EPS,
                                              op0=ALU.mult, op1=ALU.add), r=allss, w=allr)
        S.op("act", lambda e: e.activation(out=rstd[:, 0:4], in_=rstd[:, 0:4], func=AF.Sqrt), r=allr, w=allr)
        S.op("dve", lambda e: e.reciprocal(out=rstd[:, 0:4], in_=rstd[:, 0:4]), r=allr, w=allr)
        for c in range(4):
            hbuf, hkey = (hb, "hb") if c % 2 == 0 else (hb2, "hb2")
            S.op("dve", lambda e, c=c, hbuf=hbuf: e.scalar_tensor_tensor(
                out=hbuf[:], in0=xt[:, c, :], scalar=rstd[:, c:c + 1], in1=rt[:, rtidx, :], op0=ALU.mult,
                op1=ALU.mult), r=[("xt", c), ("rstd", c), ("rt", rtidx)], w=[hkey])
            transpose_into(hbuf, dstT, c, hkey, dstkey)

    def transpose_into(srcb, dstT, c, srckey, dstkey):
        p = npst()
        for kt in range(8):
            S.op("pe", lambda e, kt=kt: e.transpose(out=pst[p][:, kt * 128:(kt + 1) * 128],
                                                    in_=srcb[:, kt * 128:(kt + 1) * 128], identity=ident[:]),
                 r=[srckey, "ident"], w=[("pst", p)])
        S.op("act", lambda e: e.copy(out=dstT[:, :, c * 128:(c + 1) * 128],
                                     in_=pst[p][:].rearrange("p (k t) -> p k t", k=8)),
             r=[("pst", p)], w=[dstkey])

    hT = carve(0, [8, G], BF16)
    wp.append(carve(75776, [8, 256], BF16))
    wp.append(carve(79872, [8, 256], BF16))
    tT = carve(8192, [NFT, G], BF16)
    wd = [carve(30720 + i * 22528, [NFT, 512], BF16) for i in range(2)]
    sg = [sb("sg%d" % i, [128, 512]) for i in range(2)]
    gtmp = sb("gtmp", [128, 256])

    def ffn(rtidx, Wg, Wu, Wd, load_wd=True):
        cnt["nwp"] = 6
        norm_group(rtidx, hT, "hT")
        for blk in range(NFT // 2):
            ig = load_w(Wg, blk * 256)
            iu = load_w(Wu, blk * 256)
            for j in range(2):
                ft = blk * 2 + j
                pg, pu = nps(), nps()
                for kt in range(8):
                    S.op("pe", lambda e, kt=kt, pg=pg, ig=ig, j=j: e.matmul(
                        ps[pg][:], lhsT=wp[ig][:, kt, j * 128:(j + 1) * 128], rhs=hT[:, kt, :],
                        start=(kt == 0), stop=(kt == 7)), r=[("wp", ig), "hT"], w=[("ps", pg)])
                for kt in range(8):
                    S.op("pe", lambda e, kt=kt, pu=pu, iu=iu, j=j: e.matmul(
                        ps[pu][:], lhsT=wp[iu][:, kt, j * 128:(j + 1) * 128], rhs=hT[:, kt, :],
                        start=(kt == 0), stop=(kt == 7)), r=[("wp", iu), "hT"], w=[("ps", pu)])
                si = ft % 2
                if j == 0 and load_wd:
                    for half in range(2):
                        S.dma("pool", lambda e, half=half, blk=blk: e.dma_start(
                            out=wd[half][:, 2 * blk:2 * blk + 2, :],
                            in_=Wd[blk * 256:(blk + 1) * 256, half * 512:(half + 1) * 512].rearrange(
                                "(ft p) c -> p ft c", p=128)),
                            "wd%d_%d" % (half, blk), w=[("wd", half, blk)])
                S.op("act", lambda e, pg=pg, si=si: e.activation(out=sg[si][:], in_=ps[pg][:], func=AF.Silu),
                     r=[("ps", pg)], w=[("sg", si)])
                S.op("dve", lambda e, pu=pu, si=si, ft=ft: e.tensor_tensor(out=tT[:, ft, :], in0=sg[si][:],
                                                                           in1=ps[pu][:], op=ALU.mult),
                     r=[("ps", pu), ("sg", si)], w=[("tT", ft)])
        for half in range(2):
            for tt in range(4):
                p = nps()
                for ft in range(NFT):
                    S.op("pe", lambda e, ft=ft, p=p, tt=tt, half=half: e.matmul(
                        ps[p][:], lhsT=tT[:, ft, tt * 128:(tt + 1) * 128], rhs=wd[half][:, ft, :],
                        start=(ft == 0), stop=(ft == NFT - 1)), r=[("tT", ft), ("wd", half, ft // 2)], w=[("ps", p)])
                S.op("dve", lambda e, p=p, tt=tt, half=half: e.scalar_tensor_tensor(
                    out=xt[:, tt, half * 512:(half + 1) * 512], in0=ps[p][:], scalar=0.5,
                    in1=xt[:, tt, half * 512:(half + 1) * 512], op0=ALU.mult, op1=ALU.add),
                    r=[("ps", p), ("xt", tt)], w=[("xt", tt)])
        cnt["nwp"] = 4

    def load_xt(src, g, srckey=None):
        for c in range(4):
            S.dma("sp", lambda e, c=c: e.dma_start(out=xt[:, c, :], in_=src[g * G + c * 128:g * G + (c + 1) * 128, :]),
                  "xt%d" % c, r=([(srckey, g, c)] if srckey else []), w=[("xt", c)])

    def store_xt(dst, g, dstkey):
        for c in range(4):
            S.dma("sp", lambda e, c=c: e.dma_start(out=dst[g * G + c * 128:g * G + (c + 1) * 128, :], in_=xt[:, c, :]),
                  "xt_st%d" % c, r=[("xt", c)], w=[(dstkey, g, c)])

    def rows(ap, g):
        return ap[g * G:(g + 1) * G, :].rearrange("(c p) d -> p c d", p=128)

    def own_rows(ap, g):
        return rows(ap, g)

    rt_load(SL_FFN1, RT_FFN1)
    rt_load(SL_MIX, RT_MIX)
    for g in [7, 6, 5, 4, 0, 1, 2, 3]:
        load_xt(xall, g)
        ffn(SL_FFN1, w1g, w1u, w1d, load_wd=(g == 7))
        if g < NG_OWN:
            store_xt(x1s, g, "x1s")
        norm_group(SL_MIX, h2T, "h2T")
        S.dma("sp", lambda e, g=g: e.dma_start(out=h2Ts[g], in_=h2T[:].rearrange("p k t -> p (k t)")), "h2T_st",
              r=["h2T"], w=[("h2Ts", g)])

    if stop_after == "A":
        return finish(nc, S, es, y, None)

    ktm = sb("ktm", [128, 4, D], BF16)
    vtm = sb("vtm", [128, 4, D], BF16)
    Vd = sb("Vd", [128, D], BF16)
    cstm = sb("cstm", [128, 2, 4, 128])

    def load_h2T(g):
        S.dma("sp", lambda e: e.dma_start(out=h2T[:].rearrange("p k t -> p (k t)"), in_=h2Ts[g]), "h2T_ld",
              r=[("h2Ts", g)], w=["h2T"])

    def proj_tm(off, bp, bi, consume):
        for cbp in range(2):
            iws = [load_w(win, off + (2 * cbp + hf) * 256) for hf in range(2)]
            for tt in range(4):
                p = nps()
                for hf in range(2):
                    iw = iws[hf]
                    cb = 2 * cbp + hf
                    for kt in range(8):
                        S.op("pe", lambda e, kt=kt, p=p, tt=tt, iw=iw, hf=hf: e.matmul(
                            ps[p][:, hf * 256:(hf + 1) * 256], lhsT=h2T[:, kt, tt * 128:(tt + 1) * 128],
                            rhs=wp[iw][:, kt, :], start=(kt == 0), stop=False), r=["h2T", ("wp", iw)], w=[("ps", p)])
                    S.op("pe", lambda e, p=p, cb=cb, hf=hf: e.matmul(
                        ps[p][:, hf * 256:(hf + 1) * 256], lhsT=ones33[bp:bp + 1, :],
                        rhs=brow[bp:bp + 1, bi, cb * 256:(cb + 1) * 256], start=False, stop=True),
                        r=["ones33", ("brow", bp, bi)], w=[("ps", p)])
                consume(p, cbp, tt)

    rA4 = sg[0][:].rearrange("p (a b) -> p a b", a=4)
    rB4 = sg[1][:].rearrange("p (a b) -> p a b", a=4)

    def kv_tm(g):
        for s2 in range(2):
            S.dma("sp", lambda e, s2=s2: e.dma_start(
                out=cstm[:, s2, :, :], in_=cs_tm[s2, g * G:(g + 1) * G, :].rearrange("(c p) f -> p c f", p=128)),
                "cstm%d" % s2, w=[("cstm", s2)])

        def k_consume(p, cbp, tt):
            pv = ps[p][:].rearrange("p (a b) -> p a b", a=4)
            S.op("dve", lambda e: e.tensor_tensor(
                out=rA4, in0=pv, in1=cstm[:, 0, tt, :].unsqueeze(1).to_broadcast([128, 4, 128]),
                op=ALU.mult), r=[("ps", p), ("cstm", 0)], w=[("sg", 0)])
            S.op("dve", lambda e: e.tensor_tensor(
                out=rB4, in0=pv, in1=cstm[:, 1, tt, :].unsqueeze(1).to_broadcast([128, 4, 128]),
                op=ALU.mult), r=[("ps", p), ("cstm", 1)], w=[("sg", 1)])
            kv4 = ktm[:, tt, cbp * 512:(cbp + 1) * 512].rearrange("p (c t f) -> p c t f", c=2, t=2)
            a4 = rA4.rearrange("p (c t) f -> p c t f", c=2)
            b4 = rB4.rearrange("p (c t) f -> p c t f", c=2)
            S.op("dve", lambda e: e.tensor_tensor(out=kv4[:, :, 0, :], in0=a4[:, :, 0, :], in1=b4[:, :, 1, :],
                                                  op=ALU.subtract), r=[("sg", 0), ("sg", 1)], w=[("ktm", tt)])
            S.op("dve", lambda e: e.tensor_tensor(out=kv4[:, :, 1, :], in0=a4[:, :, 1, :], in1=b4[:, :, 0, :],
                                                  op=ALU.add), r=[("sg", 0), ("sg", 1)], w=[("ktm", tt)])

        def v_consume(p, cbp, tt):
            S.op("act", lambda e: e.copy(out=vtm[:, tt, cbp * 512:(cbp + 1) * 512], in_=ps[p][:]),
                 r=[("ps", p)], w=[("vtm", tt)])

        proj_tm(OFF_K, 0, 0, k_consume)
        proj_tm(OFF_V, 0, 1, v_consume)

    def state_mm(c, kd):
        for h in range(4):
            S.op("act", lambda e, h=h: e.activation(out=Vd[:, h * 256:(h + 1) * 256], in_=vtm[:, c, h * 256:(h + 1) * 256],
                                                    func=AF.Copy, scale=dec[:, kd + h:kd + h + 1]),
                 r=[("vtm", c), "dec"], w=["Vd"])
        pids = []
        for h in range(4):
            p = nps()
            pids.append(p)
            for dt in range(2):
                S.op("pe", lambda e, h=h, dt=dt, p=p: e.matmul(
                    ps[p][:, dt * 256:(dt + 1) * 256], lhsT=ktm[:, c, h * 256 + dt * 128:h * 256 + (dt + 1) * 128],
                    rhs=Vd[:, h * 256:(h + 1) * 256], start=True, stop=True),
                    r=[("ktm", c), "Vd"], w=[("ps", p)])
        return pids

    def state_acc(Sin, inkey, Sout, outkey, gd, pids):
        for h in range(4):
            p = pids[h]
            S.op("dve", lambda e, h=h, p=p: e.scalar_tensor_tensor(
                out=Sout[:, 2 * h:2 * h + 2, :], in0=Sin[:, 2 * h:2 * h + 2, :], scalar=dec[:, gd + h:gd + h + 1],
                in1=ps[p][:].rearrange("p (a b) -> p a b", a=2), op0=ALU.mult, op1=ALU.add),
                r=[(inkey, h), ("ps", p), "dec"], w=[(outkey, h)])

    def state_update(St, stkey, c, kd, gd):
        state_acc(St, stkey, St, stkey, gd, state_mm(c, kd))

    for g in [7, 6, 5, 4]:
        load_h2T(g)
        kv_tm(g)
        for c in [3, 2, 1, 0]:
            state_update(Tst, "Tst", c, KD2, G2)

    if debug:
        S.dma("sp", lambda e: e.dma_start(out=dbg_state[:, :], in_=Tst[:].rearrange("p a b -> p (a b)")), "dbg",
              r=[("Tst", h) for h in range(4)], w=["dbgs"])
    if stop_after == "B":
        return finish(nc, S, es, y, None)

    barrier(FFN_KEYS, MIX_KEYS)
    qT = carve(0, [8, G], BF16)
    kT = carve(8192, [8, G], BF16)
    rpg = carve(16384, [4, D])
    sgr = carve(32768, [4, D])
    rgT = carve(49152, [8, G], BF16)
    rr = carve(57344, [1, D])[:, 0, :]
    qtmp = carve(61440, [2, G])
    fA = carve(65536, [2, G])
    fB = carve(69632, [2, G])
    csfm = carve(73728, [2, G])
    rp = carve(77824, [1, D])[:, 0, :]
    ri = carve(81920, [1, 256])[:, 0, :]
    PT = [carve(82944 + 256 * i, [1, 128], BF16)[:, 0, :] for i in range(2)]
    sga = carve(61440, [8, G], BF16)
    sgb = carve(69632, [8, G], BF16)
    mixT = carve(77824, [8, G], BF16)

    def proj_fm_rot(off, ctoff, dst, dstkey):
        for blk in range(4):
            iw = load_w(win, off + blk * 256)
            for j in range(2):
                jt = 2 * blk + j
                p = nps()
                for kt in range(8):
                    S.op("pe", lambda e, kt=kt, p=p, j=j, iw=iw: e.matmul(
                        ps[p][:], lhsT=wp[iw][:, kt, j * 128:(j + 1) * 128], rhs=h2T[:, kt, :],
                        start=(kt == 0), stop=(kt == 7)), r=["h2T", ("wp", iw)], w=[("ps", p)])
                S.op("dve", lambda e, p=p, j=j, jt=jt: ts_ap(e, qtmp[:, j, :], ps[p][:], ct[:, ctoff + jt:ctoff + jt + 1], ALU.add), r=[("ps", p), "ct"], w=["qtmp"])
            S.op("dve", lambda e: e.tensor_tensor(
                out=fA[:], in0=qtmp[:], in1=csfm[:, 0, :].unsqueeze(1).to_broadcast([128, 2, G]), op=ALU.mult),
                r=["qtmp", "csfm"], w=["fA"])
            S.op("dve", lambda e: e.tensor_tensor(
                out=fB[:], in0=qtmp[:], in1=csfm[:, 1, :].unsqueeze(1).to_broadcast([128, 2, G]), op=ALU.mult),
                r=["qtmp", "csfm"], w=["fB"])
            S.op("dve", lambda e, blk=blk: e.tensor_tensor(out=dst[:, 2 * blk, :], in0=fA[:, 0, :], in1=fB[:, 1, :],
                                                           op=ALU.subtract), r=["fA", "fB"], w=[dstkey])
            S.op("dve", lambda e, blk=blk: e.tensor_tensor(out=dst[:, 2 * blk + 1, :], in0=fA[:, 1, :],
                                                           in1=fB[:, 0, :], op=ALU.add), r=["fA", "fB"], w=[dstkey])

    for g in range(NG_OWN):
        load_h2T(g)
        kv_tm(g)
        S.dma("sp", lambda e, g=g: e.dma_start(out=rows(ktms, g), in_=ktm[:]), "ktm_st",
              r=[("ktm", i) for i in range(4)], w=[("ktms", g)])
        S.dma("sp", lambda e, g=g: e.dma_start(out=rows(vtms, g), in_=vtm[:]), "vtm_st",
              r=[("vtm", i) for i in range(4)], w=[("vtms", g)])
        S.dma("sp", lambda e, g=g: e.dma_start(
            out=csfm[:], in_=cs_fm[:, :, g * G:(g + 1) * G].rearrange("s p t -> p s t")), "csfm", w=["csfm"])
        proj_fm_rot(OFF_Q, CT_BQ, qT, "qT")
        proj_fm_rot(OFF_K, CT_BK, kT, "kT")
        if debug and g == 0:
            S.dma("sp", lambda e: e.dma_start(out=dbg_kT[:, :], in_=kT[:].rearrange("p k t -> p (k t)")), "dbg5",
                  r=["kT"], w=["dbg5"])
        S.dma("sp", lambda e, g=g: e.dma_start(out=qTs[g], in_=qT[:].rearrange("p k t -> p (k t)")), "qT_st",
              r=["qT"], w=[("qTs", g)])
        for c in range(4):
            cs = slice(c * 128, (c + 1) * 128)
            state_update(Sst, "Sst", c, KD1, G1)
            for h in range(4):
                p1 = nps()
                for dt in range(2):
                    S.op("pe", lambda e, h=h, dt=dt, p1=p1, cs=cs: e.matmul(
                        ps[p1][:, 0:128], lhsT=kT[:, 2 * h + dt, cs], rhs=qT[:, 2 * h + dt, cs],
                        start=(dt == 0), stop=(dt == 1)), r=["kT", "qT"], w=[("ps", p1)])
                pi = h % 2
                S.op("dve", lambda e, h=h, p1=p1, pi=pi: e.tensor_tensor(out=PT[pi][:], in0=ps[p1][:, 0:128],
                                                                         in1=MT[:, h, :], op=ALU.mult),
                     r=[("ps", p1), "MT"], w=[("PT", pi)])
                p2 = nps()
                S.op("pe", lambda e, h=h, p2=p2, pi=pi, c=c: e.matmul(
                    ps[p2][:, 0:256], lhsT=PT[pi][:], rhs=vtm[:, c, h * 256:(h + 1) * 256], start=True, stop=True),
                    r=[("PT", pi), ("vtm", c)], w=[("ps", p2)])
                for dt in range(2):
                    S.op("pe", lambda e, h=h, dt=dt, p2=p2, cs=cs: e.matmul(
                        ps[p2][:, 256:512], lhsT=qT[:, 2 * h + dt, cs], rhs=Stb[:, 2 * h + dt, :],
                        start=(dt == 0), stop=(dt == 1)), r=["qT", "Stb"], w=[("ps", p2)])
                S.op("act", lambda e, p2=p2: e.copy(out=ri[:], in_=ps[p2][:, 0:256]), r=[("ps", p2)], w=["ri"])
                S.op("dve", lambda e, h=h, p2=p2: e.scalar_tensor_tensor(
                    out=rp[:, h * 256:(h + 1) * 256], in0=ps[p2][:, 256:512], scalar=dec[:, QD1 + h:QD1 + h + 1],
                    in1=ri[:], op0=ALU.mult, op1=ALU.add), r=[("ps", p2), "ri", "dec"], w=["rp"])
            S.dma("sp", lambda e, g=g, c=c: e.dma_start(out=rps[g * G + c * 128:g * G + (c + 1) * 128, :], in_=rp[:]),
                  "rp_st", r=["rp"], w=[("rps", g, c)])
            S.op("act", lambda e: e.copy(out=Stb[:], in_=Sst[:]), r=[("Sst", h) for h in range(4)], w=["Stb"])

    if stop_after == "C":
        return finish(nc, S, es, y, None)

    Tbufs = [(Tst, "Tst"), (Sst, "Sst")]
    for g in [3, 2, 1, 0]:
        load_h2T(g)
        S.dma("sp", lambda e, g=g: e.dma_start(out=qT[:].rearrange("p k t -> p (k t)"), in_=qTs[g]), "qT_ld",
              r=[("qTs", g)], w=["qT"])
        S.dma("sp", lambda e, g=g: e.dma_start(out=ktm[:], in_=rows(ktms, g)), "ktm_ld", r=[("ktms", g)],
              w=[("ktm", i) for i in range(4)])
        S.dma("sp", lambda e, g=g: e.dma_start(out=vtm[:], in_=rows(vtms, g)), "vtm_ld", r=[("vtms", g)],
              w=[("vtm", i) for i in range(4)])
        S.dma("sp", lambda e, g=g: e.dma_start(out=rpg[:], in_=rows(rps, g)), "rpg_ld",
              r=[("rps", g, c) for c in range(4)], w=["rpg"] + [("rpgc", c, h) for c in range(4) for h in range(4)])
        def gr_piece(cbp):
            iws = [load_w(win, OFF_GR + (2 * cbp + hf) * 256) for hf in range(2)]
            for tt in range(4):
                p = nps()
                for hf in range(2):
                    iw = iws[hf]
                    cb = 2 * cbp + hf
                    for kt in range(8):
                        S.op("pe", lambda e, kt=kt, p=p, tt=tt, iw=iw, hf=hf: e.matmul(
                            ps[p][:, hf * 256:(hf + 1) * 256], lhsT=h2T[:, kt, tt * 128:(tt + 1) * 128],
                            rhs=wp[iw][:, kt, :], start=(kt == 0), stop=False), r=["h2T", ("wp", iw)], w=[("ps", p)])
                    S.op("pe", lambda e, p=p, cb=cb, hf=hf: e.matmul(
                        ps[p][:, hf * 256:(hf + 1) * 256], lhsT=ones33[32:33, :],
                        rhs=brow[32:33, 0, cb * 256:(cb + 1) * 256], start=False, stop=True),
                        r=["ones33", ("brow", 32, 0)], w=[("ps", p)])
                S.op("act", lambda e, cbp=cbp, tt=tt, p=p: e.activation(
                    out=sgr[:, tt, cbp * 512:(cbp + 1) * 512], in_=ps[p][:], func=AF.Silu),
                    r=[("ps", p)], w=[("sgr", tt)])

        for c in [3, 2, 1, 0]:
            cs = slice(c * 128, (c + 1) * 128)
            cur, curk = Tbufs[0]
            nxt, nxtk = Tbufs[1]
            pids = state_mm(c, KD2)
            S.op("act", lambda e, cur=cur: e.copy(out=Stb[:], in_=cur[:]), r=[(curk, h) for h in range(4)], w=["Stb"])
            state_acc(cur, curk, nxt, nxtk, G2, pids)
            Tbufs.reverse()
            if c >= 2:
                gr_piece(3 - c)
            for h in range(4):
                p = nps()
                for dt in range(2):
                    S.op("pe", lambda e, h=h, dt=dt, p=p, cs=cs: e.matmul(
                        ps[p][:, 0:256], lhsT=qT[:, 2 * h + dt, cs], rhs=Stb[:, 2 * h + dt, :],
                        start=(dt == 0), stop=(dt == 1)), r=["qT", "Stb"], w=[("ps", p)])
                S.op("dve", lambda e, h=h, p=p, c=c: e.scalar_tensor_tensor(
                    out=rpg[:, c, h * 256:(h + 1) * 256], in0=ps[p][:, 0:256], scalar=dec[:, QD2 + h:QD2 + h + 1],
                    in1=rpg[:, c, h * 256:(h + 1) * 256], op0=ALU.mult, op1=ALU.add),
                    r=[("ps", p), ("rpgc", c, h), "rpg", "dec"], w=[("rpgc", c, h)])
        for c in range(4):
            for h in range(4):
                S.op("act", lambda e, h=h, c=c: e.activation(
                    out=junk[:, 0:256], in_=rpg[:, c, h * 256:(h + 1) * 256], func=AF.Square,
                    accum_out=ss[:, 16 + 4 * c + h:17 + 4 * c + h]), r=[("rpgc", c, h), "rpg"], w=[("ssd", c)])
        allss = [("ssd", c) for c in range(4)]
        S.op("dve", lambda e: e.tensor_scalar(out=rstd[:, 16:32], in0=ss[:, 16:32], scalar1=1.0 / 256, scalar2=EPS,
                                              op0=ALU.mult, op1=ALU.add), r=allss, w=["rstdd"])
        S.op("act", lambda e: e.activation(out=rstd[:, 16:32], in_=rstd[:, 16:32], func=AF.Sqrt), r=["rstdd"], w=["rstdd"])
        S.op("dve", lambda e: e.reciprocal(out=rstd[:, 16:32], in_=rstd[:, 16:32]), r=["rstdd"], w=["rstdd"])
        for c in range(4):
            hbuf, hkey = (hb, "hb") if c % 2 == 0 else (hb2, "hb2")
            for h in range(4):
                S.op("dve", lambda e, h=h, c=c, hbuf=hbuf: e.scalar_tensor_tensor(
                    out=hbuf[:, h * 256:(h + 1) * 256], in0=rpg[:, c, h * 256:(h + 1) * 256],
                    scalar=rstd[:, 16 + 4 * c + h:17 + 4 * c + h], in1=sgr[:, c, h * 256:(h + 1) * 256],
                    op0=ALU.mult, op1=ALU.mult), r=[("rpgc", c, h), "rpg", "rstdd", ("sgr", c)], w=[hkey])
            transpose_into(hbuf, rgT, c, hkey, "rgT")
        S.dma("sp", lambda e, g=g: e.dma_start(out=rgTs[g], in_=rgT[:].rearrange("p k t -> p (k t)")), "rgT_st",
              r=["rgT"], w=[("rgTs", g)])

    if stop_after == "D":
        return finish(nc, S, es, y, None)

    vaf = rpg
    vn = ktm
    uT = kT
    aT = qT
    barrier(CTMP_KEYS, E1_KEYS)
    rt_load(SL_BVA, RT_BVA)
    rt_load(SL_SGUG, RT_SGUG)
    rt_load(SL_SGUB, RT_SGUB)
    maT = sgr
    maTv = maT[:].rearrange("p a b -> p (a b)").rearrange("p (k t) -> p k t", k=8)
    for g in range(NG_OWN):
        load_h2T(g)
        S.dma("sp", lambda e, g=g: e.dma_start(out=rgT[:].rearrange("p k t -> p (k t)"), in_=rgTs[g]), "rgT_ld",
              r=[("rgTs", g)], w=["rgT"])
        load_xt(x1s, g, "x1s")
        for cb in range(4):
            iw = load_w(win, OFF_VA + cb * 256)
            for tt in range(4):
                p = nps()
                for kt in range(8):
                    S.op("pe", lambda e, kt=kt, p=p, tt=tt, iw=iw: e.matmul(
                        ps[p][:, 0:256], lhsT=h2T[:, kt, tt * 128:(tt + 1) * 128], rhs=wp[iw][:, kt, :],
                        start=(kt == 0), stop=(kt == 7)), r=["h2T", ("wp", iw)], w=[("ps", p)])
                S.op("dve", lambda e, p=p, cb=cb, tt=tt: e.tensor_tensor(
                    out=vaf[:, tt, cb * 256:(cb + 1) * 256], in0=ps[p][:, 0:256],
                    in1=rt[:, SL_BVA, cb * 256:(cb + 1) * 256], op=ALU.add), r=[("ps", p), ("rt", SL_BVA)], w=["rpg"])
        for tt in range(4):
            S.op("dve", lambda e: e.memset(ss[:, 8:10], 0.0), w=["ss01"])
            S.op("act", lambda e, tt=tt: e.activation(out=vaf[:, tt, :], in_=vaf[:, tt, :], func=AF.Gelu,
                                                      accum_out=ss[:, 8:9]), r=["rpg", "ss01"], w=["rpg", "ss01"])
            S.op("dve", lambda e: e.tensor_scalar(out=ss[:, 10:11], in0=ss[:, 8:9], scalar1=1.0 / D, scalar2=0.0,
                                                  op0=ALU.mult, op1=ALU.add), r=["ss01"], w=["ssm"])
            S.op("dve", lambda e, tt=tt: ts_ap(e, vaf[:, tt, :], vaf[:, tt, :], ss[:, 10:11], ALU.subtract), r=["rpg", "ssm"], w=["rpg"])
            S.op("act", lambda e, tt=tt: e.activation(out=rr[:], in_=vaf[:, tt, :], func=AF.Square,
                                                      accum_out=ss[:, 9:10]), r=["rpg", "ss01"], w=["rr", "ss01"])
            S.op("dve", lambda e: e.tensor_scalar(out=ss[:, 11:12], in0=ss[:, 9:10], scalar1=1.0 / D, scalar2=EPS,
                                                  op0=ALU.mult, op1=ALU.add), r=["ss01"], w=["ssr"])
            S.op("act", lambda e: e.activation(out=ss[:, 11:12], in_=ss[:, 11:12], func=AF.Sqrt), r=["ssr"], w=["ssr"])
            S.op("dve", lambda e: e.reciprocal(out=ss[:, 11:12], in_=ss[:, 11:12]), r=["ssr"], w=["ssr"])
            S.op("dve", lambda e, tt=tt: e.scalar_tensor_tensor(
                out=vaf[:, tt, :], in0=vaf[:, tt, :], scalar=ss[:, 11:12], in1=rt[:, SL_SGUG, :], op0=ALU.mult,
                op1=ALU.mult), r=["rpg", "ssr", ("rt", SL_SGUG)], w=["rpg"])
            S.op("dve", lambda e, tt=tt: e.tensor_tensor(out=vn[:, tt, :], in0=vaf[:, tt, :], in1=rt[:, SL_SGUB, :],
                                                         op=ALU.add), r=["rpg", ("rt", SL_SGUB)], w=[("ktm", tt)])
        for blk in range(4):
            iw = load_w(win, OFF_U + blk * 256)
            for j in range(2):
                jt = 2 * blk + j
                p = nps()
                for kt in range(8):
                    S.op("pe", lambda e, kt=kt, p=p, j=j, iw=iw: e.matmul(
                        ps[p][:], lhsT=wp[iw][:, kt, j * 128:(j + 1) * 128], rhs=h2T[:, kt, :],
                        start=(kt == 0), stop=(kt == 7)), r=["h2T", ("wp", iw)], w=[("ps", p)])
                si = jt % 2
                S.op("dve", lambda e, p=p, jt=jt, si=si: ts_ap(e, sg[si][:], ps[p][:], ct[:, CT_BU + jt:CT_BU + jt + 1], ALU.add), r=[("ps", p), "ct"], w=[("sg", si)])
                S.op("act", lambda e, jt=jt, si=si: e.activation(out=uT[:, jt, :], in_=sg[si][:], func=AF.Gelu),
                     r=[("sg", si)], w=["kT"])
        for c in range(4):
            cs = slice(c * 128, (c + 1) * 128)
            for fh in range(2):
                p = nps()
                for f4 in range(4):
                    ft = fh * 4 + f4
                    gg = ft // 2
                    S.op("pe", lambda e, ft=ft, f4=f4, gg=gg, p=p, c=c: e.matmul(
                        ps[p][:, f4 * 128:(f4 + 1) * 128], lhsT=vn[:, c, ft * 128:(ft + 1) * 128],
                        rhs=wsT[:, gg * 128:(gg + 1) * 128], start=True, stop=False),
                        r=[("ktm", c), "wsT"], w=[("ps", p)])
                    S.op("pe", lambda e, f4=f4, gg=gg, p=p: e.matmul(
                        ps[p][:, f4 * 128:(f4 + 1) * 128], lhsT=ones[0:1, :], rhs=bsr[0:1, gg * 128:(gg + 1) * 128],
                        start=False, stop=True), r=["ones", "bsr"], w=[("ps", p)])
                S.op("dve", lambda e, fh=fh, p=p, cs=cs: e.tensor_tensor(
                    out=aT[:, fh * 4:(fh + 1) * 4, cs], in0=uT[:, fh * 4:(fh + 1) * 4, cs],
                    in1=ps[p][:].rearrange("p (a b) -> p a b", a=4), op=ALU.mult), r=["kT", ("ps", p)], w=["qT"])
        for (off, cto, dst, dk) in ((OFF_GA, CT_BGA, sga, "sga"), (OFF_GB, CT_BGB, sgb, "sgb")):
            for blk in range(4):
                iw = load_w(win, off + blk * 256)
                for j in range(2):
                    jt = 2 * blk + j
                    p = nps()
                    for kt in range(8):
                        S.op("pe", lambda e, kt=kt, p=p, j=j, iw=iw: e.matmul(
                            ps[p][:], lhsT=wp[iw][:, kt, j * 128:(j + 1) * 128], rhs=h2T[:, kt, :],
                            start=(kt == 0), stop=(kt == 7)), r=["h2T", ("wp", iw)], w=[("ps", p)])
                    si = jt % 2
                    S.op("dve", lambda e, p=p, jt=jt, si=si, cto=cto: ts_ap(e, sg[si][:], ps[p][:], ct[:, cto + jt:cto + jt + 1], ALU.add), r=[("ps", p), "ct"], w=[("sg", si)])
                    S.op("act", lambda e, jt=jt, dst=dst, si=si: e.activation(
                        out=dst[:, jt, :], in_=sg[si][:], func=AF.Sigmoid), r=[("sg", si)], w=[dk])
        for blk in range(4):
            iw = load_w(wa, blk * 256)
            for j in range(2):
                jt = 2 * blk + j
                p = nps()
                for kt in range(8):
                    S.op("pe", lambda e, kt=kt, p=p, j=j, iw=iw: e.matmul(
                        ps[p][:], lhsT=wp[iw][:, kt, j * 128:(j + 1) * 128], rhs=aT[:, kt, :],
                        start=(kt == 0), stop=(kt == 7)), r=["qT", ("wp", iw)], w=[("ps", p)])
                S.op("dve", lambda e, p=p, jt=jt: e.tensor_tensor(out=maTv[:, jt, :], in0=ps[p][:], in1=sga[:, jt, :],
                                                                  op=ALU.mult), r=[("ps", p), "sga"],
                     w=[("sgr", i) for i in range(4)])
        for blk in range(4):
            iw = load_w(wb, blk * 256)
            for j in range(2):
                jt = 2 * blk + j
                p = nps()
                for kt in range(8):
                    S.op("pe", lambda e, kt=kt, p=p, j=j, iw=iw: e.matmul(
                        ps[p][:], lhsT=wp[iw][:, kt, j * 128:(j + 1) * 128], rhs=rgT[:, kt, :],
                        start=(kt == 0), stop=(kt == 7)), r=["rgT", ("wp", iw)], w=[("ps", p)])
                si = jt % 2
                S.op("dve", lambda e, p=p, jt=jt, si=si: e.tensor_tensor(out=sg[si][:], in0=ps[p][:],
                                                                         in1=sgb[:, jt, :], op=ALU.mult),
                     r=[("ps", p), "sgb"], w=[("sg", si)])
                S.op("dve", lambda e, jt=jt, si=si: e.tensor_tensor(out=mixT[:, jt, :], in0=sg[si][:],
                                                                    in1=maTv[:, jt, :], op=ALU.add),
                     r=[("sg", si)] + [("sgr", i) for i in range(4)], w=["mixT"])
        for cb in range(4):
            iw = load_w(wo, cb * 256)
            for tt in range(4):
                p = nps()
                for kt in range(8):
                    S.op("pe", lambda e, kt=kt, p=p, tt=tt, iw=iw: e.matmul(
                        ps[p][:, 0:256], lhsT=mixT[:, kt, tt * 128:(tt + 1) * 128], rhs=wp[iw][:, kt, :],
                        start=(kt == 0), stop=(kt == 7)), r=["mixT", ("wp", iw)], w=[("ps", p)])
                S.op("dve", lambda e, p=p, cb=cb, tt=tt: e.tensor_tensor(
                    out=xt[:, tt, cb * 256:(cb + 1) * 256], in0=ps[p][:, 0:256],
                    in1=xt[:, tt, cb * 256:(cb + 1) * 256], op=ALU.add), r=[("ps", p), ("xt", tt)], w=[("xt", tt)])
        store_xt(x2s, g, "x2s")

    if stop_after == "E1":
        return finish(nc, S, es, y, None)

    barrier(MIX_KEYS + E1_KEYS, FFN_KEYS)
    rt_load(SL_FFN2, RT_FFN2)
    rt_load(SL_FINAL, RT_FINAL)
    ykeys = []
    for g in range(NG_OWN):
        load_xt(x2s, g, "x2s")
        ffn(SL_FFN2, w2g, w2u, w2d, load_wd=(g == 0))
        for c in range(4):
            S.op("act", lambda e, c=c: e.activation(out=junk[:], in_=xt[:, c, :], func=AF.Square,
                                                    accum_out=ss[:, c:c + 1]), r=[("xt", c)], w=[("ss", c)])
        allss = [("ss", c) for c in range(4)]
        allr = [("rstd", c) for c in range(4)]
        S.op("dve", lambda e: e.tensor_scalar(out=rstd[:, 0:4], in0=ss[:, 0:4], scalar1=1.0 / D, scalar2=EPS,
                                              op0=ALU.mult, op1=ALU.add), r=allss, w=allr)
        S.op("act", lambda e: e.activation(out=rstd[:, 0:4], in_=rstd[:, 0:4], func=AF.Sqrt), r=allr, w=allr)
        S.op("dve", lambda e: e.reciprocal(out=rstd[:, 0:4], in_=rstd[:, 0:4]), r=allr, w=allr)
        for c in range(4):
            S.op("dve", lambda e, c=c: e.scalar_tensor_tensor(out=xt[:, c, :], in0=xt[:, c, :],
                                                              scalar=rstd[:, c:c + 1], in1=rt[:, SL_FINAL, :],
                                                              op0=ALU.mult, op1=ALU.mult),
                 r=[("xt", c), ("rstd", c), ("rt", SL_FINAL)], w=[("xt", c)])
        store_xt(y, g, "y")
    return finish(nc, S, es, y, ykeys)


def finish(nc, S, es, y, ykeys):
    allw = set()
    for o in S.ops:
        if o["dma"]:
            allw.update(o["w"])
    S.op("sp", None, r=sorted(allw, key=str))
    S.emit()
    es.close()
    return nc


def _host_inputs(inputs):
    f = lambda a: np.ascontiguousarray(np.asarray(a, dtype=np.float32))
    x = f(inputs["x"])
    L = 0
    rep = lambda v: np.ascontiguousarray(np.broadcast_to(f(v).reshape(1, -1), (128, v.size)))
    b_in = f(inputs["b_in"])[L]
    col = lambda v: np.ascontiguousarray(f(v).reshape(8, 128).T)
    rowtab = np.stack([rep(f(inputs["ffn1_norm"])[L]), rep(f(inputs["mix_norm"])[L]),
                       rep(f(inputs["ffn2_norm"])[L]), rep(f(inputs["final_norm"])),
                       rep(f(inputs["sgu_norm_g"])[L]), rep(f(inputs["sgu_norm_b"])[L]),
                       rep(b_in[OFF_K:OFF_K + D]), rep(b_in[OFF_V:OFF_V + D]),
                       rep(b_in[OFF_GR:OFF_GR + D]), rep(b_in[OFF_VA:OFF_VA + D])], axis=0)
    ws = f(inputs["sgu_w_s"])[L]
    bs = f(inputs["sgu_b_s"])[L]
    logit = f(inputs["ret_decay_logit"])[L]
    p = np.arange(128, dtype=np.float32)
    consts = np.concatenate([p[:, None], p[:, None] + 1.0, (p[None, :] - p[:, None])], axis=1).astype(np.float32)
    ident = np.eye(128, dtype=np.float32)
    theta = (10000.0 ** (-np.arange(0, 256, 2, dtype=np.float32) / np.float32(256))).astype(np.float32)
    shared = dict(
        w1g=f(inputs["ffn1_w_gate"])[L], w1u=f(inputs["ffn1_w_up"])[L], w1d=f(inputs["ffn1_w_down"])[L],
        w2g=f(inputs["ffn2_w_gate"])[L], w2u=f(inputs["ffn2_w_up"])[L], w2d=f(inputs["ffn2_w_down"])[L],
        win=f(inputs["w_in"])[L], wa=f(inputs["w_branch_a"])[L], wb=f(inputs["w_branch_b"])[L],
        wo=f(inputs["w_out"])[L], rowtab=np.ascontiguousarray(rowtab), consts=consts, ident=ident)
    maps = []
    for core in range(8):
        b, half = core // 2, core % 2
        xs = x[b] if half == 0 else x[b, ::-1]
        pos = np.arange(SEQ, dtype=np.float32) if half == 0 else np.arange(SEQ - 1, -1, -1, dtype=np.float32)
        ang = (pos[:, None] * theta[None, :]).astype(np.float32)
        cs_tm = np.stack([np.cos(ang), np.sin(ang)]).astype(np.float32)
        cs_fm = np.ascontiguousarray(cs_tm[:, :HALF, :].transpose(0, 2, 1))
        if half == 0:
            ws_l, bs_l, lg_l = ws, bs, logit
        else:
            ws_l, bs_l, lg_l = ws[:, ::-1, ::-1], bs[:, ::-1], logit[::-1]
        wst = np.ascontiguousarray(ws_l.transpose(2, 0, 1).reshape(128, 512))
        coltab = np.concatenate([col(b_in[OFF_Q:OFF_Q + D]), col(b_in[OFF_K:OFF_K + D]), col(b_in[OFF_U:OFF_U + D]),
                                 col(b_in[OFF_GA:OFF_GA + D]), col(b_in[OFF_GB:OFF_GB + D]),
                                 np.broadcast_to(np.ascontiguousarray(lg_l).reshape(1, 8), (128, 8))], axis=1)
        m = dict(shared)
        m.update(xall=np.ascontiguousarray(xs), coltab=np.ascontiguousarray(coltab.astype(np.float32)), wst=wst,
                 bsrow=np.ascontiguousarray(bs_l.reshape(1, 512)), cs_fm=cs_fm, cs_tm=np.ascontiguousarray(cs_tm))
        maps.append(m)
    return maps


def kernel(**inputs):
    maps = _host_inputs(inputs)
    nc = build_program()
    res = run_bass_kernel_spmd(nc, maps, core_ids=list(range(8)))
    out = np.empty((4, SEQ, D), dtype=np.float32)
    for core in range(8):
        b, half = core // 2, core % 2
        yc = np.asarray(res.results[core]["y"], dtype=np.float32)
        if half == 0:
            out[b, :HALF] = yc
        else:
            out[b, HALF:] = yc[::-1]
    return out
```

```python
import os
import numpy as np
import ml_dtypes
from contextlib import ExitStack
import concourse.bass as bass
import concourse.mybir as mybir
from concourse.bass_utils import run_bass_kernel_spmd

F32 = mybir.dt.float32
BF16 = mybir.dt.bfloat16
AF = mybir.ActivationFunctionType
ALU = mybir.AluOpType

D = 1024
DFF = 2816
NFT = DFF // 128
SEQ = 4096
HALF = 2048
G = 512
NG_ALL = SEQ // G
NG_OWN = HALF // G
EPS = 1e-6
OFF_U, OFF_VA, OFF_Q, OFF_K, OFF_V, OFF_GR, OFF_GA, OFF_GB = [i * 1024 for i in range(8)]
RT_FFN1, RT_MIX, RT_FFN2, RT_FINAL, RT_SGUG, RT_SGUB, RT_BK, RT_BV, RT_BGR, RT_BVA = range(10)
NRT = 10
SL_FFN1, SL_MIX, SL_BK, SL_BV, SL_BGR, SL_BVA, SL_SGUG, SL_SGUB, SL_FFN2, SL_FINAL = 0, 1, 2, 3, 0, 1, 2, 3, 0, 1
CT_BQ, CT_BK, CT_BU, CT_BGA, CT_BGB, CT_LOGIT = 0, 8, 16, 24, 32, 40
NCT = 48


class Sched:
    def __init__(self, nc):
        self.nc = nc
        self.ops = []

    def op(self, eng, fn, r=(), w=()):
        self.ops.append(dict(eng=eng, fn=fn, r=tuple(r), w=tuple(w), dma=False, chan=None, signal=False))

    def dma(self, eng, fn, chan, r=(), w=()):
        self.ops.append(dict(eng=eng, fn=fn, r=tuple(r), w=tuple(w), dma=True, chan=("ch", chan), signal=True))

    def emit(self):
        nc = self.nc
        ops = self.ops
        last_w, readers = {}, {}
        for i, o in enumerate(ops):
            deps = set()
            for k in o["r"]:
                if k in last_w:
                    deps.add(last_w[k])
            for k in o["w"]:
                if k in last_w:
                    deps.add(last_w[k])
                deps.update(readers.get(k, ()))
            deps.discard(i)
            o["deps"] = deps
            for k in o["r"]:
                readers.setdefault(k, []).append(i)
            for k in o["w"]:
                last_w[k] = i
                readers[k] = []
        for i, o in enumerate(ops):
            keep = set()
            for j in o["deps"]:
                p = ops[j]
                if (not p["dma"]) and (not o["dma"]) and p["eng"] == o["eng"] == "pe":
                    continue
                if not p["dma"]:
                    p["signal"] = True
                keep.add(j)
            o["deps"] = keep
        counts = {}
        for o in ops:
            key = o["chan"] if o["dma"] else ("eng", o["eng"])
            o["key"] = key
            if o["signal"] and o["fn"] is not None:
                counts[key] = counts.get(key, 0) + (16 if o["dma"] else 1)
                o["sigval"] = counts[key]
        keys = sorted(counts.keys(), key=str)
        with ExitStack() as es:
            sems = {k: es.enter_context(nc.semaphore("s%d" % n)) for n, k in enumerate(keys)}
            block = es.enter_context(nc.Block())
            engmap = dict(pe=block.tensor, act=block.scalar, dve=block.vector, pool=block.gpsimd, sp=block.sync)

            def make(engname):
                def body(eng):
                    waited = {}
                    for o in ops:
                        if o["eng"] != engname:
                            continue
                        need = {}
                        for j in o["deps"]:
                            p = ops[j]
                            need[p["key"]] = max(need.get(p["key"], 0), p["sigval"])
                        for k, v in sorted(need.items(), key=str):
                            if waited.get(k, 0) < v:
                                eng.wait_ge(sems[k], v)
                                waited[k] = v
                        if o["fn"] is None:
                            continue
                        ins = o["fn"](eng)
                        if o["signal"]:
                            ins.then_inc(sems[o["key"]], 16 if o["dma"] else 1)
                return body

            for name, deco in engmap.items():
                deco(make(name))


def build_program(debug=False, stop_after=None):
    nc = bass.Bass("TRN2", target_bir_lowering=False)
    ein = lambda name, shape, dt=F32: nc.dram_tensor(name, list(shape), dt, kind="ExternalInput").ap()
    xall = ein("xall", [SEQ, D])
    w1g, w1u, w1d = ein("w1g", [D, DFF]), ein("w1u", [D, DFF]), ein("w1d", [DFF, D])
    w2g, w2u, w2d = ein("w2g", [D, DFF]), ein("w2u", [D, DFF]), ein("w2d", [DFF, D])
    win = ein("win", [D, 8 * D])
    wa, wb, wo = ein("wa", [D, D]), ein("wb", [D, D]), ein("wo", [D, D])
    rowtab = ein("rowtab", [NRT, 128, D])
    coltab = ein("coltab", [128, NCT])
    wst = ein("wst", [128, 4 * 128])
    bsrow = ein("bsrow", [1, 4 * 128])
    ident_in = ein("ident", [128, 128])
    consts = ein("consts", [128, 2 + 128])
    cs_fm = ein("cs_fm", [2, 128, HALF])
    cs_tm = ein("cs_tm", [2, SEQ, 128])
    y = nc.dram_tensor("y", [HALF, D], F32, kind="ExternalOutput").ap()

    skind = "ExternalOutput" if debug else "Internal"
    scr = lambda name, shape, dt: nc.dram_tensor(name, list(shape), dt, kind=skind).ap()
    x1s = scr("x1s", [HALF, D], F32)
    h2Ts = scr("h2Ts", [NG_ALL, 128, 8 * G], BF16)
    rps = scr("rps", [HALF, D], F32)
    qTs = scr("qTs", [NG_OWN, 128, 8 * G], BF16)
    ktms = scr("ktms", [HALF, D], BF16)
    vtms = scr("vtms", [HALF, D], BF16)
    rgTs = scr("rgTs", [NG_OWN, 128, 8 * G], BF16)
    x2s = scr("x2s", [HALF, D], F32)
    dbg_state = scr("dbg_state", [128, 8 * 256], F32) if debug else None
    dbg_ct = scr("dbg_ct", [128, NCT], F32) if debug else None
    dbg_lg = scr("dbg_lg", [128, 8], F32) if debug else None
    dbg_dec = scr("dbg_dec", [128, 24], F32) if debug else None
    dbg_MT = scr("dbg_MT", [128, 512], F32) if debug else None
    dbg_kT = scr("dbg_kT", [128, 8 * G], BF16) if debug else None

    S = Sched(nc)
    es = ExitStack()
    sb = lambda name, shape, dt=F32: es.enter_context(nc.sbuf_tensor(name, list(shape), dt))
    rt = sb("rt", [128, 4, D])
    ct = sb("ct", [128, NCT])
    cst = sb("cst", [128, 130])
    ident = sb("identb", [128, 128], BF16)
    wsT = sb("wsT", [128, 512], BF16)
    bsr = sb("bsr", [1, 512], BF16)
    ones = sb("ones", [1, 128], BF16)
    ones33 = sb("ones33", [33, 128], BF16)
    brow = sb("brow", [33, 2, D], BF16)
    lg = sb("lg", [128, 8])
    dec = sb("dec", [128, 24])
    MT = sb("MT", [128, 4, 128])
    mtmp = sb("mtmp", [128, 2, 128])
    Sst = sb("Sst", [128, 8, 256])
    Tst = sb("Tst", [128, 8, 256])
    Stb = sb("Stb", [128, 8, 256], BF16)
    ss = sb("ss", [128, 32])
    rstd = sb("rstd", [128, 32])
    xt = sb("xt", [128, 4, D])
    hb = sb("hb", [128, D], BF16)
    hb2 = sb("hb2", [128, D], BF16)
    junk = sb("junk", [128, D], BF16)
    h2T = sb("h2T", [128, 8, G], BF16)
    wp = [sb("wp%d" % i, [128, 8, 256], BF16) for i in range(4)]
    ps = [es.enter_context(nc.psum_tensor("ps%d" % i, [128, 512], F32)) for i in range(6)]
    pst = [es.enter_context(nc.psum_tensor("pst%d" % i, [128, 1024], BF16)) for i in range(2)]
    cnt = dict(ps=0, pst=0, wp=0, nwp=4)

    def nps():
        i = cnt["ps"] % 6
        cnt["ps"] += 1
        return i

    def npst():
        i = cnt["pst"] % 2
        cnt["pst"] += 1
        return i

    def nwp():
        i = cnt["wp"] % cnt["nwp"]
        cnt["wp"] += 1
        return i

    def ts_ap(e, out, in0, sc, op0):
        return e.tensor_scalar(out=out, in0=in0, scalar1=sc, scalar2=None, op0=op0)

    S.dma("sp", lambda e: e.dma_start(out=ct[:], in_=coltab[:, :]), "ct", w=["ct"])
    S.dma("sp", lambda e: e.dma_start(out=cst[:], in_=consts[:, :]), "cst", w=["cst"])
    S.dma("pool", lambda e: e.dma_start(out=ident[:], in_=ident_in[:, :]), "ident", w=["ident"])
    S.dma("pool", lambda e: e.dma_start(out=wsT[:], in_=wst[:, :]), "wsT", w=["wsT"])
    S.dma("pool", lambda e: e.dma_start(out=bsr[:], in_=bsrow[:, :]), "bsr", w=["bsr"])
    S.op("dve", lambda e: e.memset(ones[:], 1.0), w=["ones"])
    S.op("dve", lambda e: e.memset(ones33[:], 1.0), w=["ones33"])
    for (bp, bi, ridx) in ((0, 0, RT_BK), (0, 1, RT_BV), (32, 0, RT_BGR), (32, 1, RT_BVA)):
        S.dma("pool", lambda e, bp=bp, bi=bi, ridx=ridx: e.dma_start(out=brow[bp:bp + 1, bi, :],
                                                                      in_=rowtab[ridx][0:1, :]),
              "brow%d_%d" % (bp, bi), w=[("brow", bp, bi)])
    S.op("dve", lambda e: e.memset(Sst[:], 0.0), w=[("Sst", h) for h in range(4)])
    S.op("dve", lambda e: e.memset(Tst[:], 0.0), w=[("Tst", h) for h in range(4)])
    S.op("dve", lambda e: e.memset(Stb[:], 0.0), w=["Stb"])
    S.op("dve", lambda e: e.tensor_scalar(out=lg[:], in0=ct[:, CT_LOGIT:CT_LOGIT + 8], scalar1=-1.0, scalar2=0.0,
                                          op0=ALU.mult, op1=ALU.add), r=["ct"], w=["lg"])
    S.op("act", lambda e: e.activation(out=lg[:], in_=lg[:], func=AF.Exp), r=["lg"], w=["lg"])
    S.op("dve", lambda e: e.tensor_scalar(out=lg[:], in0=lg[:], scalar1=1.0, scalar2=0.0, op0=ALU.add, op1=ALU.add),
         r=["lg"], w=["lg"])
    S.op("act", lambda e: e.activation(out=lg[:], in_=lg[:], func=AF.Ln), r=["lg"], w=["lg"])
    S.op("dve", lambda e: e.tensor_scalar(out=lg[:], in0=lg[:], scalar1=-1.0, scalar2=0.0, op0=ALU.mult, op1=ALU.add),
         r=["lg"], w=["lg"])
    S.op("dve", lambda e: ts_ap(e, dec[:, 0:4], lg[:, 0:4], cst[:, 1:2], ALU.mult), r=["lg", "cst"], w=["dec"])
    S.op("dve", lambda e: e.tensor_scalar(out=mtmp[:, 0, 0:1], in0=cst[:, 0:1], scalar1=-1.0, scalar2=127.0,
                                          op0=ALU.mult, op1=ALU.add), r=["cst"], w=["mtmp"])
    S.op("dve", lambda e: ts_ap(e, dec[:, 4:8], lg[:, 0:4], mtmp[:, 0, 0:1], ALU.mult), r=["lg", "mtmp"], w=["dec"])
    S.op("dve", lambda e: e.tensor_scalar(out=dec[:, 8:12], in0=lg[:, 0:4], scalar1=128.0, scalar2=0.0,
                                          op0=ALU.mult, op1=ALU.add), r=["lg"], w=["dec"])
    S.op("dve", lambda e: e.tensor_scalar(out=mtmp[:, 0, 1:2], in0=cst[:, 0:1], scalar1=-1.0, scalar2=128.0,
                                          op0=ALU.mult, op1=ALU.add), r=["cst", "mtmp"], w=["mtmp"])
    S.op("dve", lambda e: ts_ap(e, dec[:, 12:16], lg[:, 4:8], mtmp[:, 0, 1:2], ALU.mult), r=["lg", "mtmp"], w=["dec"])
    S.op("dve", lambda e: ts_ap(e, dec[:, 16:20], lg[:, 4:8], cst[:, 0:1], ALU.mult), r=["lg", "cst"], w=["dec"])
    S.op("dve", lambda e: e.tensor_scalar(out=dec[:, 20:24], in0=lg[:, 4:8], scalar1=128.0, scalar2=0.0,
                                          op0=ALU.mult, op1=ALU.add), r=["lg"], w=["dec"])
    S.op("act", lambda e: e.activation(out=dec[:], in_=dec[:], func=AF.Exp), r=["dec"], w=["dec"])
    S.op("dve", lambda e: e.tensor_scalar(out=dec[:, 4:8], in0=dec[:, 4:8], scalar1=0.0625, scalar2=0.0,
                                          op0=ALU.mult, op1=ALU.add), r=["dec"], w=["dec"])
    S.op("dve", lambda e: e.tensor_scalar(out=dec[:, 16:20], in0=dec[:, 16:20], scalar1=0.0625, scalar2=0.0,
                                          op0=ALU.mult, op1=ALU.add), r=["dec"], w=["dec"])
    QD1, KD1, G1, QD2, KD2, G2 = 0, 4, 8, 12, 16, 20
    S.op("dve", lambda e: e.tensor_scalar(out=mtmp[:, 0, :], in0=cst[:, 2:130], scalar1=0.0, scalar2=0.0,
                                          op0=ALU.max, op1=ALU.add), r=["cst", "mtmp"], w=["mtmp"])
    S.op("dve", lambda e: e.tensor_scalar(out=mtmp[:, 1, :], in0=cst[:, 2:130], scalar1=-1.0, scalar2=0.0,
                                          op0=ALU.mult, op1=ALU.max), r=["cst", "mtmp"], w=["mtmp"])
    for h in range(4):
        S.op("dve", lambda e, h=h: ts_ap(e, MT[:, h, :], mtmp[:, 0, :], lg[:, h:h + 1], ALU.mult), r=["mtmp", "lg"], w=["MT"])
        S.op("dve", lambda e, h=h: e.scalar_tensor_tensor(out=MT[:, h, :], in0=mtmp[:, 1, :],
                                                          scalar=lg[:, 4 + h:5 + h], in1=MT[:, h, :],
                                                          op0=ALU.mult, op1=ALU.add), r=["mtmp", "lg", "MT"], w=["MT"])
    S.op("act", lambda e: e.activation(out=MT[:], in_=MT[:], func=AF.Exp), r=["MT"], w=["MT"])
    S.op("dve", lambda e: e.tensor_scalar(out=MT[:], in0=MT[:], scalar1=0.0625, scalar2=0.0, op0=ALU.mult, op1=ALU.add),
         r=["MT"], w=["MT"])

    if debug:
        S.dma("sp", lambda e: e.dma_start(out=dbg_ct[:, :], in_=ct[:]), "dbg1", r=["ct"], w=["dbg1"])
        S.dma("sp", lambda e: e.dma_start(out=dbg_lg[:, :], in_=lg[:]), "dbg2", r=["lg"], w=["dbg2"])
        S.dma("sp", lambda e: e.dma_start(out=dbg_dec[:, :], in_=dec[:]), "dbg3", r=["dec"], w=["dbg3"])
        S.dma("sp", lambda e: e.dma_start(out=dbg_MT[:, :], in_=MT[:].rearrange("p a b -> p (a b)")), "dbg4",
              r=["MT"], w=["dbg4"])
    def rt_load(slot, idx):
        S.dma("sp", lambda e: e.dma_start(out=rt[:, slot, :], in_=rowtab[idx]), "rt%d" % slot, w=[("rt", slot)])

    arena = sb("arena", [128, 21504])
    dummy = sb("bdummy", [128, 8])

    def carve(off, shape, dt=F32):
        nb = int(np.prod(shape)) * (4 if dt == F32 else 2)
        ap = arena[:, off // 4:(off + nb) // 4]
        if dt == BF16:
            ap = ap.bitcast(BF16)
        if len(shape) == 2:
            ap = ap.rearrange("p (a b) -> p a b", a=shape[0])
        return ap

    FFN_KEYS = ["hT", ("wp", 4), ("wp", 5)] + [("tT", i) for i in range(NFT)] + \
        [("wd", h, b) for h in range(2) for b in range(NFT // 2)]
    CTMP_KEYS = ["qtmp", "fA", "fB", "csfm", "rp", "ri", ("PT", 0), ("PT", 1)]
    E1_KEYS = ["sga", "sgb", "mixT"]
    CRS_KEYS = [("crs", c) for c in range(4)]
    MIX_KEYS = ["qT", "kT", "rpg", "rgT", "rr"] + [("sgr", i) for i in range(4)] + CTMP_KEYS

    def barrier(rk, wk):
        S.op("dve", lambda e: e.memset(dummy[:], 0.0), w=list(rk) + list(wk))

    def load_w(W, col0, ncols=256, row_tiles=8):
        i = nwp()
        S.dma("pool", lambda e: e.dma_start(
            out=wp[i][:, 0:row_tiles, 0:ncols],
            in_=W[0:row_tiles * 128, col0:col0 + ncols].rearrange("(kt p) c -> p kt c", p=128)),
            "wp%d" % i, w=[("wp", i)])
        return i

    def norm_group(rtidx, dstT, dstkey):
        for c in range(4):
            S.op("act", lambda e, c=c: e.activation(out=junk[:], in_=xt[:, c, :], func=AF.Square,
                                                    accum_out=ss[:, c:c + 1]), r=[("xt", c)], w=[("ss", c)])
        allss = [("ss", c) for c in range(4)]
        allr = [("rstd", c) for c in range(4)]
        S.op("dve", lambda e: e.tensor_scalar(out=rstd[:, 0:4], in0=ss[:, 0:4], scalar1=1.0 / D, scalar2=EPS,
                                              op0=ALU.mult, op1=ALU.add), r=allss, w=allr)
        S.op("act", lambda e: e.activation(out=rstd[:, 0:4], in_=rstd[:, 0:4], func=AF.Sqrt), r=allr, w=allr)
        S.op("dve", lambda e: e.reciprocal(out=rstd[:, 0:4], in_=rstd[:, 0:4]), r=allr, w=allr)
        for c in range(4):
            hbuf, hkey = (hb, "hb") if c % 2 == 0 else (hb2, "hb2")
            S.op("dve", lambda e, c=c, hbuf=hbuf: e.scalar_tensor_tensor(
                out=hbuf[:], in0=xt[:, c, :], scalar=rstd[:, c:c + 1], in1=rt[:, rtidx, :], op0=ALU.mult,
                op1=ALU.mult), r=[("xt", c), ("rstd", c), ("rt", rtidx)], w=[hkey])
            transpose_into(hbuf, dstT, c, hkey, dstkey)

    def transpose_into(srcb, dstT, c, srckey, dstkey):
        p = npst()
        for kt in range(8):
            S.op("pe", lambda e, kt=kt: e.transpose(out=pst[p][:, kt * 128:(kt + 1) * 128],
                                                    in_=srcb[:, kt * 128:(kt + 1) * 128], identity=ident[:]),
                 r=[srckey, "ident"], w=[("pst", p)])
        S.op("act", lambda e: e.copy(out=dstT[:, :, c * 128:(c + 1) * 128],
                                     in_=pst[p][:].rearrange("p (k t) -> p k t", k=8)),
             r=[("pst", p)], w=[dstkey])

    hT = carve(0, [8, G], BF16)
    wp.append(carve(75776, [8, 256], BF16))
    wp.append(carve(79872, [8, 256], BF16))
    tT = carve(8192, [NFT, G], BF16)
    wd = [carve(30720 + i * 22528, [NFT, 512], BF16) for i in range(2)]
    sg = [sb("sg%d" % i, [128, 512]) for i in range(2)]
    gtmp = sb("gtmp", [128, 256])

    def ffn(rtidx, Wg, Wu, Wd, load_wd=True):
        cnt["nwp"] = 6
        norm_group(rtidx, hT, "hT")
        for blk in range(NFT // 2):
            ig = load_w(Wg, blk * 256)
            iu = load_w(Wu, blk * 256)
            for j in range(2):
                ft = blk * 2 + j
                pg, pu = nps(), nps()
                for kt in range(8):
                    S.op("pe", lambda e, kt=kt, pg=pg, ig=ig, j=j: e.matmul(
                        ps[pg][:], lhsT=wp[ig][:, kt, j * 128:(j + 1) * 128], rhs=hT[:, kt, :],
                        start=(kt == 0), stop=(kt == 7)), r=[("wp", ig), "hT"], w=[("ps", pg)])
                for kt in range(8):
                    S.op("pe", lambda e, kt=kt, pu=pu, iu=iu, j=j: e.matmul(
                        ps[pu][:], lhsT=wp[iu][:, kt, j * 128:(j + 1) * 128], rhs=hT[:, kt, :],
                        start=(kt == 0), stop=(kt == 7)), r=[("wp", iu), "hT"], w=[("ps", pu)])
                si = ft % 2
                if j == 0 and load_wd:
                    for half in range(2):
                        S.dma("pool", lambda e, half=half, blk=blk: e.dma_start(
                            out=wd[half][:, 2 * blk:2 * blk + 2, :],
                            in_=Wd[blk * 256:(blk + 1) * 256, half * 512:(half + 1) * 512].rearrange(
                                "(ft p) c -> p ft c", p=128)),
                            "wd%d_%d" % (half, blk), w=[("wd", half, blk)])
                S.op("act", lambda e, pg=pg, si=si: e.activation(out=sg[si][:], in_=ps[pg][:], func=AF.Silu),
                     r=[("ps", pg)], w=[("sg", si)])
                S.op("dve", lambda e, pu=pu, si=si, ft=ft: e.tensor_tensor(out=tT[:, ft, :], in0=sg[si][:],
                                                                           in1=ps[pu][:], op=ALU.mult),
                     r=[("ps", pu), ("sg", si)], w=[("tT", ft)])
        for tt in range(4):
            for half in range(2):
                p = nps()
                for ft in range(NFT):
                    S.op("pe", lambda e, ft=ft, p=p, tt=tt, half=half: e.matmul(
                        ps[p][:], lhsT=tT[:, ft, tt * 128:(tt + 1) * 128], rhs=wd[half][:, ft, :],
                        start=(ft == 0), stop=(ft == NFT - 1)), r=[("tT", ft), ("wd", half, ft // 2)], w=[("ps", p)])
                S.op("dve", lambda e, p=p, tt=tt, half=half: e.scalar_tensor_tensor(
                    out=xt[:, tt, half * 512:(half + 1) * 512], in0=ps[p][:], scalar=0.5,
                    in1=xt[:, tt, half * 512:(half + 1) * 512], op0=ALU.mult, op1=ALU.add),
                    r=[("ps", p), ("xt", tt)], w=[("xt", tt)])
        cnt["nwp"] = 4

    def load_xt(src, g, srckey=None):
        for c in range(4):
            S.dma("sp", lambda e, c=c: e.dma_start(out=xt[:, c, :], in_=src[g * G + c * 128:g * G + (c + 1) * 128, :]),
                  "xt%d" % c, r=([(srckey, g, c)] if srckey else []), w=[("xt", c)])

    def store_xt(dst, g, dstkey):
        for c in range(4):
            S.dma("sp", lambda e, c=c: e.dma_start(out=dst[g * G + c * 128:g * G + (c + 1) * 128, :], in_=xt[:, c, :]),
                  "xt_st%d" % c, r=[("xt", c)], w=[(dstkey, g, c)])

    def rows(ap, g):
        return ap[g * G:(g + 1) * G, :].rearrange("(c p) d -> p c d", p=128)

    def own_rows(ap, g):
        return rows(ap, g)

    rt_load(SL_FFN1, RT_FFN1)
    rt_load(SL_MIX, RT_MIX)
    for g in [7, 6, 5, 4, 0, 1, 2, 3]:
        load_xt(xall, g)
        ffn(SL_FFN1, w1g, w1u, w1d, load_wd=(g == 7))
        if g < NG_OWN:
            store_xt(x1s, g, "x1s")
        norm_group(SL_MIX, h2T, "h2T")
        S.dma("sp", lambda e, g=g: e.dma_start(out=h2Ts[g], in_=h2T[:].rearrange("p k t -> p (k t)")), "h2T_st",
              r=["h2T"], w=[("h2Ts", g)])

    if stop_after == "A":
        return finish(nc, S, es, y, None)

    ktm = sb("ktm", [128, 4, D], BF16)
    vtm = sb("vtm", [128, 4, D], BF16)
    Vd = sb("Vd", [128, D], BF16)
    cstm = sb("cstm", [128, 2, 4, 128])

    def load_h2T(g):
        S.dma("sp", lambda e: e.dma_start(out=h2T[:].rearrange("p k t -> p (k t)"), in_=h2Ts[g]), "h2T_ld",
              r=[("h2Ts", g)], w=["h2T"])

    def proj_tm(off, bp, bi, consume):
        for cbp in range(2):
            iws = [load_w(win, off + (2 * cbp + hf) * 256) for hf in range(2)]
            for tt in range(4):
                p = nps()
                for hf in range(2):
                    iw = iws[hf]
                    cb = 2 * cbp + hf
                    for kt in range(8):
                        S.op("pe", lambda e, kt=kt, p=p, tt=tt, iw=iw, hf=hf: e.matmul(
                            ps[p][:, hf * 256:(hf + 1) * 256], lhsT=h2T[:, kt, tt * 128:(tt + 1) * 128],
                            rhs=wp[iw][:, kt, :], start=(kt == 0), stop=False), r=["h2T", ("wp", iw)], w=[("ps", p)])
                    S.op("pe", lambda e, p=p, cb=cb, hf=hf: e.matmul(
                        ps[p][:, hf * 256:(hf + 1) * 256], lhsT=ones33[bp:bp + 1, :],
                        rhs=brow[bp:bp + 1, bi, cb * 256:(cb + 1) * 256], start=False, stop=True),
                        r=["ones33", ("brow", bp, bi)], w=[("ps", p)])
                consume(p, cbp, tt)

    rA4 = sg[0][:].rearrange("p (a b) -> p a b", a=4)
    rB4 = sg[1][:].rearrange("p (a b) -> p a b", a=4)

    def kv_tm(g):
        for s2 in range(2):
            S.dma("sp", lambda e, s2=s2: e.dma_start(
                out=cstm[:, s2, :, :], in_=cs_tm[s2, g * G:(g + 1) * G, :].rearrange("(c p) f -> p c f", p=128)),
                "cstm%d" % s2, w=[("cstm", s2)])

        def k_consume(p, cbp, tt):
            pv = ps[p][:].rearrange("p (a b) -> p a b", a=4)
            S.op("dve", lambda e: e.tensor_tensor(
                out=rA4, in0=pv, in1=cstm[:, 0, tt, :].unsqueeze(1).to_broadcast([128, 4, 128]),
                op=ALU.mult), r=[("ps", p), ("cstm", 0)], w=[("sg", 0)])
            S.op("dve", lambda e: e.tensor_tensor(
                out=rB4, in0=pv, in1=cstm[:, 1, tt, :].unsqueeze(1).to_broadcast([128, 4, 128]),
                op=ALU.mult), r=[("ps", p), ("cstm", 1)], w=[("sg", 1)])
            kv4 = ktm[:, tt, cbp * 512:(cbp + 1) * 512].rearrange("p (c t f) -> p c t f", c=2, t=2)
            a4 = rA4.rearrange("p (c t) f -> p c t f", c=2)
            b4 = rB4.rearrange("p (c t) f -> p c t f", c=2)
            S.op("dve", lambda e: e.tensor_tensor(out=kv4[:, :, 0, :], in0=a4[:, :, 0, :], in1=b4[:, :, 1, :],
                                                  op=ALU.subtract), r=[("sg", 0), ("sg", 1)], w=[("ktm", tt)])
            S.op("dve", lambda e: e.tensor_tensor(out=kv4[:, :, 1, :], in0=a4[:, :, 1, :], in1=b4[:, :, 0, :],
                                                  op=ALU.add), r=[("sg", 0), ("sg", 1)], w=[("ktm", tt)])

        def v_consume(p, cbp, tt):
            S.op("act", lambda e: e.copy(out=vtm[:, tt, cbp * 512:(cbp + 1) * 512], in_=ps[p][:]),
                 r=[("ps", p)], w=[("vtm", tt)])

        proj_tm(OFF_K, 0, 0, k_consume)
        proj_tm(OFF_V, 0, 1, v_consume)

    def state_mm(c, kd, vd_on_dve=False):
        if vd_on_dve:
            S.op("dve", lambda e: e.tensor_tensor(
                out=Vd[:].rearrange("p (h e) -> p h e", h=4), in0=vtm[:, c, :].rearrange("p (h e) -> p h e", h=4),
                in1=dec[:, kd:kd + 4].unsqueeze(2).to_broadcast([128, 4, 256]), op=ALU.mult),
                r=[("vtm", c), "dec"], w=["Vd"])
        else:
            for h in range(4):
                S.op("act", lambda e, h=h: e.activation(
                    out=Vd[:, h * 256:(h + 1) * 256], in_=vtm[:, c, h * 256:(h + 1) * 256], func=AF.Copy,
                    scale=dec[:, kd + h:kd + h + 1]), r=[("vtm", c), "dec"], w=["Vd"])
        pids = []
        for h in range(4):
            p = nps()
            pids.append(p)
            for dt in range(2):
                S.op("pe", lambda e, h=h, dt=dt, p=p: e.matmul(
                    ps[p][:, dt * 256:(dt + 1) * 256], lhsT=ktm[:, c, h * 256 + dt * 128:h * 256 + (dt + 1) * 128],
                    rhs=Vd[:, h * 256:(h + 1) * 256], start=True, stop=True),
                    r=[("ktm", c), "Vd"], w=[("ps", p)])
        return pids

    def state_acc(Sin, inkey, Sout, outkey, gd, pids):
        for h in range(4):
            p = pids[h]
            S.op("dve", lambda e, h=h, p=p: e.scalar_tensor_tensor(
                out=Sout[:, 2 * h:2 * h + 2, :], in0=Sin[:, 2 * h:2 * h + 2, :], scalar=dec[:, gd + h:gd + h + 1],
                in1=ps[p][:].rearrange("p (a b) -> p a b", a=2), op0=ALU.mult, op1=ALU.add),
                r=[(inkey, h), ("ps", p), "dec"], w=[(outkey, h)])

    def state_update(St, stkey, c, kd, gd):
        state_acc(St, stkey, St, stkey, gd, state_mm(c, kd))

    for g in [7, 6, 5, 4]:
        load_h2T(g)
        kv_tm(g)
        for c in [3, 2, 1, 0]:
            state_update(Tst, "Tst", c, KD2, G2)

    if debug:
        S.dma("sp", lambda e: e.dma_start(out=dbg_state[:, :], in_=Tst[:].rearrange("p a b -> p (a b)")), "dbg",
              r=[("Tst", h) for h in range(4)], w=["dbgs"])
    if stop_after == "B":
        return finish(nc, S, es, y, None)

    barrier(FFN_KEYS, MIX_KEYS)
    qT = carve(0, [8, G], BF16)
    kT = carve(8192, [8, G], BF16)
    rpg = carve(16384, [4, D])
    sgr = carve(32768, [4, D])
    rgT = carve(49152, [8, G], BF16)
    rr = carve(57344, [1, D])[:, 0, :]
    qtmp = carve(61440, [2, G])
    fA = carve(65536, [2, G])
    fB = carve(69632, [2, G])
    csfm = carve(73728, [2, G])
    rp = carve(77824, [1, D])[:, 0, :]
    ri = carve(81920, [1, 256])[:, 0, :]
    PT = [carve(82944 + 256 * i, [1, 128], BF16)[:, 0, :] for i in range(2)]
    crs = carve(61440, [4, D])
    sga = carve(61440, [8, G], BF16)
    sgb = carve(69632, [8, G], BF16)
    mixT = carve(77824, [8, G], BF16)

    def proj_fm_rot(off, ctoff, dst, dstkey):
        for blk in range(4):
            iw = load_w(win, off + blk * 256)
            for j in range(2):
                jt = 2 * blk + j
                p = nps()
                for kt in range(8):
                    S.op("pe", lambda e, kt=kt, p=p, j=j, iw=iw: e.matmul(
                        ps[p][:], lhsT=wp[iw][:, kt, j * 128:(j + 1) * 128], rhs=h2T[:, kt, :],
                        start=(kt == 0), stop=(kt == 7)), r=["h2T", ("wp", iw)], w=[("ps", p)])
                S.op("dve", lambda e, p=p, j=j, jt=jt: ts_ap(e, qtmp[:, j, :], ps[p][:], ct[:, ctoff + jt:ctoff + jt + 1], ALU.add), r=[("ps", p), "ct"], w=["qtmp"])
            S.op("dve", lambda e: e.tensor_tensor(
                out=fA[:], in0=qtmp[:], in1=csfm[:, 0, :].unsqueeze(1).to_broadcast([128, 2, G]), op=ALU.mult),
                r=["qtmp", "csfm"], w=["fA"])
            S.op("dve", lambda e: e.tensor_tensor(
                out=fB[:], in0=qtmp[:], in1=csfm[:, 1, :].unsqueeze(1).to_broadcast([128, 2, G]), op=ALU.mult),
                r=["qtmp", "csfm"], w=["fB"])
            S.op("dve", lambda e, blk=blk: e.tensor_tensor(out=dst[:, 2 * blk, :], in0=fA[:, 0, :], in1=fB[:, 1, :],
                                                           op=ALU.subtract), r=["fA", "fB"], w=[dstkey])
            S.op("dve", lambda e, blk=blk: e.tensor_tensor(out=dst[:, 2 * blk + 1, :], in0=fA[:, 1, :],
                                                           in1=fB[:, 0, :], op=ALU.add), r=["fA", "fB"], w=[dstkey])

    for g in range(NG_OWN):
        load_h2T(g)
        kv_tm(g)
        S.dma("sp", lambda e, g=g: e.dma_start(out=rows(ktms, g), in_=ktm[:]), "ktm_st",
              r=[("ktm", i) for i in range(4)], w=[("ktms", g)])
        S.dma("sp", lambda e, g=g: e.dma_start(out=rows(vtms, g), in_=vtm[:]), "vtm_st",
              r=[("vtm", i) for i in range(4)], w=[("vtms", g)])
        S.dma("sp", lambda e, g=g: e.dma_start(
            out=csfm[:], in_=cs_fm[:, :, g * G:(g + 1) * G].rearrange("s p t -> p s t")), "csfm", w=["csfm"])
        proj_fm_rot(OFF_Q, CT_BQ, qT, "qT")
        proj_fm_rot(OFF_K, CT_BK, kT, "kT")
        if debug and g == 0:
            S.dma("sp", lambda e: e.dma_start(out=dbg_kT[:, :], in_=kT[:].rearrange("p k t -> p (k t)")), "dbg5",
                  r=["kT"], w=["dbg5"])
        S.dma("sp", lambda e, g=g: e.dma_start(out=qTs[g], in_=qT[:].rearrange("p k t -> p (k t)")), "qT_st",
              r=["qT"], w=[("qTs", g)])
        for c in range(4):
            cs = slice(c * 128, (c + 1) * 128)
            state_update(Sst, "Sst", c, KD1, G1)
            for h in range(4):
                p1 = nps()
                for dt in range(2):
                    S.op("pe", lambda e, h=h, dt=dt, p1=p1, cs=cs: e.matmul(
                        ps[p1][:, 0:128], lhsT=kT[:, 2 * h + dt, cs], rhs=qT[:, 2 * h + dt, cs],
                        start=(dt == 0), stop=(dt == 1)), r=["kT", "qT"], w=[("ps", p1)])
                pi = h % 2
                S.op("dve", lambda e, h=h, p1=p1, pi=pi: e.tensor_tensor(out=PT[pi][:], in0=ps[p1][:, 0:128],
                                                                         in1=MT[:, h, :], op=ALU.mult),
                     r=[("ps", p1), "MT"], w=[("PT", pi)])
                p2 = nps()
                S.op("pe", lambda e, h=h, p2=p2, pi=pi, c=c: e.matmul(
                    ps[p2][:, 0:256], lhsT=PT[pi][:], rhs=vtm[:, c, h * 256:(h + 1) * 256], start=True, stop=True),
                    r=[("PT", pi), ("vtm", c)], w=[("ps", p2)])
                for dt in range(2):
                    S.op("pe", lambda e, h=h, dt=dt, p2=p2, cs=cs: e.matmul(
                        ps[p2][:, 256:512], lhsT=qT[:, 2 * h + dt, cs], rhs=Stb[:, 2 * h + dt, :],
                        start=(dt == 0), stop=(dt == 1)), r=["qT", "Stb"], w=[("ps", p2)])
                S.op("act", lambda e, p2=p2: e.copy(out=ri[:], in_=ps[p2][:, 0:256]), r=[("ps", p2)], w=["ri"])
                S.op("dve", lambda e, h=h, p2=p2: e.scalar_tensor_tensor(
                    out=rp[:, h * 256:(h + 1) * 256], in0=ps[p2][:, 256:512], scalar=dec[:, QD1 + h:QD1 + h + 1],
                    in1=ri[:], op0=ALU.mult, op1=ALU.add), r=[("ps", p2), "ri", "dec"], w=["rp"])
            S.dma("sp", lambda e, g=g, c=c: e.dma_start(out=rps[g * G + c * 128:g * G + (c + 1) * 128, :], in_=rp[:]),
                  "rp_st", r=["rp"], w=[("rps", g, c)])
            S.op("act", lambda e: e.copy(out=Stb[:], in_=Sst[:]), r=[("Sst", h) for h in range(4)], w=["Stb"])

    if stop_after == "C":
        return finish(nc, S, es, y, None)

    Tbufs = [(Tst, "Tst"), (Sst, "Sst")]
    barrier(CTMP_KEYS, CRS_KEYS)
    for g in [3, 2, 1, 0]:
        load_h2T(g)
        S.dma("sp", lambda e, g=g: e.dma_start(out=qT[:].rearrange("p k t -> p (k t)"), in_=qTs[g]), "qT_ld",
              r=[("qTs", g)], w=["qT"])
        S.dma("sp", lambda e, g=g: e.dma_start(out=ktm[:], in_=rows(ktms, g)), "ktm_ld", r=[("ktms", g)],
              w=[("ktm", i) for i in range(4)])
        S.dma("sp", lambda e, g=g: e.dma_start(out=vtm[:], in_=rows(vtms, g)), "vtm_ld", r=[("vtms", g)],
              w=[("vtm", i) for i in range(4)])
        S.dma("sp", lambda e, g=g: e.dma_start(out=rpg[:], in_=rows(rps, g)), "rpg_ld",
              r=[("rps", g, c) for c in range(4)], w=["rpg"] + [("rpgc", c) for c in range(4)])
        def gr_piece(cbp):
            iws = [load_w(win, OFF_GR + (2 * cbp + hf) * 256) for hf in range(2)]
            for tt in range(4):
                p = nps()
                for hf in range(2):
                    iw = iws[hf]
                    cb = 2 * cbp + hf
                    for kt in range(8):
                        S.op("pe", lambda e, kt=kt, p=p, tt=tt, iw=iw, hf=hf: e.matmul(
                            ps[p][:, hf * 256:(hf + 1) * 256], lhsT=h2T[:, kt, tt * 128:(tt + 1) * 128],
                            rhs=wp[iw][:, kt, :], start=(kt == 0), stop=False), r=["h2T", ("wp", iw)], w=[("ps", p)])
                    S.op("pe", lambda e, p=p, cb=cb, hf=hf: e.matmul(
                        ps[p][:, hf * 256:(hf + 1) * 256], lhsT=ones33[32:33, :],
                        rhs=brow[32:33, 0, cb * 256:(cb + 1) * 256], start=False, stop=True),
                        r=["ones33", ("brow", 32, 0)], w=[("ps", p)])
                S.op("act", lambda e, cbp=cbp, tt=tt, p=p: e.activation(
                    out=sgr[:, tt, cbp * 512:(cbp + 1) * 512], in_=ps[p][:], func=AF.Silu),
                    r=[("ps", p)], w=[("sgr", tt)])

        for c in [3, 2, 1, 0]:
            cs = slice(c * 128, (c + 1) * 128)
            cur, curk = Tbufs[0]
            nxt, nxtk = Tbufs[1]
            pids = state_mm(c, KD2, vd_on_dve=True)
            S.op("act", lambda e, cur=cur: e.copy(out=Stb[:], in_=cur[:]), r=[(curk, h) for h in range(4)], w=["Stb"])
            state_acc(cur, curk, nxt, nxtk, G2, pids)
            Tbufs.reverse()
            if c >= 2:
                gr_piece(3 - c)
            for h in range(4):
                p = nps()
                for dt in range(2):
                    S.op("pe", lambda e, h=h, dt=dt, p=p, cs=cs: e.matmul(
                        ps[p][:, 0:256], lhsT=qT[:, 2 * h + dt, cs], rhs=Stb[:, 2 * h + dt, :],
                        start=(dt == 0), stop=(dt == 1)), r=["qT", "Stb"], w=[("ps", p)])
                S.op("act", lambda e, h=h, p=p, c=c: e.activation(
                    out=crs[:, c, h * 256:(h + 1) * 256], in_=ps[p][:, 0:256], func=AF.Copy,
                    scale=dec[:, QD2 + h:QD2 + h + 1]), r=[("ps", p), "dec"], w=[("crs", c)])
        for c in range(4):
            S.op("dve", lambda e, c=c: e.tensor_tensor(out=rpg[:, c, :], in0=rpg[:, c, :], in1=crs[:, c, :], op=ALU.add),
                 r=["rpg", ("crs", c)], w=[("rpgc", c)])
        for c in range(4):
            for h in range(4):
                S.op("act", lambda e, h=h, c=c: e.activation(
                    out=junk[:, 0:256], in_=rpg[:, c, h * 256:(h + 1) * 256], func=AF.Square,
                    accum_out=ss[:, 16 + 4 * c + h:17 + 4 * c + h]), r=[("rpgc", c), "rpg"], w=[("ssd", c)])
        allss = [("ssd", c) for c in range(4)]
        S.op("dve", lambda e: e.tensor_scalar(out=rstd[:, 16:32], in0=ss[:, 16:32], scalar1=1.0 / 256, scalar2=EPS,
                                              op0=ALU.mult, op1=ALU.add), r=allss, w=["rstdd"])
        S.op("act", lambda e: e.activation(out=rstd[:, 16:32], in_=rstd[:, 16:32], func=AF.Sqrt), r=["rstdd"], w=["rstdd"])
        S.op("dve", lambda e: e.reciprocal(out=rstd[:, 16:32], in_=rstd[:, 16:32]), r=["rstdd"], w=["rstdd"])
        for c in range(4):
            hbuf, hkey = (hb, "hb") if c % 2 == 0 else (hb2, "hb2")
            for h in range(4):
                S.op("dve", lambda e, h=h, c=c, hbuf=hbuf: e.scalar_tensor_tensor(
                    out=hbuf[:, h * 256:(h + 1) * 256], in0=rpg[:, c, h * 256:(h + 1) * 256],
                    scalar=rstd[:, 16 + 4 * c + h:17 + 4 * c + h], in1=sgr[:, c, h * 256:(h + 1) * 256],
                    op0=ALU.mult, op1=ALU.mult), r=[("rpgc", c), "rpg", "rstdd", ("sgr", c)], w=[hkey])
            transpose_into(hbuf, rgT, c, hkey, "rgT")
        S.dma("sp", lambda e, g=g: e.dma_start(out=rgTs[g], in_=rgT[:].rearrange("p k t -> p (k t)")), "rgT_st",
              r=["rgT"], w=[("rgTs", g)])

    if stop_after == "D":
        return finish(nc, S, es, y, None)

    vaf = rpg
    vn = ktm
    uT = kT
    aT = qT
    barrier(CTMP_KEYS + CRS_KEYS, E1_KEYS)
    rt_load(SL_BVA, RT_BVA)
    rt_load(SL_SGUG, RT_SGUG)
    rt_load(SL_SGUB, RT_SGUB)
    maT = sgr
    maTv = maT[:].rearrange("p a b -> p (a b)").rearrange("p (k t) -> p k t", k=8)
    for g in range(NG_OWN):
        load_h2T(g)
        S.dma("sp", lambda e, g=g: e.dma_start(out=rgT[:].rearrange("p k t -> p (k t)"), in_=rgTs[g]), "rgT_ld",
              r=[("rgTs", g)], w=["rgT"])
        load_xt(x1s, g, "x1s")
        for cb in range(4):
            iw = load_w(win, OFF_VA + cb * 256)
            for tt in range(4):
                p = nps()
                for kt in range(8):
                    S.op("pe", lambda e, kt=kt, p=p, tt=tt, iw=iw: e.matmul(
                        ps[p][:, 0:256], lhsT=h2T[:, kt, tt * 128:(tt + 1) * 128], rhs=wp[iw][:, kt, :],
                        start=(kt == 0), stop=(kt == 7)), r=["h2T", ("wp", iw)], w=[("ps", p)])
                S.op("dve", lambda e, p=p, cb=cb, tt=tt: e.tensor_tensor(
                    out=vaf[:, tt, cb * 256:(cb + 1) * 256], in0=ps[p][:, 0:256],
                    in1=rt[:, SL_BVA, cb * 256:(cb + 1) * 256], op=ALU.add), r=[("ps", p), ("rt", SL_BVA)], w=["rpg"])
        for tt in range(4):
            S.op("dve", lambda e: e.memset(ss[:, 8:10], 0.0), w=["ss01"])
            S.op("act", lambda e, tt=tt: e.activation(out=vaf[:, tt, :], in_=vaf[:, tt, :], func=AF.Gelu,
                                                      accum_out=ss[:, 8:9]), r=["rpg", "ss01"], w=["rpg", "ss01"])
            S.op("dve", lambda e: e.tensor_scalar(out=ss[:, 10:11], in0=ss[:, 8:9], scalar1=1.0 / D, scalar2=0.0,
                                                  op0=ALU.mult, op1=ALU.add), r=["ss01"], w=["ssm"])
            S.op("dve", lambda e, tt=tt: ts_ap(e, vaf[:, tt, :], vaf[:, tt, :], ss[:, 10:11], ALU.subtract), r=["rpg", "ssm"], w=["rpg"])
            S.op("act", lambda e, tt=tt: e.activation(out=rr[:], in_=vaf[:, tt, :], func=AF.Square,
                                                      accum_out=ss[:, 9:10]), r=["rpg", "ss01"], w=["rr", "ss01"])
            S.op("dve", lambda e: e.tensor_scalar(out=ss[:, 11:12], in0=ss[:, 9:10], scalar1=1.0 / D, scalar2=EPS,
                                                  op0=ALU.mult, op1=ALU.add), r=["ss01"], w=["ssr"])
            S.op("act", lambda e: e.activation(out=ss[:, 11:12], in_=ss[:, 11:12], func=AF.Sqrt), r=["ssr"], w=["ssr"])
            S.op("dve", lambda e: e.reciprocal(out=ss[:, 11:12], in_=ss[:, 11:12]), r=["ssr"], w=["ssr"])
            S.op("dve", lambda e, tt=tt: e.scalar_tensor_tensor(
                out=vaf[:, tt, :], in0=vaf[:, tt, :], scalar=ss[:, 11:12], in1=rt[:, SL_SGUG, :], op0=ALU.mult,
                op1=ALU.mult), r=["rpg", "ssr", ("rt", SL_SGUG)], w=["rpg"])
            S.op("dve", lambda e, tt=tt: e.tensor_tensor(out=vn[:, tt, :], in0=vaf[:, tt, :], in1=rt[:, SL_SGUB, :],
                                                         op=ALU.add), r=["rpg", ("rt", SL_SGUB)], w=[("ktm", tt)])
        for blk in range(4):
            iw = load_w(win, OFF_U + blk * 256)
            for j in range(2):
                jt = 2 * blk + j
                p = nps()
                for kt in range(8):
                    S.op("pe", lambda e, kt=kt, p=p, j=j, iw=iw: e.matmul(
                        ps[p][:], lhsT=wp[iw][:, kt, j * 128:(j + 1) * 128], rhs=h2T[:, kt, :],
                        start=(kt == 0), stop=(kt == 7)), r=["h2T", ("wp", iw)], w=[("ps", p)])
                si = jt % 2
                S.op("dve", lambda e, p=p, jt=jt, si=si: ts_ap(e, sg[si][:], ps[p][:], ct[:, CT_BU + jt:CT_BU + jt + 1], ALU.add), r=[("ps", p), "ct"], w=[("sg", si)])
                S.op("act", lambda e, jt=jt, si=si: e.activation(out=uT[:, jt, :], in_=sg[si][:], func=AF.Gelu),
                     r=[("sg", si)], w=["kT"])
        for c in range(4):
            cs = slice(c * 128, (c + 1) * 128)
            for fh in range(2):
                p = nps()
                for f4 in range(4):
                    ft = fh * 4 + f4
                    gg = ft // 2
                    S.op("pe", lambda e, ft=ft, f4=f4, gg=gg, p=p, c=c: e.matmul(
                        ps[p][:, f4 * 128:(f4 + 1) * 128], lhsT=vn[:, c, ft * 128:(ft + 1) * 128],
                        rhs=wsT[:, gg * 128:(gg + 1) * 128], start=True, stop=False),
                        r=[("ktm", c), "wsT"], w=[("ps", p)])
                    S.op("pe", lambda e, f4=f4, gg=gg, p=p: e.matmul(
                        ps[p][:, f4 * 128:(f4 + 1) * 128], lhsT=ones[0:1, :], rhs=bsr[0:1, gg * 128:(gg + 1) * 128],
                        start=False, stop=True), r=["ones", "bsr"], w=[("ps", p)])
                S.op("dve", lambda e, fh=fh, p=p, cs=cs: e.tensor_tensor(
                    out=aT[:, fh * 4:(fh + 1) * 4, cs], in0=uT[:, fh * 4:(fh + 1) * 4, cs],
                    in1=ps[p][:].rearrange("p (a b) -> p a b", a=4), op=ALU.mult), r=["kT", ("ps", p)], w=["qT"])
        for (off, cto, dst, dk) in ((OFF_GA, CT_BGA, sga, "sga"), (OFF_GB, CT_BGB, sgb, "sgb")):
            for blk in range(4):
                iw = load_w(win, off + blk * 256)
                for j in range(2):
                    jt = 2 * blk + j
                    p = nps()
                    for kt in range(8):
                        S.op("pe", lambda e, kt=kt, p=p, j=j, iw=iw: e.matmul(
                            ps[p][:], lhsT=wp[iw][:, kt, j * 128:(j + 1) * 128], rhs=h2T[:, kt, :],
                            start=(kt == 0), stop=(kt == 7)), r=["h2T", ("wp", iw)], w=[("ps", p)])
                    si = jt % 2
                    S.op("dve", lambda e, p=p, jt=jt, si=si, cto=cto: ts_ap(e, sg[si][:], ps[p][:], ct[:, cto + jt:cto + jt + 1], ALU.add), r=[("ps", p), "ct"], w=[("sg", si)])
                    S.op("act", lambda e, jt=jt, dst=dst, si=si: e.activation(
                        out=dst[:, jt, :], in_=sg[si][:], func=AF.Sigmoid), r=[("sg", si)], w=[dk])
        for blk in range(4):
            iw = load_w(wa, blk * 256)
            for j in range(2):
                jt = 2 * blk + j
                p = nps()
                for kt in range(8):
                    S.op("pe", lambda e, kt=kt, p=p, j=j, iw=iw: e.matmul(
                        ps[p][:], lhsT=wp[iw][:, kt, j * 128:(j + 1) * 128], rhs=aT[:, kt, :],
                        start=(kt == 0), stop=(kt == 7)), r=["qT", ("wp", iw)], w=[("ps", p)])
                S.op("dve", lambda e, p=p, jt=jt: e.tensor_tensor(out=maTv[:, jt, :], in0=ps[p][:], in1=sga[:, jt, :],
                                                                  op=ALU.mult), r=[("ps", p), "sga"],
                     w=[("sgr", i) for i in range(4)])
        for blk in range(4):
            iw = load_w(wb, blk * 256)
            for j in range(2):
                jt = 2 * blk + j
                p = nps()
                for kt in range(8):
                    S.op("pe", lambda e, kt=kt, p=p, j=j, iw=iw: e.matmul(
                        ps[p][:], lhsT=wp[iw][:, kt, j * 128:(j + 1) * 128], rhs=rgT[:, kt, :],
                        start=(kt == 0), stop=(kt == 7)), r=["rgT", ("wp", iw)], w=[("ps", p)])
                si = jt % 2
                S.op("dve", lambda e, p=p, jt=jt, si=si: e.tensor_tensor(out=sg[si][:], in0=ps[p][:],
                                                                         in1=sgb[:, jt, :], op=ALU.mult),
                     r=[("ps", p), "sgb"], w=[("sg", si)])
                S.op("dve", lambda e, jt=jt, si=si: e.tensor_tensor(out=mixT[:, jt, :], in0=sg[si][:],
                                                                    in1=maTv[:, jt, :], op=ALU.add),
                     r=[("sg", si)] + [("sgr", i) for i in range(4)], w=["mixT"])
        for cb in range(4):
            iw = load_w(wo, cb * 256)
            for tt in range(4):
                p = nps()
                for kt in range(8):
                    S.op("pe", lambda e, kt=kt, p=p, tt=tt, iw=iw: e.matmul(
                        ps[p][:, 0:256], lhsT=mixT[:, kt, tt * 128:(tt + 1) * 128], rhs=wp[iw][:, kt, :],
                        start=(kt == 0), stop=(kt == 7)), r=["mixT", ("wp", iw)], w=[("ps", p)])
                S.op("dve", lambda e, p=p, cb=cb, tt=tt: e.tensor_tensor(
                    out=xt[:, tt, cb * 256:(cb + 1) * 256], in0=ps[p][:, 0:256],
                    in1=xt[:, tt, cb * 256:(cb + 1) * 256], op=ALU.add), r=[("ps", p), ("xt", tt)], w=[("xt", tt)])
        store_xt(x2s, g, "x2s")

    if stop_after == "E1":
        return finish(nc, S, es, y, None)

    barrier(MIX_KEYS + E1_KEYS, FFN_KEYS)
    rt_load(SL_FFN2, RT_FFN2)
    rt_load(SL_FINAL, RT_FINAL)
    ykeys = []
    for g in range(NG_OWN):
        load_xt(x2s, g, "x2s")
        ffn(SL_FFN2, w2g, w2u, w2d, load_wd=(g == 0))
        for c in range(4):
            S.op("act", lambda e, c=c: e.activation(out=junk[:], in_=xt[:, c, :], func=AF.Square,
                                                    accum_out=ss[:, c:c + 1]), r=[("xt", c)], w=[("ss", c)])
        allss = [("ss", c) for c in range(4)]
        allr = [("rstd", c) for c in range(4)]
        S.op("dve", lambda e: e.tensor_scalar(out=rstd[:, 0:4], in0=ss[:, 0:4], scalar1=1.0 / D, scalar2=EPS,
                                              op0=ALU.mult, op1=ALU.add), r=allss, w=allr)
        S.op("act", lambda e: e.activation(out=rstd[:, 0:4], in_=rstd[:, 0:4], func=AF.Sqrt), r=allr, w=allr)
        S.op("dve", lambda e: e.reciprocal(out=rstd[:, 0:4], in_=rstd[:, 0:4]), r=allr, w=allr)
        for c in range(4):
            S.op("dve", lambda e, c=c: e.scalar_tensor_tensor(out=xt[:, c, :], in0=xt[:, c, :],
                                                              scalar=rstd[:, c:c + 1], in1=rt[:, SL_FINAL, :],
                                                              op0=ALU.mult, op1=ALU.mult),
                 r=[("xt", c), ("rstd", c), ("rt", SL_FINAL)], w=[("xt", c)])
        store_xt(y, g, "y")
    return finish(nc, S, es, y, ykeys)


def finish(nc, S, es, y, ykeys):
    allw = set()
    for o in S.ops:
        if o["dma"]:
            allw.update(o["w"])
    S.op("sp", None, r=sorted(allw, key=str))
    S.emit()
    es.close()
    return nc


def _host_inputs(inputs):
    f = lambda a: np.ascontiguousarray(np.asarray(a, dtype=np.float32))
    x = f(inputs["x"])
    L = 0
    rep = lambda v: np.ascontiguousarray(np.broadcast_to(f(v).reshape(1, -1), (128, v.size)))
    b_in = f(inputs["b_in"])[L]
    col = lambda v: np.ascontiguousarray(f(v).reshape(8, 128).T)
    rowtab = np.stack([rep(f(inputs["ffn1_norm"])[L]), rep(f(inputs["mix_norm"])[L]),
                       rep(f(inputs["ffn2_norm"])[L]), rep(f(inputs["final_norm"])),
                       rep(f(inputs["sgu_norm_g"])[L]), rep(f(inputs["sgu_norm_b"])[L]),
                       rep(b_in[OFF_K:OFF_K + D]), rep(b_in[OFF_V:OFF_V + D]),
                       rep(b_in[OFF_GR:OFF_GR + D]), rep(b_in[OFF_VA:OFF_VA + D])], axis=0)
    ws = f(inputs["sgu_w_s"])[L]
    bs = f(inputs["sgu_b_s"])[L]
    logit = f(inputs["ret_decay_logit"])[L]
    p = np.arange(128, dtype=np.float32)
    consts = np.concatenate([p[:, None], p[:, None] + 1.0, (p[None, :] - p[:, None])], axis=1).astype(np.float32)
    ident = np.eye(128, dtype=np.float32)
    theta = (10000.0 ** (-np.arange(0, 256, 2, dtype=np.float32) / np.float32(256))).astype(np.float32)
    shared = dict(
        w1g=f(inputs["ffn1_w_gate"])[L], w1u=f(inputs["ffn1_w_up"])[L], w1d=f(inputs["ffn1_w_down"])[L],
        w2g=f(inputs["ffn2_w_gate"])[L], w2u=f(inputs["ffn2_w_up"])[L], w2d=f(inputs["ffn2_w_down"])[L],
        win=f(inputs["w_in"])[L], wa=f(inputs["w_branch_a"])[L], wb=f(inputs["w_branch_b"])[L],
        wo=f(inputs["w_out"])[L], rowtab=np.ascontiguousarray(rowtab), consts=consts, ident=ident)
    maps = []
    for core in range(8):
        b, half = core // 2, core % 2
        xs = x[b] if half == 0 else x[b, ::-1]
        pos = np.arange(SEQ, dtype=np.float32) if half == 0 else np.arange(SEQ - 1, -1, -1, dtype=np.float32)
        ang = (pos[:, None] * theta[None, :]).astype(np.float32)
        cs_tm = np.stack([np.cos(ang), np.sin(ang)]).astype(np.float32)
        cs_fm = np.ascontiguousarray(cs_tm[:, :HALF, :].transpose(0, 2, 1))
        if half == 0:
            ws_l, bs_l, lg_l = ws, bs, logit
        else:
            ws_l, bs_l, lg_l = ws[:, ::-1, ::-1], bs[:, ::-1], logit[::-1]
        wst = np.ascontiguousarray(ws_l.transpose(2, 0, 1).reshape(128, 512))
        coltab = np.concatenate([col(b_in[OFF_Q:OFF_Q + D]), col(b_in[OFF_K:OFF_K + D]), col(b_in[OFF_U:OFF_U + D]),
                                 col(b_in[OFF_GA:OFF_GA + D]), col(b_in[OFF_GB:OFF_GB + D]),
                                 np.broadcast_to(np.ascontiguousarray(lg_l).reshape(1, 8), (128, 8))], axis=1)
        m = dict(shared)
        m.update(xall=np.ascontiguousarray(xs), coltab=np.ascontiguousarray(coltab.astype(np.float32)), wst=wst,
                 bsrow=np.ascontiguousarray(bs_l.reshape(1, 512)), cs_fm=cs_fm, cs_tm=np.ascontiguousarray(cs_tm))
        maps.append(m)
    return maps


def kernel(**inputs):
    maps = _host_inputs(inputs)
    nc = build_program()
    res = run_bass_kernel_spmd(nc, maps, core_ids=list(range(8)))
    out = np.empty((4, SEQ, D), dtype=np.float32)
    for core in range(8):
        b, half = core // 2, core % 2
        yc = np.asarray(res.results[core]["y"], dtype=np.float32)
        if half == 0:
            out[b, :HALF] = yc
        else:
            out[b, HALF:] = yc[::-1]
    return out
```

```python
import os
import numpy as np
import ml_dtypes
from contextlib import ExitStack
import concourse.bass as bass
import concourse.mybir as mybir
from concourse.bass_utils import run_bass_kernel_spmd

F32 = mybir.dt.float32
BF16 = mybir.dt.bfloat16
AF = mybir.ActivationFunctionType
ALU = mybir.AluOpType

D = 1024
DFF = 2816
NFT = DFF // 128
SEQ = 4096
HALF = 2048
G = 512
NG_ALL = SEQ // G
NG_OWN = HALF // G
EPS = 1e-6
OFF_U, OFF_VA, OFF_Q, OFF_K, OFF_V, OFF_GR, OFF_GA, OFF_GB = [i * 1024 for i in range(8)]
RT_FFN1, RT_MIX, RT_FFN2, RT_FINAL, RT_SGUG, RT_SGUB, RT_BK, RT_BV, RT_BGR, RT_BVA = range(10)
NRT = 10
SL_FFN1, SL_MIX, SL_BK, SL_BV, SL_BGR, SL_BVA, SL_SGUG, SL_SGUB, SL_FFN2, SL_FINAL = 0, 1, 2, 3, 0, 1, 2, 3, 0, 1
CT_BQ, CT_BK, CT_BU, CT_BGA, CT_BGB, CT_LOGIT = 0, 8, 16, 24, 32, 40
NCT = 48


class Sched:
    def __init__(self, nc):
        self.nc = nc
        self.ops = []

    def op(self, eng, fn, r=(), w=()):
        self.ops.append(dict(eng=eng, fn=fn, r=tuple(r), w=tuple(w), dma=False, chan=None, signal=False))

    def dma(self, eng, fn, chan, r=(), w=()):
        self.ops.append(dict(eng=eng, fn=fn, r=tuple(r), w=tuple(w), dma=True, chan=("ch", chan), signal=True))

    def emit(self):
        nc = self.nc
        ops = self.ops
        last_w, readers = {}, {}
        for i, o in enumerate(ops):
            deps = set()
            for k in o["r"]:
                if k in last_w:
                    deps.add(last_w[k])
            for k in o["w"]:
                if k in last_w:
                    deps.add(last_w[k])
                deps.update(readers.get(k, ()))
            deps.discard(i)
            o["deps"] = deps
            for k in o["r"]:
                readers.setdefault(k, []).append(i)
            for k in o["w"]:
                last_w[k] = i
                readers[k] = []
        for i, o in enumerate(ops):
            keep = set()
            for j in o["deps"]:
                p = ops[j]
                if (not p["dma"]) and (not o["dma"]) and p["eng"] == o["eng"] == "pe":
                    continue
                if not p["dma"]:
                    p["signal"] = True
                keep.add(j)
            o["deps"] = keep
        counts = {}
        for o in ops:
            key = o["chan"] if o["dma"] else ("eng", o["eng"])
            o["key"] = key
            if o["signal"] and o["fn"] is not None:
                counts[key] = counts.get(key, 0) + (16 if o["dma"] else 1)
                o["sigval"] = counts[key]
        keys = sorted(counts.keys(), key=str)
        with ExitStack() as es:
            sems = {k: es.enter_context(nc.semaphore("s%d" % n)) for n, k in enumerate(keys)}
            block = es.enter_context(nc.Block())
            engmap = dict(pe=block.tensor, act=block.scalar, dve=block.vector, pool=block.gpsimd, sp=block.sync)

            def make(engname):
                def body(eng):
                    waited = {}
                    for o in ops:
                        if o["eng"] != engname:
                            continue
                        need = {}
                        for j in o["deps"]:
                            p = ops[j]
                            need[p["key"]] = max(need.get(p["key"], 0), p["sigval"])
                        for k, v in sorted(need.items(), key=str):
                            if waited.get(k, 0) < v:
                                eng.wait_ge(sems[k], v)
                                waited[k] = v
                        if o["fn"] is None:
                            continue
                        ins = o["fn"](eng)
                        if o["signal"]:
                            ins.then_inc(sems[o["key"]], 16 if o["dma"] else 1)
                return body

            for name, deco in engmap.items():
                deco(make(name))


def build_program(debug=False, stop_after=None):
    nc = bass.Bass("TRN2", target_bir_lowering=False)
    ein = lambda name, shape, dt=F32: nc.dram_tensor(name, list(shape), dt, kind="ExternalInput").ap()
    xall = ein("xall", [SEQ, D])
    w1g, w1u, w1d = ein("w1g", [D, DFF]), ein("w1u", [D, DFF]), ein("w1d", [DFF, D])
    w2g, w2u, w2d = ein("w2g", [D, DFF]), ein("w2u", [D, DFF]), ein("w2d", [DFF, D])
    win = ein("win", [D, 8 * D])
    wa, wb, wo = ein("wa", [D, D]), ein("wb", [D, D]), ein("wo", [D, D])
    rowtab = ein("rowtab", [NRT, 128, D])
    coltab = ein("coltab", [128, NCT])
    wst = ein("wst", [128, 4 * 128])
    bsrow = ein("bsrow", [1, 4 * 128])
    ident_in = ein("ident", [128, 128])
    consts = ein("consts", [128, 2 + 128])
    cs_fm = ein("cs_fm", [2, 128, HALF])
    cs_tm = ein("cs_tm", [2, SEQ, 128])
    y = nc.dram_tensor("y", [HALF, D], F32, kind="ExternalOutput").ap()

    skind = "ExternalOutput" if debug else "Internal"
    scr = lambda name, shape, dt: nc.dram_tensor(name, list(shape), dt, kind=skind).ap()
    x1s = scr("x1s", [HALF, D], F32)
    h2Ts = scr("h2Ts", [NG_ALL, 128, 8 * G], BF16)
    rps = scr("rps", [HALF, D], F32)
    qTs = scr("qTs", [NG_OWN, 128, 8 * G], BF16)
    ktms = scr("ktms", [HALF, D], BF16)
    vtms = scr("vtms", [HALF, D], BF16)
    rgTs = scr("rgTs", [NG_OWN, 128, 8 * G], BF16)
    x2s = scr("x2s", [HALF, D], F32)
    dbg_state = scr("dbg_state", [128, 8 * 256], F32) if debug else None
    dbg_ct = scr("dbg_ct", [128, NCT], F32) if debug else None
    dbg_lg = scr("dbg_lg", [128, 8], F32) if debug else None
    dbg_dec = scr("dbg_dec", [128, 24], F32) if debug else None
    dbg_MT = scr("dbg_MT", [128, 512], F32) if debug else None
    dbg_kT = scr("dbg_kT", [128, 8 * G], BF16) if debug else None

    S = Sched(nc)
    es = ExitStack()
    sb = lambda name, shape, dt=F32: es.enter_context(nc.sbuf_tensor(name, list(shape), dt))
    rt = sb("rt", [128, 4, D])
    ct = sb("ct", [128, NCT])
    cst = sb("cst", [128, 130])
    ident = sb("identb", [128, 128], BF16)
    wsT = sb("wsT", [128, 512], BF16)
    bsr = sb("bsr", [1, 512], BF16)
    ones = sb("ones", [1, 128], BF16)
    ones33 = sb("ones33", [33, 128], BF16)
    brow = sb("brow", [33, 2, D], BF16)
    lg = sb("lg", [128, 8])
    dec = sb("dec", [128, 24])
    MT = sb("MT", [128, 4, 128])
    mtmp = sb("mtmp", [128, 2, 128])
    Sst = sb("Sst", [128, 8, 256])
    Tst = sb("Tst", [128, 8, 256])
    Stb = sb("Stb", [128, 8, 256], BF16)
    ss = sb("ss", [128, 32])
    rstd = sb("rstd", [128, 32])
    xt = sb("xt", [128, 4, D])
    hb = sb("hb", [128, D], BF16)
    hb2 = sb("hb2", [128, D], BF16)
    junk = sb("junk", [128, D], BF16)
    h2T = sb("h2T", [128, 8, G], BF16)
    wp = [sb("wp%d" % i, [128, 8, 256], BF16) for i in range(4)]
    ps = [es.enter_context(nc.psum_tensor("ps%d" % i, [128, 512], F32)) for i in range(6)]
    pst = [es.enter_context(nc.psum_tensor("pst%d" % i, [128, 1024], BF16)) for i in range(2)]
    cnt = dict(ps=0, pst=0, wp=0, nwp=4)

    def nps():
        i = cnt["ps"] % 6
        cnt["ps"] += 1
        return i

    def npst():
        i = cnt["pst"] % 2
        cnt["pst"] += 1
        return i

    def nwp():
        i = cnt["wp"] % cnt["nwp"]
        cnt["wp"] += 1
        return i

    def ts_ap(e, out, in0, sc, op0):
        return e.tensor_scalar(out=out, in0=in0, scalar1=sc, scalar2=None, op0=op0)

    S.dma("sp", lambda e: e.dma_start(out=ct[:], in_=coltab[:, :]), "ct", w=["ct"])
    S.dma("sp", lambda e: e.dma_start(out=cst[:], in_=consts[:, :]), "cst", w=["cst"])
    S.dma("pool", lambda e: e.dma_start(out=ident[:], in_=ident_in[:, :]), "ident", w=["ident"])
    S.dma("pool", lambda e: e.dma_start(out=wsT[:], in_=wst[:, :]), "wsT", w=["wsT"])
    S.dma("pool", lambda e: e.dma_start(out=bsr[:], in_=bsrow[:, :]), "bsr", w=["bsr"])
    S.op("dve", lambda e: e.memset(ones[:], 1.0), w=["ones"])
    S.op("dve", lambda e: e.memset(ones33[:], 1.0), w=["ones33"])
    for (bp, bi, ridx) in ((0, 0, RT_BK), (0, 1, RT_BV), (32, 0, RT_BGR), (32, 1, RT_BVA)):
        S.dma("pool", lambda e, bp=bp, bi=bi, ridx=ridx: e.dma_start(out=brow[bp:bp + 1, bi, :],
                                                                      in_=rowtab[ridx][0:1, :]),
              "brow%d_%d" % (bp, bi), w=[("brow", bp, bi)])
    S.op("dve", lambda e: e.memset(Sst[:], 0.0), w=[("Sst", h) for h in range(4)])
    S.op("dve", lambda e: e.memset(Tst[:], 0.0), w=[("Tst", h) for h in range(4)])
    S.op("dve", lambda e: e.memset(Stb[:], 0.0), w=["Stb"])
    S.op("dve", lambda e: e.tensor_scalar(out=lg[:], in0=ct[:, CT_LOGIT:CT_LOGIT + 8], scalar1=-1.0, scalar2=0.0,
                                          op0=ALU.mult, op1=ALU.add), r=["ct"], w=["lg"])
    S.op("act", lambda e: e.activation(out=lg[:], in_=lg[:], func=AF.Exp), r=["lg"], w=["lg"])
    S.op("dve", lambda e: e.tensor_scalar(out=lg[:], in0=lg[:], scalar1=1.0, scalar2=0.0, op0=ALU.add, op1=ALU.add),
         r=["lg"], w=["lg"])
    S.op("act", lambda e: e.activation(out=lg[:], in_=lg[:], func=AF.Ln), r=["lg"], w=["lg"])
    S.op("dve", lambda e: e.tensor_scalar(out=lg[:], in0=lg[:], scalar1=-1.0, scalar2=0.0, op0=ALU.mult, op1=ALU.add),
         r=["lg"], w=["lg"])
    S.op("dve", lambda e: ts_ap(e, dec[:, 0:4], lg[:, 0:4], cst[:, 1:2], ALU.mult), r=["lg", "cst"], w=["dec"])
    S.op("dve", lambda e: e.tensor_scalar(out=mtmp[:, 0, 0:1], in0=cst[:, 0:1], scalar1=-1.0, scalar2=127.0,
                                          op0=ALU.mult, op1=ALU.add), r=["cst"], w=["mtmp"])
    S.op("dve", lambda e: ts_ap(e, dec[:, 4:8], lg[:, 0:4], mtmp[:, 0, 0:1], ALU.mult), r=["lg", "mtmp"], w=["dec"])
    S.op("dve", lambda e: e.tensor_scalar(out=dec[:, 8:12], in0=lg[:, 0:4], scalar1=128.0, scalar2=0.0,
                                          op0=ALU.mult, op1=ALU.add), r=["lg"], w=["dec"])
    S.op("dve", lambda e: e.tensor_scalar(out=mtmp[:, 0, 1:2], in0=cst[:, 0:1], scalar1=-1.0, scalar2=128.0,
                                          op0=ALU.mult, op1=ALU.add), r=["cst", "mtmp"], w=["mtmp"])
    S.op("dve", lambda e: ts_ap(e, dec[:, 12:16], lg[:, 4:8], mtmp[:, 0, 1:2], ALU.mult), r=["lg", "mtmp"], w=["dec"])
    S.op("dve", lambda e: ts_ap(e, dec[:, 16:20], lg[:, 4:8], cst[:, 0:1], ALU.mult), r=["lg", "cst"], w=["dec"])
    S.op("dve", lambda e: e.tensor_scalar(out=dec[:, 20:24], in0=lg[:, 4:8], scalar1=128.0, scalar2=0.0,
                                          op0=ALU.mult, op1=ALU.add), r=["lg"], w=["dec"])
    S.op("act", lambda e: e.activation(out=dec[:], in_=dec[:], func=AF.Exp), r=["dec"], w=["dec"])
    S.op("dve", lambda e: e.tensor_scalar(out=dec[:, 4:8], in0=dec[:, 4:8], scalar1=0.0625, scalar2=0.0,
                                          op0=ALU.mult, op1=ALU.add), r=["dec"], w=["dec"])
    S.op("dve", lambda e: e.tensor_scalar(out=dec[:, 16:20], in0=dec[:, 16:20], scalar1=0.0625, scalar2=0.0,
                                          op0=ALU.mult, op1=ALU.add), r=["dec"], w=["dec"])
    QD1, KD1, G1, QD2, KD2, G2 = 0, 4, 8, 12, 16, 20
    S.op("dve", lambda e: e.tensor_scalar(out=mtmp[:, 0, :], in0=cst[:, 2:130], scalar1=0.0, scalar2=0.0,
                                          op0=ALU.max, op1=ALU.add), r=["cst", "mtmp"], w=["mtmp"])
    S.op("dve", lambda e: e.tensor_scalar(out=mtmp[:, 1, :], in0=cst[:, 2:130], scalar1=-1.0, scalar2=0.0,
                                          op0=ALU.mult, op1=ALU.max), r=["cst", "mtmp"], w=["mtmp"])
    for h in range(4):
        S.op("dve", lambda e, h=h: ts_ap(e, MT[:, h, :], mtmp[:, 0, :], lg[:, h:h + 1], ALU.mult), r=["mtmp", "lg"], w=["MT"])
        S.op("dve", lambda e, h=h: e.scalar_tensor_tensor(out=MT[:, h, :], in0=mtmp[:, 1, :],
                                                          scalar=lg[:, 4 + h:5 + h], in1=MT[:, h, :],
                                                          op0=ALU.mult, op1=ALU.add), r=["mtmp", "lg", "MT"], w=["MT"])
    S.op("act", lambda e: e.activation(out=MT[:], in_=MT[:], func=AF.Exp), r=["MT"], w=["MT"])
    S.op("dve", lambda e: e.tensor_scalar(out=MT[:], in0=MT[:], scalar1=0.0625, scalar2=0.0, op0=ALU.mult, op1=ALU.add),
         r=["MT"], w=["MT"])

    if debug:
        S.dma("sp", lambda e: e.dma_start(out=dbg_ct[:, :], in_=ct[:]), "dbg1", r=["ct"], w=["dbg1"])
        S.dma("sp", lambda e: e.dma_start(out=dbg_lg[:, :], in_=lg[:]), "dbg2", r=["lg"], w=["dbg2"])
        S.dma("sp", lambda e: e.dma_start(out=dbg_dec[:, :], in_=dec[:]), "dbg3", r=["dec"], w=["dbg3"])
        S.dma("sp", lambda e: e.dma_start(out=dbg_MT[:, :], in_=MT[:].rearrange("p a b -> p (a b)")), "dbg4",
              r=["MT"], w=["dbg4"])
    def rt_load(slot, idx):
        S.dma("sp", lambda e: e.dma_start(out=rt[:, slot, :], in_=rowtab[idx]), "rt%d" % slot, w=[("rt", slot)])

    arena = sb("arena", [128, 21504])
    dummy = sb("bdummy", [128, 8])

    def carve(off, shape, dt=F32):
        nb = int(np.prod(shape)) * (4 if dt == F32 else 2)
        ap = arena[:, off // 4:(off + nb) // 4]
        if dt == BF16:
            ap = ap.bitcast(BF16)
        if len(shape) == 2:
            ap = ap.rearrange("p (a b) -> p a b", a=shape[0])
        return ap

    FFN_KEYS = ["hT", ("wp", 4), ("wp", 5)] + [("tT", i) for i in range(NFT)] + \
        [("wd", h, b) for h in range(2) for b in range(NFT // 2)]
    CTMP_KEYS = ["qtmp", "fA", "fB", "csfm", "rp", "ri", ("PT", 0), ("PT", 1)]
    E1_KEYS = ["sga", "sgb", "mixT"]
    CRS_KEYS = [("crs", c) for c in range(4)]
    MIX_KEYS = ["qT", "kT", "rpg", "rgT", "rr"] + [("sgr", i) for i in range(4)] + CTMP_KEYS

    def barrier(rk, wk):
        S.op("dve", lambda e: e.memset(dummy[:], 0.0), w=list(rk) + list(wk))

    def load_w(W, col0, ncols=256, row_tiles=8):
        i = nwp()
        S.dma("pool", lambda e: e.dma_start(
            out=wp[i][:, 0:row_tiles, 0:ncols],
            in_=W[0:row_tiles * 128, col0:col0 + ncols].rearrange("(kt p) c -> p kt c", p=128)),
            "wp%d" % i, w=[("wp", i)])
        return i

    def norm_group(rtidx, dstT, dstkey):
        for c in range(4):
            S.op("act", lambda e, c=c: e.activation(out=junk[:], in_=xt[:, c, :], func=AF.Square,
                                                    accum_out=ss[:, c:c + 1]), r=[("xt", c)], w=[("ss", c)])
        allss = [("ss", c) for c in range(4)]
        allr = [("rstd", c) for c in range(4)]
        S.op("dve", lambda e: e.tensor_scalar(out=rstd[:, 0:4], in0=ss[:, 0:4], scalar1=1.0 / D, scalar2=EPS,
                                              op0=ALU.mult, op1=ALU.add), r=allss, w=allr)
        S.op("act", lambda e: e.activation(out=rstd[:, 0:4], in_=rstd[:, 0:4], func=AF.Sqrt), r=allr, w=allr)
        S.op("dve", lambda e: e.reciprocal(out=rstd[:, 0:4], in_=rstd[:, 0:4]), r=allr, w=allr)
        for c in range(4):
            hbuf, hkey = (hb, "hb") if c % 2 == 0 else (hb2, "hb2")
            S.op("dve", lambda e, c=c, hbuf=hbuf: e.scalar_tensor_tensor(
                out=hbuf[:], in0=xt[:, c, :], scalar=rstd[:, c:c + 1], in1=rt[:, rtidx, :], op0=ALU.mult,
                op1=ALU.mult), r=[("xt", c), ("rstd", c), ("rt", rtidx)], w=[hkey])
            transpose_into(hbuf, dstT, c, hkey, dstkey)

    def transpose_into(srcb, dstT, c, srckey, dstkey):
        p = npst()
        for kt in range(8):
            S.op("pe", lambda e, kt=kt: e.transpose(out=pst[p][:, kt * 128:(kt + 1) * 128],
                                                    in_=srcb[:, kt * 128:(kt + 1) * 128], identity=ident[:]),
                 r=[srckey, "ident"], w=[("pst", p)])
        S.op("act", lambda e: e.copy(out=dstT[:, :, c * 128:(c + 1) * 128],
                                     in_=pst[p][:].rearrange("p (k t) -> p k t", k=8)),
             r=[("pst", p)], w=[dstkey])

    hT = carve(0, [8, G], BF16)
    wp.append(carve(75776, [8, 256], BF16))
    wp.append(carve(79872, [8, 256], BF16))
    tT = carve(8192, [NFT, G], BF16)
    wd = [carve(30720 + i * 22528, [NFT, 512], BF16) for i in range(2)]
    sg = [sb("sg%d" % i, [128, 512]) for i in range(2)]
    gtmp = sb("gtmp", [128, 256])

    def ffn(rtidx, Wg, Wu, Wd, load_wd=True):
        cnt["nwp"] = 6
        norm_group(rtidx, hT, "hT")
        for blk in range(NFT // 2):
            ig = load_w(Wg, blk * 256)
            iu = load_w(Wu, blk * 256)
            for j in range(2):
                ft = blk * 2 + j
                pg, pu = nps(), nps()
                for kt in range(8):
                    S.op("pe", lambda e, kt=kt, pg=pg, ig=ig, j=j: e.matmul(
                        ps[pg][:], lhsT=wp[ig][:, kt, j * 128:(j + 1) * 128], rhs=hT[:, kt, :],
                        start=(kt == 0), stop=(kt == 7)), r=[("wp", ig), "hT"], w=[("ps", pg)])
                for kt in range(8):
                    S.op("pe", lambda e, kt=kt, pu=pu, iu=iu, j=j: e.matmul(
                        ps[pu][:], lhsT=wp[iu][:, kt, j * 128:(j + 1) * 128], rhs=hT[:, kt, :],
                        start=(kt == 0), stop=(kt == 7)), r=[("wp", iu), "hT"], w=[("ps", pu)])
                si = ft % 2
                if j == 0 and load_wd:
                    for half in range(2):
                        S.dma("pool", lambda e, half=half, blk=blk: e.dma_start(
                            out=wd[half][:, 2 * blk:2 * blk + 2, :],
                            in_=Wd[blk * 256:(blk + 1) * 256, half * 512:(half + 1) * 512].rearrange(
                                "(ft p) c -> p ft c", p=128)),
                            "wd%d_%d" % (half, blk), w=[("wd", half, blk)])
                S.op("act", lambda e, pg=pg, si=si: e.activation(out=sg[si][:], in_=ps[pg][:], func=AF.Silu),
                     r=[("ps", pg)], w=[("sg", si)])
                S.op("dve", lambda e, pu=pu, si=si, ft=ft: e.tensor_tensor(out=tT[:, ft, :], in0=sg[si][:],
                                                                           in1=ps[pu][:], op=ALU.mult),
                     r=[("ps", pu), ("sg", si)], w=[("tT", ft)])
        for tt in range(4):
            for half in range(2):
                p = nps()
                for ft in range(NFT):
                    S.op("pe", lambda e, ft=ft, p=p, tt=tt, half=half: e.matmul(
                        ps[p][:], lhsT=tT[:, ft, tt * 128:(tt + 1) * 128], rhs=wd[half][:, ft, :],
                        start=(ft == 0), stop=(ft == NFT - 1)), r=[("tT", ft), ("wd", half, ft // 2)], w=[("ps", p)])
                S.op("dve", lambda e, p=p, tt=tt, half=half: e.scalar_tensor_tensor(
                    out=xt[:, tt, half * 512:(half + 1) * 512], in0=ps[p][:], scalar=0.5,
                    in1=xt[:, tt, half * 512:(half + 1) * 512], op0=ALU.mult, op1=ALU.add),
                    r=[("ps", p), ("xt", tt)], w=[("xt", tt)])
        cnt["nwp"] = 4

    def load_xt_c(src, g, srckey, c):
        S.dma("sp", lambda e: e.dma_start(out=xt[:, c, :], in_=src[g * G + c * 128:g * G + (c + 1) * 128, :]),
              "xt%d" % c, r=([(srckey, g, c)] if srckey else []), w=[("xt", c)])

    def store_xt_c(dst, g, dstkey, c):
        S.dma("sp", lambda e: e.dma_start(out=dst[g * G + c * 128:g * G + (c + 1) * 128, :], in_=xt[:, c, :]),
              "xt_st%d" % c, r=[("xt", c)], w=[(dstkey, g, c)])

    def load_xt(src, g, srckey=None):
        for c in range(4):
            load_xt_c(src, g, srckey, c)

    def store_xt(dst, g, dstkey):
        for c in range(4):
            store_xt_c(dst, g, dstkey, c)

    def rows(ap, g):
        return ap[g * G:(g + 1) * G, :].rearrange("(c p) d -> p c d", p=128)

    def own_rows(ap, g):
        return rows(ap, g)

    rt_load(SL_FFN1, RT_FFN1)
    rt_load(SL_MIX, RT_MIX)
    A_ORDER = [7, 6, 5, 4, 0, 1, 2, 3]
    load_xt(xall, A_ORDER[0])
    for gi, g in enumerate(A_ORDER):
        ffn(SL_FFN1, w1g, w1u, w1d, load_wd=(g == 7))
        if g < NG_OWN:
            store_xt(x1s, g, "x1s")
        norm_group(SL_MIX, h2T, "h2T")
        if gi + 1 < len(A_ORDER):
            load_xt(xall, A_ORDER[gi + 1])
        S.dma("sp", lambda e, g=g: e.dma_start(out=h2Ts[g], in_=h2T[:].rearrange("p k t -> p (k t)")), "h2T_st",
              r=["h2T"], w=[("h2Ts", g)])

    if stop_after == "A":
        return finish(nc, S, es, y, None)

    ktm = sb("ktm", [128, 4, D], BF16)
    vtm = sb("vtm", [128, 4, D], BF16)
    Vd = sb("Vd", [128, D], BF16)
    cstm = sb("cstm", [128, 2, 4, 128])

    def load_h2T(g):
        S.dma("sp", lambda e: e.dma_start(out=h2T[:].rearrange("p k t -> p (k t)"), in_=h2Ts[g]), "h2T_ld",
              r=[("h2Ts", g)], w=["h2T"])

    def proj_tm(off, bp, bi, consume):
        for cbp in range(2):
            iws = [load_w(win, off + (2 * cbp + hf) * 256) for hf in range(2)]
            for tt in range(4):
                p = nps()
                for hf in range(2):
                    iw = iws[hf]
                    cb = 2 * cbp + hf
                    for kt in range(8):
                        S.op("pe", lambda e, kt=kt, p=p, tt=tt, iw=iw, hf=hf: e.matmul(
                            ps[p][:, hf * 256:(hf + 1) * 256], lhsT=h2T[:, kt, tt * 128:(tt + 1) * 128],
                            rhs=wp[iw][:, kt, :], start=(kt == 0), stop=False), r=["h2T", ("wp", iw)], w=[("ps", p)])
                    S.op("pe", lambda e, p=p, cb=cb, hf=hf: e.matmul(
                        ps[p][:, hf * 256:(hf + 1) * 256], lhsT=ones33[bp:bp + 1, :],
                        rhs=brow[bp:bp + 1, bi, cb * 256:(cb + 1) * 256], start=False, stop=True),
                        r=["ones33", ("brow", bp, bi)], w=[("ps", p)])
                consume(p, cbp, tt)

    rA4 = sg[0][:].rearrange("p (a b) -> p a b", a=4)
    rB4 = sg[1][:].rearrange("p (a b) -> p a b", a=4)

    def kv_tm(g):
        for s2 in range(2):
            S.dma("sp", lambda e, s2=s2: e.dma_start(
                out=cstm[:, s2, :, :], in_=cs_tm[s2, g * G:(g + 1) * G, :].rearrange("(c p) f -> p c f", p=128)),
                "cstm%d" % s2, w=[("cstm", s2)])

        def k_consume(p, cbp, tt):
            pv = ps[p][:].rearrange("p (a b) -> p a b", a=4)
            S.op("dve", lambda e: e.tensor_tensor(
                out=rA4, in0=pv, in1=cstm[:, 0, tt, :].unsqueeze(1).to_broadcast([128, 4, 128]),
                op=ALU.mult), r=[("ps", p), ("cstm", 0)], w=[("sg", 0)])
            S.op("dve", lambda e: e.tensor_tensor(
                out=rB4, in0=pv, in1=cstm[:, 1, tt, :].unsqueeze(1).to_broadcast([128, 4, 128]),
                op=ALU.mult), r=[("ps", p), ("cstm", 1)], w=[("sg", 1)])
            kv4 = ktm[:, tt, cbp * 512:(cbp + 1) * 512].rearrange("p (c t f) -> p c t f", c=2, t=2)
            a4 = rA4.rearrange("p (c t) f -> p c t f", c=2)
            b4 = rB4.rearrange("p (c t) f -> p c t f", c=2)
            S.op("dve", lambda e: e.tensor_tensor(out=kv4[:, :, 0, :], in0=a4[:, :, 0, :], in1=b4[:, :, 1, :],
                                                  op=ALU.subtract), r=[("sg", 0), ("sg", 1)], w=[("ktm", tt)])
            S.op("dve", lambda e: e.tensor_tensor(out=kv4[:, :, 1, :], in0=a4[:, :, 1, :], in1=b4[:, :, 0, :],
                                                  op=ALU.add), r=[("sg", 0), ("sg", 1)], w=[("ktm", tt)])

        def v_consume(p, cbp, tt):
            S.op("act", lambda e: e.copy(out=vtm[:, tt, cbp * 512:(cbp + 1) * 512], in_=ps[p][:]),
                 r=[("ps", p)], w=[("vtm", tt)])

        proj_tm(OFF_K, 0, 0, k_consume)
        proj_tm(OFF_V, 0, 1, v_consume)

    def state_mm(c, kd, vd_on_dve=False):
        if vd_on_dve:
            S.op("dve", lambda e: e.tensor_tensor(
                out=Vd[:].rearrange("p (h e) -> p h e", h=4), in0=vtm[:, c, :].rearrange("p (h e) -> p h e", h=4),
                in1=dec[:, kd:kd + 4].unsqueeze(2).to_broadcast([128, 4, 256]), op=ALU.mult),
                r=[("vtm", c), "dec"], w=["Vd"])
        else:
            for h in range(4):
                S.op("act", lambda e, h=h: e.activation(
                    out=Vd[:, h * 256:(h + 1) * 256], in_=vtm[:, c, h * 256:(h + 1) * 256], func=AF.Copy,
                    scale=dec[:, kd + h:kd + h + 1]), r=[("vtm", c), "dec"], w=["Vd"])
        pids = []
        for h in range(4):
            p = nps()
            pids.append(p)
            for dt in range(2):
                S.op("pe", lambda e, h=h, dt=dt, p=p: e.matmul(
                    ps[p][:, dt * 256:(dt + 1) * 256], lhsT=ktm[:, c, h * 256 + dt * 128:h * 256 + (dt + 1) * 128],
                    rhs=Vd[:, h * 256:(h + 1) * 256], start=True, stop=True),
                    r=[("ktm", c), "Vd"], w=[("ps", p)])
        return pids

    def state_acc(Sin, inkey, Sout, outkey, gd, pids):
        for h in range(4):
            p = pids[h]
            S.op("dve", lambda e, h=h, p=p: e.scalar_tensor_tensor(
                out=Sout[:, 2 * h:2 * h + 2, :], in0=Sin[:, 2 * h:2 * h + 2, :], scalar=dec[:, gd + h:gd + h + 1],
                in1=ps[p][:].rearrange("p (a b) -> p a b", a=2), op0=ALU.mult, op1=ALU.add),
                r=[(inkey, h), ("ps", p), "dec"], w=[(outkey, h)])

    def state_update(St, stkey, c, kd, gd):
        state_acc(St, stkey, St, stkey, gd, state_mm(c, kd))

    for g in [7, 6, 5, 4]:
        load_h2T(g)
        kv_tm(g)
        for c in [3, 2, 1, 0]:
            state_update(Tst, "Tst", c, KD2, G2)

    if debug:
        S.dma("sp", lambda e: e.dma_start(out=dbg_state[:, :], in_=Tst[:].rearrange("p a b -> p (a b)")), "dbg",
              r=[("Tst", h) for h in range(4)], w=["dbgs"])
    if stop_after == "B":
        return finish(nc, S, es, y, None)

    barrier(FFN_KEYS, MIX_KEYS)
    qT = carve(0, [8, G], BF16)
    kT = carve(8192, [8, G], BF16)
    rpg = carve(16384, [4, D])
    sgr = carve(32768, [4, D])
    rgT = carve(49152, [8, G], BF16)
    rr = carve(57344, [1, D])[:, 0, :]
    qtmp = carve(61440, [2, G])
    fA = carve(65536, [2, G])
    fB = carve(69632, [2, G])
    csfm = carve(73728, [2, G])
    rp = carve(77824, [1, D])[:, 0, :]
    ri = carve(81920, [1, 256])[:, 0, :]
    PT = [carve(82944 + 256 * i, [1, 128], BF16)[:, 0, :] for i in range(2)]
    crs = carve(61440, [4, D])
    sga = carve(61440, [8, G], BF16)
    sgb = carve(69632, [8, G], BF16)
    mixT = carve(77824, [8, G], BF16)

    def proj_fm_rot(off, ctoff, dst, dstkey):
        for blk in range(4):
            iw = load_w(win, off + blk * 256)
            for j in range(2):
                jt = 2 * blk + j
                p = nps()
                for kt in range(8):
                    S.op("pe", lambda e, kt=kt, p=p, j=j, iw=iw: e.matmul(
                        ps[p][:], lhsT=wp[iw][:, kt, j * 128:(j + 1) * 128], rhs=h2T[:, kt, :],
                        start=(kt == 0), stop=(kt == 7)), r=["h2T", ("wp", iw)], w=[("ps", p)])
                S.op("dve", lambda e, p=p, j=j, jt=jt: ts_ap(e, qtmp[:, j, :], ps[p][:], ct[:, ctoff + jt:ctoff + jt + 1], ALU.add), r=[("ps", p), "ct"], w=["qtmp"])
            S.op("dve", lambda e: e.tensor_tensor(
                out=fA[:], in0=qtmp[:], in1=csfm[:, 0, :].unsqueeze(1).to_broadcast([128, 2, G]), op=ALU.mult),
                r=["qtmp", "csfm"], w=["fA"])
            S.op("dve", lambda e: e.tensor_tensor(
                out=fB[:], in0=qtmp[:], in1=csfm[:, 1, :].unsqueeze(1).to_broadcast([128, 2, G]), op=ALU.mult),
                r=["qtmp", "csfm"], w=["fB"])
            S.op("dve", lambda e, blk=blk: e.tensor_tensor(out=dst[:, 2 * blk, :], in0=fA[:, 0, :], in1=fB[:, 1, :],
                                                           op=ALU.subtract), r=["fA", "fB"], w=[dstkey])
            S.op("dve", lambda e, blk=blk: e.tensor_tensor(out=dst[:, 2 * blk + 1, :], in0=fA[:, 1, :],
                                                           in1=fB[:, 0, :], op=ALU.add), r=["fA", "fB"], w=[dstkey])

    for g in range(NG_OWN):
        load_h2T(g)
        kv_tm(g)
        S.dma("sp", lambda e, g=g: e.dma_start(out=rows(ktms, g), in_=ktm[:]), "ktm_st",
              r=[("ktm", i) for i in range(4)], w=[("ktms", g)])
        S.dma("sp", lambda e, g=g: e.dma_start(out=rows(vtms, g), in_=vtm[:]), "vtm_st",
              r=[("vtm", i) for i in range(4)], w=[("vtms", g)])
        S.dma("sp", lambda e, g=g: e.dma_start(
            out=csfm[:], in_=cs_fm[:, :, g * G:(g + 1) * G].rearrange("s p t -> p s t")), "csfm", w=["csfm"])
        proj_fm_rot(OFF_Q, CT_BQ, qT, "qT")
        proj_fm_rot(OFF_K, CT_BK, kT, "kT")
        if debug and g == 0:
            S.dma("sp", lambda e: e.dma_start(out=dbg_kT[:, :], in_=kT[:].rearrange("p k t -> p (k t)")), "dbg5",
                  r=["kT"], w=["dbg5"])
        S.dma("sp", lambda e, g=g: e.dma_start(out=qTs[g], in_=qT[:].rearrange("p k t -> p (k t)")), "qT_st",
              r=["qT"], w=[("qTs", g)])
        for c in range(4):
            cs = slice(c * 128, (c + 1) * 128)
            state_update(Sst, "Sst", c, KD1, G1)
            for h in range(4):
                p1 = nps()
                for dt in range(2):
                    S.op("pe", lambda e, h=h, dt=dt, p1=p1, cs=cs: e.matmul(
                        ps[p1][:, 0:128], lhsT=kT[:, 2 * h + dt, cs], rhs=qT[:, 2 * h + dt, cs],
                        start=(dt == 0), stop=(dt == 1)), r=["kT", "qT"], w=[("ps", p1)])
                pi = h % 2
                S.op("dve", lambda e, h=h, p1=p1, pi=pi: e.tensor_tensor(out=PT[pi][:], in0=ps[p1][:, 0:128],
                                                                         in1=MT[:, h, :], op=ALU.mult),
                     r=[("ps", p1), "MT"], w=[("PT", pi)])
                p2 = nps()
                S.op("pe", lambda e, h=h, p2=p2, pi=pi, c=c: e.matmul(
                    ps[p2][:, 0:256], lhsT=PT[pi][:], rhs=vtm[:, c, h * 256:(h + 1) * 256], start=True, stop=True),
                    r=[("PT", pi), ("vtm", c)], w=[("ps", p2)])
                for dt in range(2):
                    S.op("pe", lambda e, h=h, dt=dt, p2=p2, cs=cs: e.matmul(
                        ps[p2][:, 256:512], lhsT=qT[:, 2 * h + dt, cs], rhs=Stb[:, 2 * h + dt, :],
                        start=(dt == 0), stop=(dt == 1)), r=["qT", "Stb"], w=[("ps", p2)])
                S.op("act", lambda e, p2=p2: e.copy(out=ri[:], in_=ps[p2][:, 0:256]), r=[("ps", p2)], w=["ri"])
                S.op("dve", lambda e, h=h, p2=p2: e.scalar_tensor_tensor(
                    out=rp[:, h * 256:(h + 1) * 256], in0=ps[p2][:, 256:512], scalar=dec[:, QD1 + h:QD1 + h + 1],
                    in1=ri[:], op0=ALU.mult, op1=ALU.add), r=[("ps", p2), "ri", "dec"], w=["rp"])
            S.dma("sp", lambda e, g=g, c=c: e.dma_start(out=rps[g * G + c * 128:g * G + (c + 1) * 128, :], in_=rp[:]),
                  "rp_st", r=["rp"], w=[("rps", g, c)])
            S.op("act", lambda e: e.copy(out=Stb[:], in_=Sst[:]), r=[("Sst", h) for h in range(4)], w=["Stb"])

    if stop_after == "C":
        return finish(nc, S, es, y, None)

    Tbufs = [(Tst, "Tst"), (Sst, "Sst")]
    barrier(CTMP_KEYS, CRS_KEYS)
    def d_loads(g):
        load_h2T(g)
        S.dma("sp", lambda e: e.dma_start(out=qT[:].rearrange("p k t -> p (k t)"), in_=qTs[g]), "qT_ld",
              r=[("qTs", g)], w=["qT"])
        S.dma("sp", lambda e: e.dma_start(out=ktm[:], in_=rows(ktms, g)), "ktm_ld", r=[("ktms", g)],
              w=[("ktm", i) for i in range(4)])
        S.dma("sp", lambda e: e.dma_start(out=vtm[:], in_=rows(vtms, g)), "vtm_ld", r=[("vtms", g)],
              w=[("vtm", i) for i in range(4)])

    def d_load_rpg(g):
        S.dma("sp", lambda e: e.dma_start(out=rpg[:], in_=rows(rps, g)), "rpg_ld",
              r=[("rps", g, c) for c in range(4)], w=["rpg"] + [("rpgc", c) for c in range(4)])

    d_loads(3)
    d_load_rpg(3)
    for g in [3, 2, 1, 0]:
        def gr_piece(cbp):
            iws = [load_w(win, OFF_GR + (2 * cbp + hf) * 256) for hf in range(2)]
            for tt in range(4):
                p = nps()
                for hf in range(2):
                    iw = iws[hf]
                    cb = 2 * cbp + hf
                    for kt in range(8):
                        S.op("pe", lambda e, kt=kt, p=p, tt=tt, iw=iw, hf=hf: e.matmul(
                            ps[p][:, hf * 256:(hf + 1) * 256], lhsT=h2T[:, kt, tt * 128:(tt + 1) * 128],
                            rhs=wp[iw][:, kt, :], start=(kt == 0), stop=False), r=["h2T", ("wp", iw)], w=[("ps", p)])
                    S.op("pe", lambda e, p=p, cb=cb, hf=hf: e.matmul(
                        ps[p][:, hf * 256:(hf + 1) * 256], lhsT=ones33[32:33, :],
                        rhs=brow[32:33, 0, cb * 256:(cb + 1) * 256], start=False, stop=True),
                        r=["ones33", ("brow", 32, 0)], w=[("ps", p)])
                S.op("act", lambda e, cbp=cbp, tt=tt, p=p: e.activation(
                    out=sgr[:, tt, cbp * 512:(cbp + 1) * 512], in_=ps[p][:], func=AF.Silu),
                    r=[("ps", p)], w=[("sgr", tt)])

        for c in [3, 2, 1, 0]:
            cs = slice(c * 128, (c + 1) * 128)
            cur, curk = Tbufs[0]
            nxt, nxtk = Tbufs[1]
            pids = state_mm(c, KD2, vd_on_dve=True)
            S.op("act", lambda e, cur=cur: e.copy(out=Stb[:], in_=cur[:]), r=[(curk, h) for h in range(4)], w=["Stb"])
            state_acc(cur, curk, nxt, nxtk, G2, pids)
            Tbufs.reverse()
            if c >= 2:
                gr_piece(3 - c)
            for h in range(4):
                p = nps()
                for dt in range(2):
                    S.op("pe", lambda e, h=h, dt=dt, p=p, cs=cs: e.matmul(
                        ps[p][:, 0:256], lhsT=qT[:, 2 * h + dt, cs], rhs=Stb[:, 2 * h + dt, :],
                        start=(dt == 0), stop=(dt == 1)), r=["qT", "Stb"], w=[("ps", p)])
                S.op("act", lambda e, h=h, p=p, c=c: e.activation(
                    out=crs[:, c, h * 256:(h + 1) * 256], in_=ps[p][:, 0:256], func=AF.Copy,
                    scale=dec[:, QD2 + h:QD2 + h + 1]), r=[("ps", p), "dec"], w=[("crs", c)])
        if g > 0:
            d_loads(g - 1)
        for c in range(4):
            S.op("dve", lambda e, c=c: e.tensor_tensor(out=rpg[:, c, :], in0=rpg[:, c, :], in1=crs[:, c, :], op=ALU.add),
                 r=["rpg", ("crs", c)], w=[("rpgc", c)])
        for c in range(4):
            for h in range(4):
                S.op("act", lambda e, h=h, c=c: e.activation(
                    out=junk[:, 0:256], in_=rpg[:, c, h * 256:(h + 1) * 256], func=AF.Square,
                    accum_out=ss[:, 16 + 4 * c + h:17 + 4 * c + h]), r=[("rpgc", c), "rpg"], w=[("ssd", c)])
        allss = [("ssd", c) for c in range(4)]
        S.op("dve", lambda e: e.tensor_scalar(out=rstd[:, 16:32], in0=ss[:, 16:32], scalar1=1.0 / 256, scalar2=EPS,
                                              op0=ALU.mult, op1=ALU.add), r=allss, w=["rstdd"])
        S.op("act", lambda e: e.activation(out=rstd[:, 16:32], in_=rstd[:, 16:32], func=AF.Sqrt), r=["rstdd"], w=["rstdd"])
        S.op("dve", lambda e: e.reciprocal(out=rstd[:, 16:32], in_=rstd[:, 16:32]), r=["rstdd"], w=["rstdd"])
        for c in range(4):
            hbuf, hkey = (hb, "hb") if c % 2 == 0 else (hb2, "hb2")
            for h in range(4):
                S.op("dve", lambda e, h=h, c=c, hbuf=hbuf: e.scalar_tensor_tensor(
                    out=hbuf[:, h * 256:(h + 1) * 256], in0=rpg[:, c, h * 256:(h + 1) * 256],
                    scalar=rstd[:, 16 + 4 * c + h:17 + 4 * c + h], in1=sgr[:, c, h * 256:(h + 1) * 256],
                    op0=ALU.mult, op1=ALU.mult), r=[("rpgc", c), "rpg", "rstdd", ("sgr", c)], w=[hkey])
            transpose_into(hbuf, rgT, c, hkey, "rgT")
        if g > 0:
            d_load_rpg(g - 1)
        S.dma("sp", lambda e, g=g: e.dma_start(out=rgTs[g], in_=rgT[:].rearrange("p k t -> p (k t)")), "rgT_st",
              r=["rgT"], w=[("rgTs", g)])

    if stop_after == "D":
        return finish(nc, S, es, y, None)

    vaf = rpg
    vn = ktm
    uT = kT
    aT = qT
    barrier(CTMP_KEYS + CRS_KEYS, E1_KEYS)
    rt_load(SL_BVA, RT_BVA)
    rt_load(SL_SGUG, RT_SGUG)
    rt_load(SL_SGUB, RT_SGUB)
    maT = sgr
    maTv = maT[:].rearrange("p a b -> p (a b)").rearrange("p (k t) -> p k t", k=8)
    def e1_loads(g):
        load_h2T(g)
        S.dma("sp", lambda e: e.dma_start(out=rgT[:].rearrange("p k t -> p (k t)"), in_=rgTs[g]), "rgT_ld",
              r=[("rgTs", g)], w=["rgT"])

    e1_loads(0)
    load_xt(x1s, 0, "x1s")
    for g in range(NG_OWN):
        for cb in range(4):
            iw = load_w(win, OFF_VA + cb * 256)
            for tt in range(4):
                p = nps()
                for kt in range(8):
                    S.op("pe", lambda e, kt=kt, p=p, tt=tt, iw=iw: e.matmul(
                        ps[p][:, 0:256], lhsT=h2T[:, kt, tt * 128:(tt + 1) * 128], rhs=wp[iw][:, kt, :],
                        start=(kt == 0), stop=(kt == 7)), r=["h2T", ("wp", iw)], w=[("ps", p)])
                S.op("dve", lambda e, p=p, cb=cb, tt=tt: e.tensor_tensor(
                    out=vaf[:, tt, cb * 256:(cb + 1) * 256], in0=ps[p][:, 0:256],
                    in1=rt[:, SL_BVA, cb * 256:(cb + 1) * 256], op=ALU.add), r=[("ps", p), ("rt", SL_BVA)], w=["rpg"])
        for tt in range(4):
            S.op("dve", lambda e: e.memset(ss[:, 8:10], 0.0), w=["ss01"])
            S.op("act", lambda e, tt=tt: e.activation(out=vaf[:, tt, :], in_=vaf[:, tt, :], func=AF.Gelu,
                                                      accum_out=ss[:, 8:9]), r=["rpg", "ss01"], w=["rpg", "ss01"])
            S.op("dve", lambda e: e.tensor_scalar(out=ss[:, 10:11], in0=ss[:, 8:9], scalar1=1.0 / D, scalar2=0.0,
                                                  op0=ALU.mult, op1=ALU.add), r=["ss01"], w=["ssm"])
            S.op("dve", lambda e, tt=tt: ts_ap(e, vaf[:, tt, :], vaf[:, tt, :], ss[:, 10:11], ALU.subtract), r=["rpg", "ssm"], w=["rpg"])
            S.op("act", lambda e, tt=tt: e.activation(out=rr[:], in_=vaf[:, tt, :], func=AF.Square,
                                                      accum_out=ss[:, 9:10]), r=["rpg", "ss01"], w=["rr", "ss01"])
            S.op("dve", lambda e: e.tensor_scalar(out=ss[:, 11:12], in0=ss[:, 9:10], scalar1=1.0 / D, scalar2=EPS,
                                                  op0=ALU.mult, op1=ALU.add), r=["ss01"], w=["ssr"])
            S.op("act", lambda e: e.activation(out=ss[:, 11:12], in_=ss[:, 11:12], func=AF.Sqrt), r=["ssr"], w=["ssr"])
            S.op("dve", lambda e: e.reciprocal(out=ss[:, 11:12], in_=ss[:, 11:12]), r=["ssr"], w=["ssr"])
            S.op("dve", lambda e, tt=tt: e.scalar_tensor_tensor(
                out=vaf[:, tt, :], in0=vaf[:, tt, :], scalar=ss[:, 11:12], in1=rt[:, SL_SGUG, :], op0=ALU.mult,
                op1=ALU.mult), r=["rpg", "ssr", ("rt", SL_SGUG)], w=["rpg"])
            S.op("dve", lambda e, tt=tt: e.tensor_tensor(out=vn[:, tt, :], in0=vaf[:, tt, :], in1=rt[:, SL_SGUB, :],
                                                         op=ALU.add), r=["rpg", ("rt", SL_SGUB)], w=[("ktm", tt)])
        for blk in range(4):
            iw = load_w(win, OFF_U + blk * 256)
            for j in range(2):
                jt = 2 * blk + j
                p = nps()
                for kt in range(8):
                    S.op("pe", lambda e, kt=kt, p=p, j=j, iw=iw: e.matmul(
                        ps[p][:], lhsT=wp[iw][:, kt, j * 128:(j + 1) * 128], rhs=h2T[:, kt, :],
                        start=(kt == 0), stop=(kt == 7)), r=["h2T", ("wp", iw)], w=[("ps", p)])
                si = jt % 2
                S.op("dve", lambda e, p=p, jt=jt, si=si: ts_ap(e, sg[si][:], ps[p][:], ct[:, CT_BU + jt:CT_BU + jt + 1], ALU.add), r=[("ps", p), "ct"], w=[("sg", si)])
                S.op("act", lambda e, jt=jt, si=si: e.activation(out=uT[:, jt, :], in_=sg[si][:], func=AF.Gelu),
                     r=[("sg", si)], w=["kT"])
        for c in range(4):
            cs = slice(c * 128, (c + 1) * 128)
            for fh in range(2):
                p = nps()
                for f4 in range(4):
                    ft = fh * 4 + f4
                    gg = ft // 2
                    S.op("pe", lambda e, ft=ft, f4=f4, gg=gg, p=p, c=c: e.matmul(
                        ps[p][:, f4 * 128:(f4 + 1) * 128], lhsT=vn[:, c, ft * 128:(ft + 1) * 128],
                        rhs=wsT[:, gg * 128:(gg + 1) * 128], start=True, stop=False),
                        r=[("ktm", c), "wsT"], w=[("ps", p)])
                    S.op("pe", lambda e, f4=f4, gg=gg, p=p: e.matmul(
                        ps[p][:, f4 * 128:(f4 + 1) * 128], lhsT=ones[0:1, :], rhs=bsr[0:1, gg * 128:(gg + 1) * 128],
                        start=False, stop=True), r=["ones", "bsr"], w=[("ps", p)])
                S.op("dve", lambda e, fh=fh, p=p, cs=cs: e.tensor_tensor(
                    out=aT[:, fh * 4:(fh + 1) * 4, cs], in0=uT[:, fh * 4:(fh + 1) * 4, cs],
                    in1=ps[p][:].rearrange("p (a b) -> p a b", a=4), op=ALU.mult), r=["kT", ("ps", p)], w=["qT"])
        for (off, cto, dst, dk) in ((OFF_GA, CT_BGA, sga, "sga"), (OFF_GB, CT_BGB, sgb, "sgb")):
            for blk in range(4):
                iw = load_w(win, off + blk * 256)
                for j in range(2):
                    jt = 2 * blk + j
                    p = nps()
                    for kt in range(8):
                        S.op("pe", lambda e, kt=kt, p=p, j=j, iw=iw: e.matmul(
                            ps[p][:], lhsT=wp[iw][:, kt, j * 128:(j + 1) * 128], rhs=h2T[:, kt, :],
                            start=(kt == 0), stop=(kt == 7)), r=["h2T", ("wp", iw)], w=[("ps", p)])
                    si = jt % 2
                    S.op("dve", lambda e, p=p, jt=jt, si=si, cto=cto: ts_ap(e, sg[si][:], ps[p][:], ct[:, cto + jt:cto + jt + 1], ALU.add), r=[("ps", p), "ct"], w=[("sg", si)])
                    S.op("act", lambda e, jt=jt, dst=dst, si=si: e.activation(
                        out=dst[:, jt, :], in_=sg[si][:], func=AF.Sigmoid), r=[("sg", si)], w=[dk])
        for blk in range(4):
            iw = load_w(wa, blk * 256)
            for j in range(2):
                jt = 2 * blk + j
                p = nps()
                for kt in range(8):
                    S.op("pe", lambda e, kt=kt, p=p, j=j, iw=iw: e.matmul(
                        ps[p][:], lhsT=wp[iw][:, kt, j * 128:(j + 1) * 128], rhs=aT[:, kt, :],
                        start=(kt == 0), stop=(kt == 7)), r=["qT", ("wp", iw)], w=[("ps", p)])
                S.op("dve", lambda e, p=p, jt=jt: e.tensor_tensor(out=maTv[:, jt, :], in0=ps[p][:], in1=sga[:, jt, :],
                                                                  op=ALU.mult), r=[("ps", p), "sga"],
                     w=[("sgr", i) for i in range(4)])
        for blk in range(4):
            iw = load_w(wb, blk * 256)
            for j in range(2):
                jt = 2 * blk + j
                p = nps()
                for kt in range(8):
                    S.op("pe", lambda e, kt=kt, p=p, j=j, iw=iw: e.matmul(
                        ps[p][:], lhsT=wp[iw][:, kt, j * 128:(j + 1) * 128], rhs=rgT[:, kt, :],
                        start=(kt == 0), stop=(kt == 7)), r=["rgT", ("wp", iw)], w=[("ps", p)])
                si = jt % 2
                S.op("dve", lambda e, p=p, jt=jt, si=si: e.tensor_tensor(out=sg[si][:], in0=ps[p][:],
                                                                         in1=sgb[:, jt, :], op=ALU.mult),
                     r=[("ps", p), "sgb"], w=[("sg", si)])
                S.op("dve", lambda e, jt=jt, si=si: e.tensor_tensor(out=mixT[:, jt, :], in0=sg[si][:],
                                                                    in1=maTv[:, jt, :], op=ALU.add),
                     r=[("sg", si)] + [("sgr", i) for i in range(4)], w=["mixT"])
        if g + 1 < NG_OWN:
            e1_loads(g + 1)
        for cb in range(4):
            iw = load_w(wo, cb * 256)
            for tt in range(4):
                p = nps()
                for kt in range(8):
                    S.op("pe", lambda e, kt=kt, p=p, tt=tt, iw=iw: e.matmul(
                        ps[p][:, 0:256], lhsT=mixT[:, kt, tt * 128:(tt + 1) * 128], rhs=wp[iw][:, kt, :],
                        start=(kt == 0), stop=(kt == 7)), r=["mixT", ("wp", iw)], w=[("ps", p)])
                S.op("dve", lambda e, p=p, cb=cb, tt=tt: e.tensor_tensor(
                    out=xt[:, tt, cb * 256:(cb + 1) * 256], in0=ps[p][:, 0:256],
                    in1=xt[:, tt, cb * 256:(cb + 1) * 256], op=ALU.add), r=[("ps", p), ("xt", tt)], w=[("xt", tt)])
        for c in range(4):
            store_xt_c(x2s, g, "x2s", c)
            if g + 1 < NG_OWN:
                load_xt_c(x1s, g + 1, "x1s", c)

    if stop_after == "E1":
        return finish(nc, S, es, y, None)

    barrier(MIX_KEYS + E1_KEYS, FFN_KEYS)
    rt_load(SL_FFN2, RT_FFN2)
    rt_load(SL_FINAL, RT_FINAL)
    ykeys = []
    load_xt(x2s, 0, "x2s")
    for g in range(NG_OWN):
        ffn(SL_FFN2, w2g, w2u, w2d, load_wd=(g == 0))
        for c in range(4):
            S.op("act", lambda e, c=c: e.activation(out=junk[:], in_=xt[:, c, :], func=AF.Square,
                                                    accum_out=ss[:, c:c + 1]), r=[("xt", c)], w=[("ss", c)])
        allss = [("ss", c) for c in range(4)]
        allr = [("rstd", c) for c in range(4)]
        S.op("dve", lambda e: e.tensor_scalar(out=rstd[:, 0:4], in0=ss[:, 0:4], scalar1=1.0 / D, scalar2=EPS,
                                              op0=ALU.mult, op1=ALU.add), r=allss, w=allr)
        S.op("act", lambda e: e.activation(out=rstd[:, 0:4], in_=rstd[:, 0:4], func=AF.Sqrt), r=allr, w=allr)
        S.op("dve", lambda e: e.reciprocal(out=rstd[:, 0:4], in_=rstd[:, 0:4]), r=allr, w=allr)
        for c in range(4):
            S.op("dve", lambda e, c=c: e.scalar_tensor_tensor(out=xt[:, c, :], in0=xt[:, c, :],
                                                              scalar=rstd[:, c:c + 1], in1=rt[:, SL_FINAL, :],
                                                              op0=ALU.mult, op1=ALU.mult),
                 r=[("xt", c), ("rstd", c), ("rt", SL_FINAL)], w=[("xt", c)])
        for c in range(4):
            store_xt_c(y, g, "y", c)
            if g + 1 < NG_OWN:
                load_xt_c(x2s, g + 1, "x2s", c)
    return finish(nc, S, es, y, ykeys)


def finish(nc, S, es, y, ykeys):
    allw = set()
    for o in S.ops:
        if o["dma"]:
            allw.update(o["w"])
    S.op("sp", None, r=sorted(allw, key=str))
    S.emit()
    es.close()
    return nc


def _host_inputs(inputs):
    f = lambda a: np.ascontiguousarray(np.asarray(a, dtype=np.float32))
    x = f(inputs["x"])
    L = 0
    rep = lambda v: np.ascontiguousarray(np.broadcast_to(f(v).reshape(1, -1), (128, v.size)))
    b_in = f(inputs["b_in"])[L]
    col = lambda v: np.ascontiguousarray(f(v).reshape(8, 128).T)
    rowtab = np.stack([rep(f(inputs["ffn1_norm"])[L]), rep(f(inputs["mix_norm"])[L]),
                       rep(f(inputs["ffn2_norm"])[L]), rep(f(inputs["final_norm"])),
                       rep(f(inputs["sgu_norm_g"])[L]), rep(f(inputs["sgu_norm_b"])[L]),
                       rep(b_in[OFF_K:OFF_K + D]), rep(b_in[OFF_V:OFF_V + D]),
                       rep(b_in[OFF_GR:OFF_GR + D]), rep(b_in[OFF_VA:OFF_VA + D])], axis=0)
    ws = f(inputs["sgu_w_s"])[L]
    bs = f(inputs["sgu_b_s"])[L]
    logit = f(inputs["ret_decay_logit"])[L]
    p = np.arange(128, dtype=np.float32)
    consts = np.concatenate([p[:, None], p[:, None] + 1.0, (p[None, :] - p[:, None])], axis=1).astype(np.float32)
    ident = np.eye(128, dtype=np.float32)
    theta = (10000.0 ** (-np.arange(0, 256, 2, dtype=np.float32) / np.float32(256))).astype(np.float32)
    shared = dict(
        w1g=f(inputs["ffn1_w_gate"])[L], w1u=f(inputs["ffn1_w_up"])[L], w1d=f(inputs["ffn1_w_down"])[L],
        w2g=f(inputs["ffn2_w_gate"])[L], w2u=f(inputs["ffn2_w_up"])[L], w2d=f(inputs["ffn2_w_down"])[L],
        win=f(inputs["w_in"])[L], wa=f(inputs["w_branch_a"])[L], wb=f(inputs["w_branch_b"])[L],
        wo=f(inputs["w_out"])[L], rowtab=np.ascontiguousarray(rowtab), consts=consts, ident=ident)
    maps = []
    for core in range(8):
        b, half = core // 2, core % 2
        xs = x[b] if half == 0 else x[b, ::-1]
        pos = np.arange(SEQ, dtype=np.float32) if half == 0 else np.arange(SEQ - 1, -1, -1, dtype=np.float32)
        ang = (pos[:, None] * theta[None, :]).astype(np.float32)
        cs_tm = np.stack([np.cos(ang), np.sin(ang)]).astype(np.float32)
        cs_fm = np.ascontiguousarray(cs_tm[:, :HALF, :].transpose(0, 2, 1))
        if half == 0:
            ws_l, bs_l, lg_l = ws, bs, logit
        else:
            ws_l, bs_l, lg_l = ws[:, ::-1, ::-1], bs[:, ::-1], logit[::-1]
        wst = np.ascontiguousarray(ws_l.transpose(2, 0, 1).reshape(128, 512))
        coltab = np.concatenate([col(b_in[OFF_Q:OFF_Q + D]), col(b_in[OFF_K:OFF_K + D]), col(b_in[OFF_U:OFF_U + D]),
                                 col(b_in[OFF_GA:OFF_GA + D]), col(b_in[OFF_GB:OFF_GB + D]),
                                 np.broadcast_to(np.ascontiguousarray(lg_l).reshape(1, 8), (128, 8))], axis=1)
        m = dict(shared)
        m.update(xall=np.ascontiguousarray(xs), coltab=np.ascontiguousarray(coltab.astype(np.float32)), wst=wst,
                 bsrow=np.ascontiguousarray(bs_l.reshape(1, 512)), cs_fm=cs_fm, cs_tm=np.ascontiguousarray(cs_tm))
        maps.append(m)
    return maps


def kernel(**inputs):
    maps = _host_inputs(inputs)
    nc = build_program()
    res = run_bass_kernel_spmd(nc, maps, core_ids=list(range(8)))
    out = np.empty((4, SEQ, D), dtype=np.float32)
    for core in range(8):
        b, half = core // 2, core % 2
        yc = np.asarray(res.results[core]["y"], dtype=np.float32)
        if half == 0:
            out[b, :HALF] = yc
        else:
            out[b, HALF:] = yc[::-1]
    return out
```

```python
import os
import numpy as np
import ml_dtypes
from contextlib import ExitStack
import concourse.bass as bass
import concourse.mybir as mybir
from concourse.bass_utils import run_bass_kernel_spmd

F32 = mybir.dt.float32
BF16 = mybir.dt.bfloat16
AF = mybir.ActivationFunctionType
ALU = mybir.AluOpType

D = 1024
DFF = 2816
NFT = DFF // 128
SEQ = 4096
HALF = 2048
G = 512
NG_ALL = SEQ // G
NG_OWN = HALF // G
EPS = 1e-6
OFF_U, OFF_VA, OFF_Q, OFF_K, OFF_V, OFF_GR, OFF_GA, OFF_GB = [i * 1024 for i in range(8)]
RT_FFN1, RT_MIX, RT_FFN2, RT_FINAL, RT_SGUG, RT_SGUB, RT_BK, RT_BV, RT_BGR, RT_BVA = range(10)
NRT = 10
SL_FFN1, SL_MIX, SL_BK, SL_BV, SL_BGR, SL_BVA, SL_SGUG, SL_SGUB, SL_FFN2, SL_FINAL = 0, 1, 2, 3, 0, 1, 2, 3, 0, 1
CT_BQ, CT_BK, CT_BU, CT_BGA, CT_BGB, CT_LOGIT = 0, 8, 16, 24, 32, 40
NCT = 48


class Sched:
    def __init__(self, nc):
        self.nc = nc
        self.ops = []

    def op(self, eng, fn, r=(), w=()):
        self.ops.append(dict(eng=eng, fn=fn, r=tuple(r), w=tuple(w), dma=False, chan=None, signal=False))

    def dma(self, eng, fn, chan, r=(), w=()):
        self.ops.append(dict(eng=eng, fn=fn, r=tuple(r), w=tuple(w), dma=True, chan=("ch", chan), signal=True))

    def emit(self):
        nc = self.nc
        ops = self.ops
        last_w, readers = {}, {}
        for i, o in enumerate(ops):
            deps = set()
            for k in o["r"]:
                if k in last_w:
                    deps.add(last_w[k])
            for k in o["w"]:
                if k in last_w:
                    deps.add(last_w[k])
                deps.update(readers.get(k, ()))
            deps.discard(i)
            o["deps"] = deps
            for k in o["r"]:
                readers.setdefault(k, []).append(i)
            for k in o["w"]:
                last_w[k] = i
                readers[k] = []
        for i, o in enumerate(ops):
            keep = set()
            for j in o["deps"]:
                p = ops[j]
                if (not p["dma"]) and (not o["dma"]) and p["eng"] == o["eng"] == "pe":
                    continue
                if not p["dma"]:
                    p["signal"] = True
                keep.add(j)
            o["deps"] = keep
        counts = {}
        for o in ops:
            key = o["chan"] if o["dma"] else ("eng", o["eng"])
            o["key"] = key
            if o["signal"] and o["fn"] is not None:
                counts[key] = counts.get(key, 0) + (16 if o["dma"] else 1)
                o["sigval"] = counts[key]
        keys = sorted(counts.keys(), key=str)
        with ExitStack() as es:
            sems = {k: es.enter_context(nc.semaphore("s%d" % n)) for n, k in enumerate(keys)}
            block = es.enter_context(nc.Block())
            engmap = dict(pe=block.tensor, act=block.scalar, dve=block.vector, pool=block.gpsimd, sp=block.sync)

            def make(engname):
                def body(eng):
                    waited = {}
                    for o in ops:
                        if o["eng"] != engname:
                            continue
                        need = {}
                        for j in o["deps"]:
                            p = ops[j]
                            need[p["key"]] = max(need.get(p["key"], 0), p["sigval"])
                        for k, v in sorted(need.items(), key=str):
                            if waited.get(k, 0) < v:
                                eng.wait_ge(sems[k], v)
                                waited[k] = v
                        if o["fn"] is None:
                            continue
                        ins = o["fn"](eng)
                        if o["signal"]:
                            ins.then_inc(sems[o["key"]], 16 if o["dma"] else 1)
                return body

            for name, deco in engmap.items():
                deco(make(name))


def build_program(debug=False, stop_after=None):
    nc = bass.Bass("TRN2", target_bir_lowering=False)
    ein = lambda name, shape, dt=F32: nc.dram_tensor(name, list(shape), dt, kind="ExternalInput").ap()
    xall = ein("xall", [SEQ, D])
    w1g, w1u, w1d = ein("w1g", [D, DFF]), ein("w1u", [D, DFF]), ein("w1d", [DFF, D])
    w2g, w2u, w2d = ein("w2g", [D, DFF]), ein("w2u", [D, DFF]), ein("w2d", [DFF, D])
    win = ein("win", [D, 8 * D])
    wa, wb, wo = ein("wa", [D, D]), ein("wb", [D, D]), ein("wo", [D, D])
    rowtab = ein("rowtab", [NRT, 128, D])
    coltab = ein("coltab", [128, NCT])
    wst = ein("wst", [128, 4 * 128])
    bsrow = ein("bsrow", [1, 4 * 128])
    ident_in = ein("ident", [128, 128])
    consts = ein("consts", [128, 2 + 128])
    cs_fm = ein("cs_fm", [2, 128, HALF])
    cs_tm = ein("cs_tm", [2, SEQ, 128])
    y = nc.dram_tensor("y", [HALF, D], F32, kind="ExternalOutput").ap()

    skind = "ExternalOutput" if debug else "Internal"
    scr = lambda name, shape, dt: nc.dram_tensor(name, list(shape), dt, kind=skind).ap()
    x1s = scr("x1s", [HALF, D], F32)
    h2Ts = scr("h2Ts", [NG_ALL, 128, 8 * G], BF16)
    rps = scr("rps", [HALF, D], F32)
    qTs = scr("qTs", [NG_OWN, 128, 8 * G], BF16)
    ktms = scr("ktms", [HALF, D], BF16)
    vtms = scr("vtms", [HALF, D], BF16)
    rgTs = scr("rgTs", [NG_OWN, 128, 8 * G], BF16)
    x2s = scr("x2s", [HALF, D], F32)
    dbg_state = scr("dbg_state", [128, 8 * 256], F32) if debug else None
    dbg_ct = scr("dbg_ct", [128, NCT], F32) if debug else None
    dbg_lg = scr("dbg_lg", [128, 8], F32) if debug else None
    dbg_dec = scr("dbg_dec", [128, 24], F32) if debug else None
    dbg_MT = scr("dbg_MT", [128, 512], F32) if debug else None
    dbg_kT = scr("dbg_kT", [128, 8 * G], BF16) if debug else None

    S = Sched(nc)
    es = ExitStack()
    sb = lambda name, shape, dt=F32: es.enter_context(nc.sbuf_tensor(name, list(shape), dt))
    rt = sb("rt", [128, 4, D])
    ct = sb("ct", [128, NCT])
    cst = sb("cst", [128, 130])
    ident = sb("identb", [128, 128], BF16)
    wsT = sb("wsT", [128, 512], BF16)
    bsr = sb("bsr", [1, 512], BF16)
    ones = sb("ones", [1, 128], BF16)
    ones33 = sb("ones33", [33, 128], BF16)
    brow = sb("brow", [33, 2, D], BF16)
    lg = sb("lg", [128, 8])
    dec = sb("dec", [128, 24])
    MT = sb("MT", [128, 4, 128])
    mtmp = sb("mtmp", [128, 2, 128])
    Sst = sb("Sst", [128, 8, 256])
    Tst = sb("Tst", [128, 8, 256])
    Stb = sb("Stb", [128, 8, 256], BF16)
    ss = sb("ss", [128, 32])
    rstd = sb("rstd", [128, 32])
    xt = sb("xt", [128, 4, D])
    hb = sb("hb", [128, D], BF16)
    hb2 = sb("hb2", [128, D], BF16)
    junk = sb("junk", [128, D], BF16)
    h2T = sb("h2T", [128, 8, G], BF16)
    wp = [sb("wp%d" % i, [128, 8, 256], BF16) for i in range(4)]
    ps = [es.enter_context(nc.psum_tensor("ps%d" % i, [128, 512], F32)) for i in range(6)]
    pst = [es.enter_context(nc.psum_tensor("pst%d" % i, [128, 1024], BF16)) for i in range(2)]
    cnt = dict(ps=0, pst=0, wp=0, nwp=4)

    def nps():
        i = cnt["ps"] % 6
        cnt["ps"] += 1
        return i

    def npst():
        i = cnt["pst"] % 2
        cnt["pst"] += 1
        return i

    def nwp():
        i = cnt["wp"] % cnt["nwp"]
        cnt["wp"] += 1
        return i

    def ts_ap(e, out, in0, sc, op0):
        return e.tensor_scalar(out=out, in0=in0, scalar1=sc, scalar2=None, op0=op0)

    S.dma("sp", lambda e: e.dma_start(out=ct[:], in_=coltab[:, :]), "ct", w=["ct"])
    S.dma("sp", lambda e: e.dma_start(out=cst[:], in_=consts[:, :]), "cst", w=["cst"])
    S.dma("pool", lambda e: e.dma_start(out=ident[:], in_=ident_in[:, :]), "ident", w=["ident"])
    S.dma("pool", lambda e: e.dma_start(out=wsT[:], in_=wst[:, :]), "wsT", w=["wsT"])
    S.dma("pool", lambda e: e.dma_start(out=bsr[:], in_=bsrow[:, :]), "bsr", w=["bsr"])
    S.op("dve", lambda e: e.memset(ones[:], 1.0), w=["ones"])
    S.op("dve", lambda e: e.memset(ones33[:], 1.0), w=["ones33"])
    for (bp, bi, ridx) in ((0, 0, RT_BK), (0, 1, RT_BV), (32, 0, RT_BGR), (32, 1, RT_BVA)):
        S.dma("pool", lambda e, bp=bp, bi=bi, ridx=ridx: e.dma_start(out=brow[bp:bp + 1, bi, :],
                                                                      in_=rowtab[ridx][0:1, :]),
              "brow%d_%d" % (bp, bi), w=[("brow", bp, bi)])
    S.op("dve", lambda e: e.memset(Sst[:], 0.0), w=[("Sst", h) for h in range(4)])
    S.op("dve", lambda e: e.memset(Tst[:], 0.0), w=[("Tst", h) for h in range(4)])
    S.op("dve", lambda e: e.memset(Stb[:], 0.0), w=["Stb"])
    S.op("dve", lambda e: e.tensor_scalar(out=lg[:], in0=ct[:, CT_LOGIT:CT_LOGIT + 8], scalar1=-1.0, scalar2=0.0,
                                          op0=ALU.mult, op1=ALU.add), r=["ct"], w=["lg"])
    S.op("act", lambda e: e.activation(out=lg[:], in_=lg[:], func=AF.Exp), r=["lg"], w=["lg"])
    S.op("dve", lambda e: e.tensor_scalar(out=lg[:], in0=lg[:], scalar1=1.0, scalar2=0.0, op0=ALU.add, op1=ALU.add),
         r=["lg"], w=["lg"])
    S.op("act", lambda e: e.activation(out=lg[:], in_=lg[:], func=AF.Ln), r=["lg"], w=["lg"])
    S.op("dve", lambda e: e.tensor_scalar(out=lg[:], in0=lg[:], scalar1=-1.0, scalar2=0.0, op0=ALU.mult, op1=ALU.add),
         r=["lg"], w=["lg"])
    S.op("dve", lambda e: ts_ap(e, dec[:, 0:4], lg[:, 0:4], cst[:, 1:2], ALU.mult), r=["lg", "cst"], w=["dec"])
    S.op("dve", lambda e: e.tensor_scalar(out=mtmp[:, 0, 0:1], in0=cst[:, 0:1], scalar1=-1.0, scalar2=127.0,
                                          op0=ALU.mult, op1=ALU.add), r=["cst"], w=["mtmp"])
    S.op("dve", lambda e: ts_ap(e, dec[:, 4:8], lg[:, 0:4], mtmp[:, 0, 0:1], ALU.mult), r=["lg", "mtmp"], w=["dec"])
    S.op("dve", lambda e: e.tensor_scalar(out=dec[:, 8:12], in0=lg[:, 0:4], scalar1=128.0, scalar2=0.0,
                                          op0=ALU.mult, op1=ALU.add), r=["lg"], w=["dec"])
    S.op("dve", lambda e: e.tensor_scalar(out=mtmp[:, 0, 1:2], in0=cst[:, 0:1], scalar1=-1.0, scalar2=128.0,
                                          op0=ALU.mult, op1=ALU.add), r=["cst", "mtmp"], w=["mtmp"])
    S.op("dve", lambda e: ts_ap(e, dec[:, 12:16], lg[:, 4:8], mtmp[:, 0, 1:2], ALU.mult), r=["lg", "mtmp"], w=["dec"])
    S.op("dve", lambda e: ts_ap(e, dec[:, 16:20], lg[:, 4:8], cst[:, 0:1], ALU.mult), r=["lg", "cst"], w=["dec"])
    S.op("dve", lambda e: e.tensor_scalar(out=dec[:, 20:24], in0=lg[:, 4:8], scalar1=128.0, scalar2=0.0,
                                          op0=ALU.mult, op1=ALU.add), r=["lg"], w=["dec"])
    S.op("act", lambda e: e.activation(out=dec[:], in_=dec[:], func=AF.Exp), r=["dec"], w=["dec"])
    S.op("dve", lambda e: e.tensor_scalar(out=dec[:, 4:8], in0=dec[:, 4:8], scalar1=0.0625, scalar2=0.0,
                                          op0=ALU.mult, op1=ALU.add), r=["dec"], w=["dec"])
    S.op("dve", lambda e: e.tensor_scalar(out=dec[:, 16:20], in0=dec[:, 16:20], scalar1=0.0625, scalar2=0.0,
                                          op0=ALU.mult, op1=ALU.add), r=["dec"], w=["dec"])
    QD1, KD1, G1, QD2, KD2, G2 = 0, 4, 8, 12, 16, 20
    S.op("dve", lambda e: e.tensor_scalar(out=mtmp[:, 0, :], in0=cst[:, 2:130], scalar1=0.0, scalar2=0.0,
                                          op0=ALU.max, op1=ALU.add), r=["cst", "mtmp"], w=["mtmp"])
    S.op("dve", lambda e: e.tensor_scalar(out=mtmp[:, 1, :], in0=cst[:, 2:130], scalar1=-1.0, scalar2=0.0,
                                          op0=ALU.mult, op1=ALU.max), r=["cst", "mtmp"], w=["mtmp"])
    for h in range(4):
        S.op("dve", lambda e, h=h: ts_ap(e, MT[:, h, :], mtmp[:, 0, :], lg[:, h:h + 1], ALU.mult), r=["mtmp", "lg"], w=["MT"])
        S.op("dve", lambda e, h=h: e.scalar_tensor_tensor(out=MT[:, h, :], in0=mtmp[:, 1, :],
                                                          scalar=lg[:, 4 + h:5 + h], in1=MT[:, h, :],
                                                          op0=ALU.mult, op1=ALU.add), r=["mtmp", "lg", "MT"], w=["MT"])
    S.op("act", lambda e: e.activation(out=MT[:], in_=MT[:], func=AF.Exp), r=["MT"], w=["MT"])
    S.op("dve", lambda e: e.tensor_scalar(out=MT[:], in0=MT[:], scalar1=0.0625, scalar2=0.0, op0=ALU.mult, op1=ALU.add),
         r=["MT"], w=["MT"])

    if debug:
        S.dma("sp", lambda e: e.dma_start(out=dbg_ct[:, :], in_=ct[:]), "dbg1", r=["ct"], w=["dbg1"])
        S.dma("sp", lambda e: e.dma_start(out=dbg_lg[:, :], in_=lg[:]), "dbg2", r=["lg"], w=["dbg2"])
        S.dma("sp", lambda e: e.dma_start(out=dbg_dec[:, :], in_=dec[:]), "dbg3", r=["dec"], w=["dbg3"])
        S.dma("sp", lambda e: e.dma_start(out=dbg_MT[:, :], in_=MT[:].rearrange("p a b -> p (a b)")), "dbg4",
              r=["MT"], w=["dbg4"])
    def rt_load(slot, idx):
        S.dma("sp", lambda e: e.dma_start(out=rt[:, slot, :], in_=rowtab[idx]), "rt%d" % slot, w=[("rt", slot)])

    arena = sb("arena", [128, 21504])
    dummy = sb("bdummy", [128, 8])

    def carve(off, shape, dt=F32):
        nb = int(np.prod(shape)) * (4 if dt == F32 else 2)
        ap = arena[:, off // 4:(off + nb) // 4]
        if dt == BF16:
            ap = ap.bitcast(BF16)
        if len(shape) == 2:
            ap = ap.rearrange("p (a b) -> p a b", a=shape[0])
        return ap

    FFN_KEYS = ["hT", ("wp", 4), ("wp", 5)] + [("tT", i) for i in range(NFT)] + \
        [("wd", h, b) for h in range(2) for b in range(NFT // 2)]
    CTMP_KEYS = ["qtmp", "fA", "fB", "csfm", "PT4"] + [("rp", b, h) for b in range(2) for h in range(4)]
    E1_KEYS = ["sga", "sgb", "mixT"]
    CRS_KEYS = [("crs", c) for c in range(4)]
    MIX_KEYS = ["qT", "kT", "rpg", "rgT", "rr"] + [("sgr", i) for i in range(4)] + CTMP_KEYS

    def barrier(rk, wk):
        S.op("dve", lambda e: e.memset(dummy[:], 0.0), w=list(rk) + list(wk))

    def load_w(W, col0, ncols=256, row_tiles=8):
        i = nwp()
        S.dma("pool", lambda e: e.dma_start(
            out=wp[i][:, 0:row_tiles, 0:ncols],
            in_=W[0:row_tiles * 128, col0:col0 + ncols].rearrange("(kt p) c -> p kt c", p=128)),
            "wp%d" % i, w=[("wp", i)])
        return i

    def norm_group(rtidx, dstT, dstkey):
        for c in range(4):
            S.op("act", lambda e, c=c: e.activation(out=junk[:], in_=xt[:, c, :], func=AF.Square,
                                                    accum_out=ss[:, c:c + 1]), r=[("xt", c)], w=[("ss", c)])
        allss = [("ss", c) for c in range(4)]
        allr = [("rstd", c) for c in range(4)]
        S.op("dve", lambda e: e.tensor_scalar(out=rstd[:, 0:4], in0=ss[:, 0:4], scalar1=1.0 / D, scalar2=EPS,
                                              op0=ALU.mult, op1=ALU.add), r=allss, w=allr)
        S.op("act", lambda e: e.activation(out=rstd[:, 0:4], in_=rstd[:, 0:4], func=AF.Sqrt), r=allr, w=allr)
        S.op("dve", lambda e: e.reciprocal(out=rstd[:, 0:4], in_=rstd[:, 0:4]), r=allr, w=allr)
        for c in range(4):
            hbuf, hkey = (hb, "hb") if c % 2 == 0 else (hb2, "hb2")
            S.op("dve", lambda e, c=c, hbuf=hbuf: e.scalar_tensor_tensor(
                out=hbuf[:], in0=xt[:, c, :], scalar=rstd[:, c:c + 1], in1=rt[:, rtidx, :], op0=ALU.mult,
                op1=ALU.mult), r=[("xt", c), ("rstd", c), ("rt", rtidx)], w=[hkey])
            transpose_into(hbuf, dstT, c, hkey, dstkey)

    def transpose_into(srcb, dstT, c, srckey, dstkey):
        p = npst()
        for kt in range(8):
            S.op("pe", lambda e, kt=kt: e.transpose(out=pst[p][:, kt * 128:(kt + 1) * 128],
                                                    in_=srcb[:, kt * 128:(kt + 1) * 128], identity=ident[:]),
                 r=[srckey, "ident"], w=[("pst", p)])
        S.op("act", lambda e: e.copy(out=dstT[:, :, c * 128:(c + 1) * 128],
                                     in_=pst[p][:].rearrange("p (k t) -> p k t", k=8)),
             r=[("pst", p)], w=[dstkey])

    hT = carve(0, [8, G], BF16)
    wp.append(carve(75776, [8, 256], BF16))
    wp.append(carve(79872, [8, 256], BF16))
    tT = carve(8192, [NFT, G], BF16)
    wd = [carve(30720 + i * 22528, [NFT, 512], BF16) for i in range(2)]
    sg = [sb("sg%d" % i, [128, 512]) for i in range(2)]
    gtmp = sb("gtmp", [128, 256])

    def ffn(rtidx, Wg, Wu, Wd, load_wd=True):
        cnt["nwp"] = 6
        norm_group(rtidx, hT, "hT")
        for blk in range(NFT // 2):
            ig = load_w(Wg, blk * 256)
            iu = load_w(Wu, blk * 256)
            for j in range(2):
                ft = blk * 2 + j
                pg, pu = nps(), nps()
                for kt in range(8):
                    S.op("pe", lambda e, kt=kt, pg=pg, ig=ig, j=j: e.matmul(
                        ps[pg][:], lhsT=wp[ig][:, kt, j * 128:(j + 1) * 128], rhs=hT[:, kt, :],
                        start=(kt == 0), stop=(kt == 7)), r=[("wp", ig), "hT"], w=[("ps", pg)])
                for kt in range(8):
                    S.op("pe", lambda e, kt=kt, pu=pu, iu=iu, j=j: e.matmul(
                        ps[pu][:], lhsT=wp[iu][:, kt, j * 128:(j + 1) * 128], rhs=hT[:, kt, :],
                        start=(kt == 0), stop=(kt == 7)), r=[("wp", iu), "hT"], w=[("ps", pu)])
                si = ft % 2
                if j == 0 and load_wd:
                    for half in range(2):
                        S.dma("pool", lambda e, half=half, blk=blk: e.dma_start(
                            out=wd[half][:, 2 * blk:2 * blk + 2, :],
                            in_=Wd[blk * 256:(blk + 1) * 256, half * 512:(half + 1) * 512].rearrange(
                                "(ft p) c -> p ft c", p=128)),
                            "wd%d_%d" % (half, blk), w=[("wd", half, blk)])
                S.op("act", lambda e, pg=pg, si=si: e.activation(out=sg[si][:], in_=ps[pg][:], func=AF.Silu),
                     r=[("ps", pg)], w=[("sg", si)])
                S.op("dve", lambda e, pu=pu, si=si, ft=ft: e.tensor_tensor(out=tT[:, ft, :], in0=sg[si][:],
                                                                           in1=ps[pu][:], op=ALU.mult),
                     r=[("ps", pu), ("sg", si)], w=[("tT", ft)])
        for tt in range(4):
            for half in range(2):
                p = nps()
                for ft in range(NFT):
                    S.op("pe", lambda e, ft=ft, p=p, tt=tt, half=half: e.matmul(
                        ps[p][:], lhsT=tT[:, ft, tt * 128:(tt + 1) * 128], rhs=wd[half][:, ft, :],
                        start=(ft == 0), stop=(ft == NFT - 1)), r=[("tT", ft), ("wd", half, ft // 2)], w=[("ps", p)])
                S.op("dve", lambda e, p=p, tt=tt, half=half: e.scalar_tensor_tensor(
                    out=xt[:, tt, half * 512:(half + 1) * 512], in0=ps[p][:], scalar=0.5,
                    in1=xt[:, tt, half * 512:(half + 1) * 512], op0=ALU.mult, op1=ALU.add),
                    r=[("ps", p), ("xt", tt)], w=[("xt", tt)])
        cnt["nwp"] = 4

    def load_xt_c(src, g, srckey, c):
        S.dma("sp", lambda e: e.dma_start(out=xt[:, c, :], in_=src[g * G + c * 128:g * G + (c + 1) * 128, :]),
              "xt%d" % c, r=([(srckey, g, c)] if srckey else []), w=[("xt", c)])

    def store_xt_c(dst, g, dstkey, c):
        S.dma("sp", lambda e: e.dma_start(out=dst[g * G + c * 128:g * G + (c + 1) * 128, :], in_=xt[:, c, :]),
              "xt_st%d" % c, r=[("xt", c)], w=[(dstkey, g, c)])

    def load_xt(src, g, srckey=None):
        for c in range(4):
            load_xt_c(src, g, srckey, c)

    def store_xt(dst, g, dstkey):
        for c in range(4):
            store_xt_c(dst, g, dstkey, c)

    def rows(ap, g):
        return ap[g * G:(g + 1) * G, :].rearrange("(c p) d -> p c d", p=128)

    def own_rows(ap, g):
        return rows(ap, g)

    rt_load(SL_FFN1, RT_FFN1)
    rt_load(SL_MIX, RT_MIX)
    A_ORDER = [7, 6, 5, 4, 0, 1, 2, 3]
    load_xt(xall, A_ORDER[0])
    for gi, g in enumerate(A_ORDER):
        ffn(SL_FFN1, w1g, w1u, w1d, load_wd=(g == 7))
        if g < NG_OWN:
            store_xt(x1s, g, "x1s")
        norm_group(SL_MIX, h2T, "h2T")
        if gi + 1 < len(A_ORDER):
            load_xt(xall, A_ORDER[gi + 1])
        S.dma("sp", lambda e, g=g: e.dma_start(out=h2Ts[g], in_=h2T[:].rearrange("p k t -> p (k t)")), "h2T_st",
              r=["h2T"], w=[("h2Ts", g)])

    if stop_after == "A":
        return finish(nc, S, es, y, None)

    ktm = sb("ktm", [128, 4, D], BF16)
    vtm = sb("vtm", [128, 4, D], BF16)
    Vd = sb("Vd", [128, D], BF16)
    cstm = sb("cstm", [128, 2, 4, 128])

    def load_h2T(g):
        S.dma("sp", lambda e: e.dma_start(out=h2T[:].rearrange("p k t -> p (k t)"), in_=h2Ts[g]), "h2T_ld",
              r=[("h2Ts", g)], w=["h2T"])

    def proj_tm(off, bp, bi, consume):
        for cbp in range(2):
            iws = [load_w(win, off + (2 * cbp + hf) * 256) for hf in range(2)]
            for tt in range(4):
                p = nps()
                for hf in range(2):
                    iw = iws[hf]
                    cb = 2 * cbp + hf
                    for kt in range(8):
                        S.op("pe", lambda e, kt=kt, p=p, tt=tt, iw=iw, hf=hf: e.matmul(
                            ps[p][:, hf * 256:(hf + 1) * 256], lhsT=h2T[:, kt, tt * 128:(tt + 1) * 128],
                            rhs=wp[iw][:, kt, :], start=(kt == 0), stop=False), r=["h2T", ("wp", iw)], w=[("ps", p)])
                    S.op("pe", lambda e, p=p, cb=cb, hf=hf: e.matmul(
                        ps[p][:, hf * 256:(hf + 1) * 256], lhsT=ones33[bp:bp + 1, :],
                        rhs=brow[bp:bp + 1, bi, cb * 256:(cb + 1) * 256], start=False, stop=True),
                        r=["ones33", ("brow", bp, bi)], w=[("ps", p)])
                consume(p, cbp, tt)

    rA4 = sg[0][:].rearrange("p (a b) -> p a b", a=4)
    rB4 = sg[1][:].rearrange("p (a b) -> p a b", a=4)

    def load_cstm(g):
        for s2 in range(2):
            S.dma("sp", lambda e, s2=s2: e.dma_start(
                out=cstm[:, s2, :, :], in_=cs_tm[s2, g * G:(g + 1) * G, :].rearrange("(c p) f -> p c f", p=128)),
                "cstm%d" % s2, w=[("cstm", s2)])

    def kv_tm(g, load_cs=True):
        if load_cs:
            load_cstm(g)

        def k_consume(p, cbp, tt):
            pv = ps[p][:].rearrange("p (a b) -> p a b", a=4)
            S.op("dve", lambda e: e.tensor_tensor(
                out=rA4, in0=pv, in1=cstm[:, 0, tt, :].unsqueeze(1).to_broadcast([128, 4, 128]),
                op=ALU.mult), r=[("ps", p), ("cstm", 0)], w=[("sg", 0)])
            S.op("dve", lambda e: e.tensor_tensor(
                out=rB4, in0=pv, in1=cstm[:, 1, tt, :].unsqueeze(1).to_broadcast([128, 4, 128]),
                op=ALU.mult), r=[("ps", p), ("cstm", 1)], w=[("sg", 1)])
            kv4 = ktm[:, tt, cbp * 512:(cbp + 1) * 512].rearrange("p (c t f) -> p c t f", c=2, t=2)
            a4 = rA4.rearrange("p (c t) f -> p c t f", c=2)
            b4 = rB4.rearrange("p (c t) f -> p c t f", c=2)
            S.op("dve", lambda e: e.tensor_tensor(out=kv4[:, :, 0, :], in0=a4[:, :, 0, :], in1=b4[:, :, 1, :],
                                                  op=ALU.subtract), r=[("sg", 0), ("sg", 1)], w=[("ktm", tt)])
            S.op("dve", lambda e: e.tensor_tensor(out=kv4[:, :, 1, :], in0=a4[:, :, 1, :], in1=b4[:, :, 0, :],
                                                  op=ALU.add), r=[("sg", 0), ("sg", 1)], w=[("ktm", tt)])

        def v_consume(p, cbp, tt):
            S.op("act", lambda e: e.copy(out=vtm[:, tt, cbp * 512:(cbp + 1) * 512], in_=ps[p][:]),
                 r=[("ps", p)], w=[("vtm", tt)])

        proj_tm(OFF_K, 0, 0, k_consume)
        proj_tm(OFF_V, 0, 1, v_consume)

    def state_mm(c, kd, vd_on_dve=False):
        if vd_on_dve:
            S.op("dve", lambda e: e.tensor_tensor(
                out=Vd[:].rearrange("p (h e) -> p h e", h=4), in0=vtm[:, c, :].rearrange("p (h e) -> p h e", h=4),
                in1=dec[:, kd:kd + 4].unsqueeze(2).to_broadcast([128, 4, 256]), op=ALU.mult),
                r=[("vtm", c), "dec"], w=["Vd"])
        else:
            for h in range(4):
                S.op("act", lambda e, h=h: e.activation(
                    out=Vd[:, h * 256:(h + 1) * 256], in_=vtm[:, c, h * 256:(h + 1) * 256], func=AF.Copy,
                    scale=dec[:, kd + h:kd + h + 1]), r=[("vtm", c), "dec"], w=["Vd"])
        pids = []
        for h in range(4):
            p = nps()
            pids.append(p)
            for dt in range(2):
                S.op("pe", lambda e, h=h, dt=dt, p=p: e.matmul(
                    ps[p][:, dt * 256:(dt + 1) * 256], lhsT=ktm[:, c, h * 256 + dt * 128:h * 256 + (dt + 1) * 128],
                    rhs=Vd[:, h * 256:(h + 1) * 256], start=True, stop=True),
                    r=[("ktm", c), "Vd"], w=[("ps", p)])
        return pids

    def state_acc(Sin, inkey, Sout, outkey, gd, pids):
        for h in range(4):
            p = pids[h]
            S.op("dve", lambda e, h=h, p=p: e.scalar_tensor_tensor(
                out=Sout[:, 2 * h:2 * h + 2, :], in0=Sin[:, 2 * h:2 * h + 2, :], scalar=dec[:, gd + h:gd + h + 1],
                in1=ps[p][:].rearrange("p (a b) -> p a b", a=2), op0=ALU.mult, op1=ALU.add),
                r=[(inkey, h), ("ps", p), "dec"], w=[(outkey, h)])

    def state_update(St, stkey, c, kd, gd):
        state_acc(St, stkey, St, stkey, gd, state_mm(c, kd))

    for g in [7, 6, 5, 4]:
        load_h2T(g)
        kv_tm(g)
        for c in [3, 2, 1, 0]:
            state_update(Tst, "Tst", c, KD2, G2)

    if debug:
        S.dma("sp", lambda e: e.dma_start(out=dbg_state[:, :], in_=Tst[:].rearrange("p a b -> p (a b)")), "dbg",
              r=[("Tst", h) for h in range(4)], w=["dbgs"])
    if stop_after == "B":
        return finish(nc, S, es, y, None)

    barrier(FFN_KEYS, MIX_KEYS)
    qT = carve(0, [8, G], BF16)
    kT = carve(8192, [8, G], BF16)
    rpg = carve(16384, [4, D])
    sgr = carve(32768, [4, D])
    rgT = carve(49152, [8, G], BF16)
    rr = carve(57344, [1, D])[:, 0, :]
    qtmp = carve(61440, [2, G])
    fA = carve(65536, [2, G])
    fB = carve(69632, [2, G])
    csfm = carve(73728, [2, G])
    rpb = [carve(77824, [1, D])[:, 0, :], rr]
    PT4 = carve(82944, [4, 128], BF16)
    crs = carve(61440, [4, D])
    sga = carve(61440, [8, G], BF16)
    sgb = carve(69632, [8, G], BF16)
    mixT = carve(77824, [8, G], BF16)

    def proj_fm_rot(off, ctoff, dst, dstkey):
        for blk in range(4):
            iw = load_w(win, off + blk * 256)
            for j in range(2):
                jt = 2 * blk + j
                p = nps()
                for kt in range(8):
                    S.op("pe", lambda e, kt=kt, p=p, j=j, iw=iw: e.matmul(
                        ps[p][:], lhsT=wp[iw][:, kt, j * 128:(j + 1) * 128], rhs=h2T[:, kt, :],
                        start=(kt == 0), stop=(kt == 7)), r=["h2T", ("wp", iw)], w=[("ps", p)])
                S.op("act", lambda e, p=p, j=j, jt=jt: e.activation(
                    out=qtmp[:, j, :], in_=ps[p][:], func=AF.Identity, bias=ct[:, ctoff + jt:ctoff + jt + 1]),
                    r=[("ps", p), "ct"], w=["qtmp"])
            S.op("dve", lambda e: e.tensor_tensor(
                out=fA[:], in0=qtmp[:], in1=csfm[:, 0, :].unsqueeze(1).to_broadcast([128, 2, G]), op=ALU.mult),
                r=["qtmp", "csfm"], w=["fA"])
            S.op("dve", lambda e: e.tensor_tensor(
                out=fB[:], in0=qtmp[:], in1=csfm[:, 1, :].unsqueeze(1).to_broadcast([128, 2, G]), op=ALU.mult),
                r=["qtmp", "csfm"], w=["fB"])
            S.op("dve", lambda e, blk=blk: e.tensor_tensor(out=dst[:, 2 * blk, :], in0=fA[:, 0, :], in1=fB[:, 1, :],
                                                           op=ALU.subtract), r=["fA", "fB"], w=[dstkey])
            S.op("dve", lambda e, blk=blk: e.tensor_tensor(out=dst[:, 2 * blk + 1, :], in0=fA[:, 1, :],
                                                           in1=fB[:, 0, :], op=ALU.add), r=["fA", "fB"], w=[dstkey])

    def load_csfm(g):
        S.dma("sp", lambda e: e.dma_start(
            out=csfm[:], in_=cs_fm[:, :, g * G:(g + 1) * G].rearrange("s p t -> p s t")), "csfm", w=["csfm"])

    load_h2T(0)
    load_cstm(0)
    load_csfm(0)
    for g in range(NG_OWN):
        kv_tm(g, load_cs=False)
        S.dma("sp", lambda e, g=g: e.dma_start(out=rows(ktms, g), in_=ktm[:]), "ktm_st",
              r=[("ktm", i) for i in range(4)], w=[("ktms", g)])
        S.dma("sp", lambda e, g=g: e.dma_start(out=rows(vtms, g), in_=vtm[:]), "vtm_st",
              r=[("vtm", i) for i in range(4)], w=[("vtms", g)])
        proj_fm_rot(OFF_Q, CT_BQ, qT, "qT")
        proj_fm_rot(OFF_K, CT_BK, kT, "kT")
        if debug and g == 0:
            S.dma("sp", lambda e: e.dma_start(out=dbg_kT[:, :], in_=kT[:].rearrange("p k t -> p (k t)")), "dbg5",
                  r=["kT"], w=["dbg5"])
        S.dma("sp", lambda e, g=g: e.dma_start(out=qTs[g], in_=qT[:].rearrange("p k t -> p (k t)")), "qT_st",
              r=["qT"], w=[("qTs", g)])
        if g + 1 < NG_OWN:
            load_h2T(g + 1)
            load_cstm(g + 1)
            load_csfm(g + 1)
        for c in range(4):
            cs = slice(c * 128, (c + 1) * 128)
            state_update(Sst, "Sst", c, KD1, G1)
            p1 = nps()
            for h in range(4):
                for dt in range(2):
                    S.op("pe", lambda e, h=h, dt=dt, p1=p1, cs=cs: e.matmul(
                        ps[p1][:, h * 128:(h + 1) * 128], lhsT=kT[:, 2 * h + dt, cs], rhs=qT[:, 2 * h + dt, cs],
                        start=(dt == 0), stop=(dt == 1)), r=["kT", "qT"], w=[("ps", p1)])
            S.op("dve", lambda e, p1=p1: e.tensor_tensor(
                out=PT4, in0=ps[p1][:].rearrange("p (h c) -> p h c", h=4), in1=MT[:], op=ALU.mult),
                r=[("ps", p1), "MT"], w=["PT4"])
            b = c % 2
            rb = rpb[b]
            for h in range(4):
                p2 = nps()
                S.op("pe", lambda e, h=h, p2=p2, c=c: e.matmul(
                    ps[p2][:, 0:256], lhsT=PT4[:, h, :], rhs=vtm[:, c, h * 256:(h + 1) * 256], start=True, stop=True),
                    r=["PT4", ("vtm", c)], w=[("ps", p2)])
                for dt in range(2):
                    S.op("pe", lambda e, h=h, dt=dt, p2=p2, cs=cs: e.matmul(
                        ps[p2][:, 256:512], lhsT=qT[:, 2 * h + dt, cs], rhs=Stb[:, 2 * h + dt, :],
                        start=(dt == 0), stop=(dt == 1)), r=["qT", "Stb"], w=[("ps", p2)])
                S.op("act", lambda e, h=h, p2=p2, rb=rb: e.activation(
                    out=rb[:, h * 256:(h + 1) * 256], in_=ps[p2][:, 256:512], func=AF.Copy,
                    scale=dec[:, QD1 + h:QD1 + h + 1]), r=[("ps", p2), "dec"], w=[("rp", b, h)])
                S.op("dve", lambda e, h=h, p2=p2, rb=rb: e.tensor_tensor(
                    out=rb[:, h * 256:(h + 1) * 256], in0=ps[p2][:, 0:256], in1=rb[:, h * 256:(h + 1) * 256],
                    op=ALU.add), r=[("ps", p2), ("rp", b, h)], w=[("rp", b, h)])
            S.dma("sp", lambda e, g=g, c=c, rb=rb: e.dma_start(
                out=rps[g * G + c * 128:g * G + (c + 1) * 128, :], in_=rb[:]),
                "rp_st%d" % b, r=[("rp", b, h) for h in range(4)], w=[("rps", g, c)])
            S.op("act", lambda e: e.copy(out=Stb[:], in_=Sst[:]), r=[("Sst", h) for h in range(4)], w=["Stb"])

    if stop_after == "C":
        return finish(nc, S, es, y, None)

    Tbufs = [(Tst, "Tst"), (Sst, "Sst")]
    barrier(CTMP_KEYS, CRS_KEYS)
    def d_loads(g):
        load_h2T(g)
        S.dma("sp", lambda e: e.dma_start(out=qT[:].rearrange("p k t -> p (k t)"), in_=qTs[g]), "qT_ld",
              r=[("qTs", g)], w=["qT"])
        S.dma("sp", lambda e: e.dma_start(out=ktm[:], in_=rows(ktms, g)), "ktm_ld", r=[("ktms", g)],
              w=[("ktm", i) for i in range(4)])
        S.dma("sp", lambda e: e.dma_start(out=vtm[:], in_=rows(vtms, g)), "vtm_ld", r=[("vtms", g)],
              w=[("vtm", i) for i in range(4)])

    def d_load_rpg(g):
        S.dma("sp", lambda e: e.dma_start(out=rpg[:], in_=rows(rps, g)), "rpg_ld",
              r=[("rps", g, c) for c in range(4)], w=["rpg"] + [("rpgc", c) for c in range(4)])

    d_loads(3)
    d_load_rpg(3)
    for g in [3, 2, 1, 0]:
        def gr_piece(cbp):
            iws = [load_w(win, OFF_GR + (2 * cbp + hf) * 256) for hf in range(2)]
            for tt in range(4):
                p = nps()
                for hf in range(2):
                    iw = iws[hf]
                    cb = 2 * cbp + hf
                    for kt in range(8):
                        S.op("pe", lambda e, kt=kt, p=p, tt=tt, iw=iw, hf=hf: e.matmul(
                            ps[p][:, hf * 256:(hf + 1) * 256], lhsT=h2T[:, kt, tt * 128:(tt + 1) * 128],
                            rhs=wp[iw][:, kt, :], start=(kt == 0), stop=False), r=["h2T", ("wp", iw)], w=[("ps", p)])
                    S.op("pe", lambda e, p=p, cb=cb, hf=hf: e.matmul(
                        ps[p][:, hf * 256:(hf + 1) * 256], lhsT=ones33[32:33, :],
                        rhs=brow[32:33, 0, cb * 256:(cb + 1) * 256], start=False, stop=True),
                        r=["ones33", ("brow", 32, 0)], w=[("ps", p)])
                S.op("act", lambda e, cbp=cbp, tt=tt, p=p: e.activation(
                    out=sgr[:, tt, cbp * 512:(cbp + 1) * 512], in_=ps[p][:], func=AF.Silu),
                    r=[("ps", p)], w=[("sgr", tt)])

        for c in [3, 2, 1, 0]:
            cs = slice(c * 128, (c + 1) * 128)
            cur, curk = Tbufs[0]
            nxt, nxtk = Tbufs[1]
            pids = state_mm(c, KD2, vd_on_dve=True)
            S.op("act", lambda e, cur=cur: e.copy(out=Stb[:], in_=cur[:]), r=[(curk, h) for h in range(4)], w=["Stb"])
            state_acc(cur, curk, nxt, nxtk, G2, pids)
            Tbufs.reverse()
            if c >= 2:
                gr_piece(3 - c)
            for h in range(4):
                p = nps()
                for dt in range(2):
                    S.op("pe", lambda e, h=h, dt=dt, p=p, cs=cs: e.matmul(
                        ps[p][:, 0:256], lhsT=qT[:, 2 * h + dt, cs], rhs=Stb[:, 2 * h + dt, :],
                        start=(dt == 0), stop=(dt == 1)), r=["qT", "Stb"], w=[("ps", p)])
                S.op("act", lambda e, h=h, p=p, c=c: e.activation(
                    out=crs[:, c, h * 256:(h + 1) * 256], in_=ps[p][:, 0:256], func=AF.Copy,
                    scale=dec[:, QD2 + h:QD2 + h + 1]), r=[("ps", p), "dec"], w=[("crs", c)])
        if g > 0:
            d_loads(g - 1)
        for c in range(4):
            S.op("dve", lambda e, c=c: e.tensor_tensor(out=rpg[:, c, :], in0=rpg[:, c, :], in1=crs[:, c, :], op=ALU.add),
                 r=["rpg", ("crs", c)], w=[("rpgc", c)])
        for c in range(4):
            for h in range(4):
                S.op("act", lambda e, h=h, c=c: e.activation(
                    out=junk[:, 0:256], in_=rpg[:, c, h * 256:(h + 1) * 256], func=AF.Square,
                    accum_out=ss[:, 16 + 4 * c + h:17 + 4 * c + h]), r=[("rpgc", c), "rpg"], w=[("ssd", c)])
        allss = [("ssd", c) for c in range(4)]
        S.op("dve", lambda e: e.tensor_scalar(out=rstd[:, 16:32], in0=ss[:, 16:32], scalar1=1.0 / 256, scalar2=EPS,
                                              op0=ALU.mult, op1=ALU.add), r=allss, w=["rstdd"])
        S.op("act", lambda e: e.activation(out=rstd[:, 16:32], in_=rstd[:, 16:32], func=AF.Sqrt), r=["rstdd"], w=["rstdd"])
        S.op("dve", lambda e: e.reciprocal(out=rstd[:, 16:32], in_=rstd[:, 16:32]), r=["rstdd"], w=["rstdd"])
        for c in range(4):
            hbuf, hkey = (hb, "hb") if c % 2 == 0 else (hb2, "hb2")
            for h in range(4):
                S.op("dve", lambda e, h=h, c=c, hbuf=hbuf: e.scalar_tensor_tensor(
                    out=hbuf[:, h * 256:(h + 1) * 256], in0=rpg[:, c, h * 256:(h + 1) * 256],
                    scalar=rstd[:, 16 + 4 * c + h:17 + 4 * c + h], in1=sgr[:, c, h * 256:(h + 1) * 256],
                    op0=ALU.mult, op1=ALU.mult), r=[("rpgc", c), "rpg", "rstdd", ("sgr", c)], w=[hkey])
            transpose_into(hbuf, rgT, c, hkey, "rgT")
        if g > 0:
            d_load_rpg(g - 1)
        S.dma("sp", lambda e, g=g: e.dma_start(out=rgTs[g], in_=rgT[:].rearrange("p k t -> p (k t)")), "rgT_st",
              r=["rgT"], w=[("rgTs", g)])

    if stop_after == "D":
        return finish(nc, S, es, y, None)

    vaf = rpg
    vn = ktm
    uT = kT
    aT = qT
    barrier(CTMP_KEYS + CRS_KEYS + ["rr"], E1_KEYS)
    rt_load(SL_BVA, RT_BVA)
    rt_load(SL_SGUG, RT_SGUG)
    rt_load(SL_SGUB, RT_SGUB)
    maT = sgr
    maTv = maT[:].rearrange("p a b -> p (a b)").rearrange("p (k t) -> p k t", k=8)
    def e1_loads(g):
        load_h2T(g)
        S.dma("sp", lambda e: e.dma_start(out=rgT[:].rearrange("p k t -> p (k t)"), in_=rgTs[g]), "rgT_ld",
              r=[("rgTs", g)], w=["rgT"])

    e1_loads(0)
    load_xt(x1s, 0, "x1s")
    for g in range(NG_OWN):
        for cb in range(4):
            iw = load_w(win, OFF_VA + cb * 256)
            for tt in range(4):
                p = nps()
                for kt in range(8):
                    S.op("pe", lambda e, kt=kt, p=p, tt=tt, iw=iw: e.matmul(
                        ps[p][:, 0:256], lhsT=h2T[:, kt, tt * 128:(tt + 1) * 128], rhs=wp[iw][:, kt, :],
                        start=(kt == 0), stop=(kt == 7)), r=["h2T", ("wp", iw)], w=[("ps", p)])
                S.op("dve", lambda e, p=p, cb=cb, tt=tt: e.tensor_tensor(
                    out=vaf[:, tt, cb * 256:(cb + 1) * 256], in0=ps[p][:, 0:256],
                    in1=rt[:, SL_BVA, cb * 256:(cb + 1) * 256], op=ALU.add), r=[("ps", p), ("rt", SL_BVA)], w=["rpg"])
        for tt in range(4):
            S.op("dve", lambda e: e.memset(ss[:, 8:10], 0.0), w=["ss01"])
            S.op("act", lambda e, tt=tt: e.activation(out=vaf[:, tt, :], in_=vaf[:, tt, :], func=AF.Gelu,
                                                      accum_out=ss[:, 8:9]), r=["rpg", "ss01"], w=["rpg", "ss01"])
            S.op("dve", lambda e: e.tensor_scalar(out=ss[:, 10:11], in0=ss[:, 8:9], scalar1=1.0 / D, scalar2=0.0,
                                                  op0=ALU.mult, op1=ALU.add), r=["ss01"], w=["ssm"])
            S.op("dve", lambda e, tt=tt: ts_ap(e, vaf[:, tt, :], vaf[:, tt, :], ss[:, 10:11], ALU.subtract), r=["rpg", "ssm"], w=["rpg"])
            S.op("act", lambda e, tt=tt: e.activation(out=rr[:], in_=vaf[:, tt, :], func=AF.Square,
                                                      accum_out=ss[:, 9:10]), r=["rpg", "ss01"], w=["rr", "ss01"])
            S.op("dve", lambda e: e.tensor_scalar(out=ss[:, 11:12], in0=ss[:, 9:10], scalar1=1.0 / D, scalar2=EPS,
                                                  op0=ALU.mult, op1=ALU.add), r=["ss01"], w=["ssr"])
            S.op("act", lambda e: e.activation(out=ss[:, 11:12], in_=ss[:, 11:12], func=AF.Sqrt), r=["ssr"], w=["ssr"])
            S.op("dve", lambda e: e.reciprocal(out=ss[:, 11:12], in_=ss[:, 11:12]), r=["ssr"], w=["ssr"])
            S.op("dve", lambda e, tt=tt: e.scalar_tensor_tensor(
                out=vaf[:, tt, :], in0=vaf[:, tt, :], scalar=ss[:, 11:12], in1=rt[:, SL_SGUG, :], op0=ALU.mult,
                op1=ALU.mult), r=["rpg", "ssr", ("rt", SL_SGUG)], w=["rpg"])
            S.op("dve", lambda e, tt=tt: e.tensor_tensor(out=vn[:, tt, :], in0=vaf[:, tt, :], in1=rt[:, SL_SGUB, :],
                                                         op=ALU.add), r=["rpg", ("rt", SL_SGUB)], w=[("ktm", tt)])
        for blk in range(4):
            iw = load_w(win, OFF_U + blk * 256)
            for j in range(2):
                jt = 2 * blk + j
                p = nps()
                for kt in range(8):
                    S.op("pe", lambda e, kt=kt, p=p, j=j, iw=iw: e.matmul(
                        ps[p][:], lhsT=wp[iw][:, kt, j * 128:(j + 1) * 128], rhs=h2T[:, kt, :],
                        start=(kt == 0), stop=(kt == 7)), r=["h2T", ("wp", iw)], w=[("ps", p)])
                si = jt % 2
                S.op("act", lambda e, jt=jt, p=p: e.activation(out=uT[:, jt, :], in_=ps[p][:], func=AF.Gelu,
                                                               bias=ct[:, CT_BU + jt:CT_BU + jt + 1]),
                     r=[("ps", p), "ct"], w=["kT"])
        for c in range(4):
            cs = slice(c * 128, (c + 1) * 128)
            for fh in range(2):
                p = nps()
                for f4 in range(4):
                    ft = fh * 4 + f4
                    gg = ft // 2
                    S.op("pe", lambda e, ft=ft, f4=f4, gg=gg, p=p, c=c: e.matmul(
                        ps[p][:, f4 * 128:(f4 + 1) * 128], lhsT=vn[:, c, ft * 128:(ft + 1) * 128],
                        rhs=wsT[:, gg * 128:(gg + 1) * 128], start=True, stop=False),
                        r=[("ktm", c), "wsT"], w=[("ps", p)])
                    S.op("pe", lambda e, f4=f4, gg=gg, p=p: e.matmul(
                        ps[p][:, f4 * 128:(f4 + 1) * 128], lhsT=ones[0:1, :], rhs=bsr[0:1, gg * 128:(gg + 1) * 128],
                        start=False, stop=True), r=["ones", "bsr"], w=[("ps", p)])
                S.op("dve", lambda e, fh=fh, p=p, cs=cs: e.tensor_tensor(
                    out=aT[:, fh * 4:(fh + 1) * 4, cs], in0=uT[:, fh * 4:(fh + 1) * 4, cs],
                    in1=ps[p][:].rearrange("p (a b) -> p a b", a=4), op=ALU.mult), r=["kT", ("ps", p)], w=["qT"])
        for (off, cto, dst, dk) in ((OFF_GA, CT_BGA, sga, "sga"), (OFF_GB, CT_BGB, sgb, "sgb")):
            for blk in range(4):
                iw = load_w(win, off + blk * 256)
                for j in range(2):
                    jt = 2 * blk + j
                    p = nps()
                    for kt in range(8):
                        S.op("pe", lambda e, kt=kt, p=p, j=j, iw=iw: e.matmul(
                            ps[p][:], lhsT=wp[iw][:, kt, j * 128:(j + 1) * 128], rhs=h2T[:, kt, :],
                            start=(kt == 0), stop=(kt == 7)), r=["h2T", ("wp", iw)], w=[("ps", p)])
                    si = jt % 2
                    S.op("act", lambda e, jt=jt, dst=dst, p=p, cto=cto: e.activation(
                        out=dst[:, jt, :], in_=ps[p][:], func=AF.Sigmoid, bias=ct[:, cto + jt:cto + jt + 1]),
                        r=[("ps", p), "ct"], w=[dk])
        for blk in range(4):
            iw = load_w(wa, blk * 256)
            for j in range(2):
                jt = 2 * blk + j
                p = nps()
                for kt in range(8):
                    S.op("pe", lambda e, kt=kt, p=p, j=j, iw=iw: e.matmul(
                        ps[p][:], lhsT=wp[iw][:, kt, j * 128:(j + 1) * 128], rhs=aT[:, kt, :],
                        start=(kt == 0), stop=(kt == 7)), r=["qT", ("wp", iw)], w=[("ps", p)])
                S.op("dve", lambda e, p=p, jt=jt: e.tensor_tensor(out=maTv[:, jt, :], in0=ps[p][:], in1=sga[:, jt, :],
                                                                  op=ALU.mult), r=[("ps", p), "sga"],
                     w=[("sgr", i) for i in range(4)])
        for blk in range(4):
            iw = load_w(wb, blk * 256)
            for j in range(2):
                jt = 2 * blk + j
                p = nps()
                for kt in range(8):
                    S.op("pe", lambda e, kt=kt, p=p, j=j, iw=iw: e.matmul(
                        ps[p][:], lhsT=wp[iw][:, kt, j * 128:(j + 1) * 128], rhs=rgT[:, kt, :],
                        start=(kt == 0), stop=(kt == 7)), r=["rgT", ("wp", iw)], w=[("ps", p)])
                si = jt % 2
                S.op("dve", lambda e, p=p, jt=jt, si=si: e.tensor_tensor(out=sg[si][:], in0=ps[p][:],
                                                                         in1=sgb[:, jt, :], op=ALU.mult),
                     r=[("ps", p), "sgb"], w=[("sg", si)])
                S.op("dve", lambda e, jt=jt, si=si: e.tensor_tensor(out=mixT[:, jt, :], in0=sg[si][:],
                                                                    in1=maTv[:, jt, :], op=ALU.add),
                     r=[("sg", si)] + [("sgr", i) for i in range(4)], w=["mixT"])
        if g + 1 < NG_OWN:
            e1_loads(g + 1)
        for cb in range(4):
            iw = load_w(wo, cb * 256)
            for tt in range(4):
                p = nps()
                for kt in range(8):
                    S.op("pe", lambda e, kt=kt, p=p, tt=tt, iw=iw: e.matmul(
                        ps[p][:, 0:256], lhsT=mixT[:, kt, tt * 128:(tt + 1) * 128], rhs=wp[iw][:, kt, :],
                        start=(kt == 0), stop=(kt == 7)), r=["mixT", ("wp", iw)], w=[("ps", p)])
                S.op("dve", lambda e, p=p, cb=cb, tt=tt: e.tensor_tensor(
                    out=xt[:, tt, cb * 256:(cb + 1) * 256], in0=ps[p][:, 0:256],
                    in1=xt[:, tt, cb * 256:(cb + 1) * 256], op=ALU.add), r=[("ps", p), ("xt", tt)], w=[("xt", tt)])
        for c in range(4):
            store_xt_c(x2s, g, "x2s", c)
            if g + 1 < NG_OWN:
                load_xt_c(x1s, g + 1, "x1s", c)

    if stop_after == "E1":
        return finish(nc, S, es, y, None)

    barrier(MIX_KEYS + E1_KEYS, FFN_KEYS)
    rt_load(SL_FFN2, RT_FFN2)
    rt_load(SL_FINAL, RT_FINAL)
    ykeys = []
    load_xt(x2s, 0, "x2s")
    for g in range(NG_OWN):
        ffn(SL_FFN2, w2g, w2u, w2d, load_wd=(g == 0))
        for c in range(4):
            S.op("act", lambda e, c=c: e.activation(out=junk[:], in_=xt[:, c, :], func=AF.Square,
                                                    accum_out=ss[:, c:c + 1]), r=[("xt", c)], w=[("ss", c)])
        allss = [("ss", c) for c in range(4)]
        allr = [("rstd", c) for c in range(4)]
        S.op("dve", lambda e: e.tensor_scalar(out=rstd[:, 0:4], in0=ss[:, 0:4], scalar1=1.0 / D, scalar2=EPS,
                                              op0=ALU.mult, op1=ALU.add), r=allss, w=allr)
        S.op("act", lambda e: e.activation(out=rstd[:, 0:4], in_=rstd[:, 0:4], func=AF.Sqrt), r=allr, w=allr)
        S.op("dve", lambda e: e.reciprocal(out=rstd[:, 0:4], in_=rstd[:, 0:4]), r=allr, w=allr)
        for c in range(4):
            S.op("dve", lambda e, c=c: e.scalar_tensor_tensor(out=xt[:, c, :], in0=xt[:, c, :],
                                                              scalar=rstd[:, c:c + 1], in1=rt[:, SL_FINAL, :],
                                                              op0=ALU.mult, op1=ALU.mult),
                 r=[("xt", c), ("rstd", c), ("rt", SL_FINAL)], w=[("xt", c)])
        for c in range(4):
            store_xt_c(y, g, "y", c)
            if g + 1 < NG_OWN:
                load_xt_c(x2s, g + 1, "x2s", c)
    return finish(nc, S, es, y, ykeys)


def finish(nc, S, es, y, ykeys):
    allw = set()
    for o in S.ops:
        if o["dma"]:
            allw.update(o["w"])
    S.op("sp", None, r=sorted(allw, key=str))
    S.emit()
    es.close()
    return nc


def _host_inputs(inputs):
    f = lambda a: np.ascontiguousarray(np.asarray(a, dtype=np.float32))
    x = f(inputs["x"])
    L = 0
    rep = lambda v: np.ascontiguousarray(np.broadcast_to(f(v).reshape(1, -1), (128, v.size)))
    b_in = f(inputs["b_in"])[L]
    col = lambda v: np.ascontiguousarray(f(v).reshape(8, 128).T)
    rowtab = np.stack([rep(f(inputs["ffn1_norm"])[L]), rep(f(inputs["mix_norm"])[L]),
                       rep(f(inputs["ffn2_norm"])[L]), rep(f(inputs["final_norm"])),
                       rep(f(inputs["sgu_norm_g"])[L]), rep(f(inputs["sgu_norm_b"])[L]),
                       rep(b_in[OFF_K:OFF_K + D]), rep(b_in[OFF_V:OFF_V + D]),
                       rep(b_in[OFF_GR:OFF_GR + D]), rep(b_in[OFF_VA:OFF_VA + D])], axis=0)
    ws = f(inputs["sgu_w_s"])[L]
    bs = f(inputs["sgu_b_s"])[L]
    logit = f(inputs["ret_decay_logit"])[L]
    p = np.arange(128, dtype=np.float32)
    consts = np.concatenate([p[:, None], p[:, None] + 1.0, (p[None, :] - p[:, None])], axis=1).astype(np.float32)
    ident = np.eye(128, dtype=np.float32)
    theta = (10000.0 ** (-np.arange(0, 256, 2, dtype=np.float32) / np.float32(256))).astype(np.float32)
    shared = dict(
        w1g=f(inputs["ffn1_w_gate"])[L], w1u=f(inputs["ffn1_w_up"])[L], w1d=f(inputs["ffn1_w_down"])[L],
        w2g=f(inputs["ffn2_w_gate"])[L], w2u=f(inputs["ffn2_w_up"])[L], w2d=f(inputs["ffn2_w_down"])[L],
        win=f(inputs["w_in"])[L], wa=f(inputs["w_branch_a"])[L], wb=f(inputs["w_branch_b"])[L],
        wo=f(inputs["w_out"])[L], rowtab=np.ascontiguousarray(rowtab), consts=consts, ident=ident)
    maps = []
    for core in range(8):
        b, half = core // 2, core % 2
        xs = x[b] if half == 0 else x[b, ::-1]
        pos = np.arange(SEQ, dtype=np.float32) if half == 0 else np.arange(SEQ - 1, -1, -1, dtype=np.float32)
        ang = (pos[:, None] * theta[None, :]).astype(np.float32)
        cs_tm = np.stack([np.cos(ang), np.sin(ang)]).astype(np.float32)
        cs_fm = np.ascontiguousarray(cs_tm[:, :HALF, :].transpose(0, 2, 1))
        if half == 0:
            ws_l, bs_l, lg_l = ws, bs, logit
        else:
            ws_l, bs_l, lg_l = ws[:, ::-1, ::-1], bs[:, ::-1], logit[::-1]
        wst = np.ascontiguousarray(ws_l.transpose(2, 0, 1).reshape(128, 512))
        coltab = np.concatenate([col(b_in[OFF_Q:OFF_Q + D]), col(b_in[OFF_K:OFF_K + D]), col(b_in[OFF_U:OFF_U + D]),
                                 col(b_in[OFF_GA:OFF_GA + D]), col(b_in[OFF_GB:OFF_GB + D]),
                                 np.broadcast_to(np.ascontiguousarray(lg_l).reshape(1, 8), (128, 8))], axis=1)
        m = dict(shared)
        m.update(xall=np.ascontiguousarray(xs), coltab=np.ascontiguousarray(coltab.astype(np.float32)), wst=wst,
                 bsrow=np.ascontiguousarray(bs_l.reshape(1, 512)), cs_fm=cs_fm, cs_tm=np.ascontiguousarray(cs_tm))
        maps.append(m)
    return maps


def kernel(**inputs):
    maps = _host_inputs(inputs)
    nc = build_program()
    res = run_bass_kernel_spmd(nc, maps, core_ids=list(range(8)))
    out = np.empty((4, SEQ, D), dtype=np.float32)
    for core in range(8):
        b, half = core // 2, core % 2
        yc = np.asarray(res.results[core]["y"], dtype=np.float32)
        if half == 0:
            out[b, :HALF] = yc
        else:
            out[b, HALF:] = yc[::-1]
    return out
```
